# Optimizing a Trainium2 kernel written in Bass

```python
import jax, jax.numpy as jnp
from jax import lax
import numpy as np

D_MODEL = 1024
BATCH = 8
SEQ = 2048
DEPTH = 1
DEC_BATCH = 128
DEC_SEQ = 8
PAST_LEN = 16384
PAGE_SIZE = 128

CHUNK = 128
GM_WIDTH = D_MODEL
GM_GROUPS = 8
GM_GROUP_DIM = GM_WIDTH // GM_GROUPS
RET_HEADS = 8
RET_DK = D_MODEL // RET_HEADS
RET_DV = 2 * RET_DK
RET_QK = RET_HEADS * RET_DK
RET_V = RET_HEADS * RET_DV
D_FF = 2816
N_MOD = 9
EPS = 1e-6
ROPE_BASE = 10000.0
IN_SIZES = (GM_WIDTH, GM_WIDTH, RET_QK, RET_QK, RET_V, RET_V, D_MODEL, D_MODEL)
IN_COLS = sum(IN_SIZES)
SPLIT_POINTS = tuple(int(s) for s in np.cumsum(IN_SIZES)[:-1])

kernel_name = "hybrid_gmlp_retention_macaron_adaln_step"


def rms_norm(x, gain):
    xf = x.astype(jnp.float32)
    y = xf * lax.rsqrt(jnp.mean(xf * xf, axis=-1, keepdims=True) + EPS)
    return (y * gain.astype(jnp.float32)).astype(x.dtype)


def layer_norm(x, gain, bias):
    xf = x.astype(jnp.float32)
    mu = jnp.mean(xf, axis=-1, keepdims=True)
    var = jnp.mean(jnp.square(xf - mu), axis=-1, keepdims=True)
    y = (xf - mu) * lax.rsqrt(var + EPS)
    return (y * gain.astype(jnp.float32) + bias.astype(jnp.float32)).astype(x.dtype)


def head_rms(o):
    return o * lax.rsqrt(jnp.mean(o * o, axis=-1, keepdims=True) + EPS)


def modulate(n, shift, scale):
    return n * (1 + scale) + shift


def swiglu(h, w_gate, w_up, w_down):
    return (jax.nn.silu(h @ w_gate) * (h @ w_up)) @ w_down


def rotary(x, pos):
    half = x.shape[-1] // 2
    inv = ROPE_BASE ** (-jnp.arange(half, dtype=jnp.float32) / half)
    ang = pos.astype(jnp.float32)[:, None] * inv[None, :]
    cos = jnp.cos(ang)[None, :, None, :]
    sin = jnp.sin(ang)[None, :, None, :]
    xf = x.astype(jnp.float32)
    x1, x2 = xf[..., :half], xf[..., half:]
    return jnp.concatenate([x1 * cos - x2 * sin, x1 * sin + x2 * cos], axis=-1)


def retention_log_gamma():
    return jnp.log1p(-jnp.power(2.0, -5.0 - jnp.arange(RET_HEADS, dtype=jnp.float32)))


def retention_chunk(q, k, v, S, log_gamma):
    L = q.shape[1]
    idx = jnp.arange(L, dtype=jnp.float32)
    diff = idx[:, None] - idx[None, :]
    causal = diff >= 0
    decay = jnp.where(causal[None], jnp.exp(jnp.maximum(diff, 0.0)[None] * log_gamma[:, None, None]), 0.0)
    scores = jnp.einsum('blhd,bshd->bhls', q, k) * decay[None]
    intra = jnp.einsum('bhls,bshe->blhe', scores, v)
    q_decay = jnp.exp((idx + 1.0)[:, None] * log_gamma[None, :])
    inter = jnp.einsum('blhd,bhde->blhe', q, S) * q_decay[None, :, :, None]
    k_decay = jnp.exp((L - 1.0 - idx)[:, None] * log_gamma[None, :])
    S_new = jnp.exp(L * log_gamma)[None, :, None, None] * S + jnp.einsum(
        'blhd,blhe->bhde', k * k_decay[None, :, :, None], v)
    return intra + inter, S_new


def retention(q, k, v, S0):
    B, L = q.shape[0], q.shape[1]
    lc = min(L, CHUNK)
    nc = L // lc
    log_gamma = retention_log_gamma()

    def to_blocks(a):
        return jnp.moveaxis(a.reshape(B, nc, lc, *a.shape[2:]), 1, 0)

    def step(S, qkv):
        o, S = retention_chunk(qkv[0], qkv[1], qkv[2], S, log_gamma)
        return S, o

    S_new, o = lax.scan(step, S0, (to_blocks(q), to_blocks(k), to_blocks(v)))
    o = jnp.moveaxis(o, 0, 1).reshape(B, L, RET_HEADS, RET_DV)
    return o, S_new


def chunk_spatial_mix(v, gm_ws, gm_bs):
    B, L, W = v.shape
    lc = min(L, CHUNK)
    nc = L // lc
    w = gm_ws[:, :lc, :lc] * jnp.tril(jnp.ones((lc, lc), gm_ws.dtype))
    vg = v.reshape(B, nc, lc, GM_GROUPS, GM_GROUP_DIM)
    mix = jnp.einsum('gts,bnsgd->bntgd', w, vg) + jnp.transpose(gm_bs[:, :lc])[None, None, :, :, None]
    return mix.reshape(B, L, W)


def token_mix(h, pos0, S0, w_in, gm_ln_g, gm_ln_b, gm_ws, gm_bs, w_a, w_b, w_o):
    B, L, _ = h.shape
    z = h @ w_in
    u, v, q, k, rv, rg, ga, gb = jnp.split(z, SPLIT_POINTS, axis=-1)
    u = jax.nn.gelu(u)
    v = layer_norm(jax.nn.gelu(v), gm_ln_g, gm_ln_b)
    o_a = u * chunk_spatial_mix(v, gm_ws, gm_bs)
    pos = pos0 + jnp.arange(L)
    qh = rotary(q.reshape(B, L, RET_HEADS, RET_DK), pos)
    kh = rotary(k.reshape(B, L, RET_HEADS, RET_DK), pos) * (RET_DK ** -0.5)
    vh = rv.reshape(B, L, RET_HEADS, RET_DV).astype(jnp.float32)
    o_r, S_new = retention(qh, kh, vh, S0)
    o_b = head_rms(o_r).reshape(B, L, RET_V).astype(h.dtype) * jax.nn.silu(rg)
    merged = jax.nn.sigmoid(ga) * (o_a @ w_a) + jax.nn.sigmoid(gb) * (o_b @ w_b)
    return merged @ w_o, S_new, v


def decoder_layer(x, c, pos0, S0, w_ada, b_ada, n1_g, w1_gate, w1_up, w1_down,
                  nm_g, w_in, gm_ln_g, gm_ln_b, gm_ws, gm_bs, w_a, w_b, w_o,
                  n2_g, w2_gate, w2_up, w2_down):
    m = jax.nn.silu(c) @ w_ada + b_ada
    sh1, sc1, g1, sh2, sc2, g2, sh3, sc3, g3 = jnp.split(m[:, None, :], N_MOD, axis=-1)
    h = modulate(rms_norm(x, n1_g), sh1, sc1)
    x = x + 0.5 * g1 * swiglu(h, w1_gate, w1_up, w1_down)
    h = modulate(rms_norm(x, nm_g), sh2, sc2)
    mix, S_new, v_rows = token_mix(h, pos0, S0, w_in, gm_ln_g, gm_ln_b, gm_ws, gm_bs, w_a, w_b, w_o)
    x = x + g2 * mix
    h = modulate(rms_norm(x, n2_g), sh3, sc3)
    x = x + 0.5 * g3 * swiglu(h, w2_gate, w2_up, w2_down)
    return x, S_new, v_rows


def setup_inputs(seed: int = 0) -> dict:
    key = jax.random.key(seed)
    ks = jax.random.split(key, 32)
    f32 = jnp.float32

    def nrm(k, shape, scale):
        return jax.random.normal(k, shape, f32) * scale

    def gain(k, shape):
        return 1.0 + 0.02 * jax.random.normal(k, shape, f32)

    return {
        "x_prompt": nrm(ks[0], (BATCH, SEQ, D_MODEL), 1.0),
        "x_sample": nrm(ks[1], (DEC_BATCH, DEC_SEQ, D_MODEL), 1.0),
        "state_ret": nrm(ks[2], (DEPTH, DEC_BATCH, RET_HEADS, RET_DK, RET_DV), 0.5),
        "c_prompt": nrm(ks[3], (BATCH, D_MODEL), 1.0),
        "c_sample": nrm(ks[4], (DEC_BATCH, D_MODEL), 1.0),
        "w_ada": nrm(ks[5], (DEPTH, D_MODEL, N_MOD * D_MODEL), 0.5 * D_MODEL ** -0.5),
        "b_ada": nrm(ks[6], (DEPTH, N_MOD * D_MODEL), 0.01),
        "n1_g": gain(ks[7], (DEPTH, D_MODEL)),
        "w1_gate": nrm(ks[8], (DEPTH, D_MODEL, D_FF), D_MODEL ** -0.5),
        "w1_up": nrm(ks[9], (DEPTH, D_MODEL, D_FF), D_MODEL ** -0.5),
        "w1_down": nrm(ks[10], (DEPTH, D_FF, D_MODEL), D_FF ** -0.5),
        "nm_g": gain(ks[11], (DEPTH, D_MODEL)),
        "w_in": nrm(ks[12], (DEPTH, D_MODEL, IN_COLS), D_MODEL ** -0.5),
        "gm_ln_g": gain(ks[13], (DEPTH, GM_WIDTH)),
        "gm_ln_b": nrm(ks[14], (DEPTH, GM_WIDTH), 0.01),
        "gm_ws": nrm(ks[15], (DEPTH, GM_GROUPS, CHUNK, CHUNK), CHUNK ** -0.5),
        "gm_bs": gain(ks[16], (DEPTH, GM_GROUPS, CHUNK)),
        "w_a": nrm(ks[17], (DEPTH, GM_WIDTH, D_MODEL), GM_WIDTH ** -0.5),
        "w_b": nrm(ks[18], (DEPTH, RET_V, D_MODEL), RET_V ** -0.5),
        "w_o": nrm(ks[19], (DEPTH, D_MODEL, D_MODEL), D_MODEL ** -0.5),
        "n2_g": gain(ks[20], (DEPTH, D_MODEL)),
        "w2_gate": nrm(ks[21], (DEPTH, D_MODEL, D_FF), D_MODEL ** -0.5),
        "w2_up": nrm(ks[22], (DEPTH, D_MODEL, D_FF), D_MODEL ** -0.5),
        "w2_down": nrm(ks[23], (DEPTH, D_FF, D_MODEL), D_FF ** -0.5),
        "final_g": gain(ks[24], (D_MODEL,)),
    }


def reference(x_prompt, x_sample, state_ret, c_prompt, c_sample, w_ada, b_ada, n1_g,
              w1_gate, w1_up, w1_down, nm_g, w_in, gm_ln_g, gm_ln_b, gm_ws, gm_bs,
              w_a, w_b, w_o, n2_g, w2_gate, w2_up, w2_down, final_g):
    yp, ys = x_prompt, x_sample
    sp_list, ss_list, vs_list = [], [], []
    for l in range(DEPTH):
        lw = (w_ada[l], b_ada[l], n1_g[l], w1_gate[l], w1_up[l], w1_down[l],
              nm_g[l], w_in[l], gm_ln_g[l], gm_ln_b[l], gm_ws[l], gm_bs[l],
              w_a[l], w_b[l], w_o[l], n2_g[l], w2_gate[l], w2_up[l], w2_down[l])
        S0_prompt = jnp.zeros((x_prompt.shape[0], RET_HEADS, RET_DK, RET_DV), jnp.float32)
        yp, sp, _ = decoder_layer(yp, c_prompt, 0, S0_prompt, *lw)
        ys, ss, vs = decoder_layer(ys, c_sample, PAST_LEN, state_ret[l].astype(jnp.float32), *lw)
        sp_list.append(sp)
        ss_list.append(ss)
        vs_list.append(vs)
    y_prompt = rms_norm(yp, final_g)
    y_sample = rms_norm(ys, final_g)
    ret_state_prompt = jnp.stack(sp_list, axis=0)
    ret_state_sample = jnp.stack(ss_list, axis=0)
    gmlp_v_sample = jnp.stack(vs_list, axis=0)
    return (y_prompt, y_sample, ret_state_prompt, ret_state_sample, gmlp_v_sample)
```

```python
import numpy as np
from contextlib import ExitStack
import concourse.bass as bass
import concourse.mybir as mybir
from concourse.bass_utils import run_bass_kernel_spmd

F32 = mybir.dt.float32
BF16 = mybir.dt.bfloat16
AF = mybir.ActivationFunctionType
ALU = mybir.AluOpType

D = 1024
DFF = 2816
NJ = DFF // 128
EPS = 1e-6
NCORES = 8
ROT0 = 0
DEC0 = ROT0 + 17 * 128
TLO0 = DEC0 + 32
TUP0 = TLO0 + 128
MKS0 = TUP0 + 128
SEL0 = MKS0 + 128
IDN0 = SEL0 + 16
NCST = IDN0 + 128

GROUPS = [
    dict(chunks=[0, 1, 2, 3, 4], sample=True),
    dict(chunks=[5, 6, 7, 8, 9, 10], sample=False),
    dict(chunks=[11, 12, 13, 14, 15], sample=False),
]
TMAX = 768
NSLOT = 4
SLOT = 4096


def _gammas():
    return [1.0 - 2.0 ** (-5.0 - h) for h in range(8)]


def _host_consts():
    cst = np.zeros((128, NCST), np.float32)
    half = 64
    inv = (np.float32(10000.0) ** (-(np.arange(half, dtype=np.float32) / np.float32(half)))).astype(np.float32)
    l = np.arange(128)
    rot = np.zeros((128, 17, 2, 64), np.float32)
    for t in range(17):
        pos = (t * 128 + l) if t < 16 else (16384 + (l % 8))
        ang = (pos.astype(np.float32)[:, None] * inv[None, :]).astype(np.float32)
        rot[:, t, 0, :] = np.cos(ang.astype(np.float64)).astype(np.float32)
        rot[:, t, 1, :] = np.sin(ang.astype(np.float64)).astype(np.float32)
    cst[:, ROT0:ROT0 + 17 * 128] = rot.reshape(128, -1)
    g = np.array(_gammas(), np.float64)
    dec = np.zeros((128, 2, 2, 8), np.float64)
    for kind in range(2):
        ll = (l if kind == 0 else (l % 8)).astype(np.float64)
        dec[:, kind, 0, :] = g[None, :] ** (ll[:, None] + 1.0)
        dec[:, kind, 1, :] = g[None, :] ** (-(ll[:, None] + 1.0)) * (128.0 ** -0.5)
    cst[:, DEC0:DEC0 + 32] = dec.reshape(128, 32).astype(np.float32)
    r = l[:, None]
    c = l[None, :]
    cst[:, TLO0:TLO0 + 128] = (c <= r)
    cst[:, TUP0:TUP0 + 128] = (c >= r)
    cst[:, MKS0:MKS0 + 128] = ((c >= r) & ((c // 8) == (r // 8)))
    cst[:, SEL0:SEL0 + 16] = ((l[:, None] // 8) == np.arange(16)[None, :])
    cst[:, IDN0:IDN0 + 128] = np.eye(128)
    return cst


class Buf:
    __slots__ = ("w", "r", "dsem", "dcnt")

    def __init__(self):
        self.w = None
        self.r = {}
        self.dsem = None
        self.dcnt = 0


class K:
    def __init__(self, nc, es):
        self.nc = nc
        self.es = es
        self.eng = {"pe": nc.tensor, "act": nc.scalar, "dve": nc.vector, "pool": nc.gpsimd, "sp": nc.sync}
        self.sem = {}
        self.cnt = {}
        for e in ("pe", "act", "dve"):
            self.sem[e] = es.enter_context(nc.semaphore("s_" + e))
            self.cnt[e] = 0
        self.waited = {e: {} for e in self.eng}
        self.nsem = 0
        self.stores = []
        self.last = {}
        self.npe = 0
        self.marks = []
        self.bar = {}

    def _wait(self, e, tok):
        if tok is None:
            return
        sem, val, key = tok
        if e == "pe" and key == "pe":
            return
        if self.waited[e].get(key, 0) >= val:
            return
        self.eng[e].wait_ge(sem, val)
        self.waited[e][key] = val

    def _pre(self, e, reads, writes):
        for b in reads:
            self._wait(e, b.w)
        for b in writes:
            self._wait(e, b.w)
            for t in b.r.values():
                self._wait(e, t)

    def _post(self, e, inst, reads, writes):
        self.cnt[e] += 1
        inst.then_inc(self.sem[e], 1)
        tok = (self.sem[e], self.cnt[e], e)
        self.last[e] = tok
        for b in reads:
            b.r[e] = tok
        for b in writes:
            b.w = tok
            b.r = {}
        return tok

    def op(self, e, fn, reads=(), writes=()):
        self._pre(e, reads, writes)
        inst = fn()
        return self._post(e, inst, reads, writes)

    def pe(self, fns, reads=(), writes=()):
        self._pre("pe", reads, writes)
        inst = None
        for f in fns:
            inst = f()
        self.npe += len(fns)
        return self._post("pe", inst, reads, writes)

    def mark(self, name):
        self.marks.append((name, self.npe))

    def mm(self, out_ap, pairs, reads, writes):
        n = len(pairs)
        nc = self.nc
        fns = []
        for i, (l, r) in enumerate(pairs):
            fns.append(lambda l=l, r=r, i=i: nc.tensor.matmul(out_ap, lhsT=l, rhs=r, start=(i == 0), stop=(i == n - 1)))
        return self.pe(fns, reads, writes)

    def dma(self, q, pairs, W=None, R=None):
        eng = self.eng[q]
        if W is not None:
            self._pre(q, (), (W,))
            b = W
        else:
            self._pre(q, (R,), ())
            b = R
        if b.dsem is None:
            b.dsem = self.es.enter_context(self.nc.semaphore("d%d" % self.nsem))
            self.nsem += 1
        for (o, i) in pairs:
            eng.dma_start(out=o, in_=i).then_inc(b.dsem, 16)
            b.dcnt += 1
        tok = (b.dsem, 16 * b.dcnt, "d%d" % id(b))
        if W is not None:
            W.w = tok
            W.r = {}
        else:
            R.r["dma"] = tok
            self.stores.append(tok)
        return tok

    def pool_after_barrier(self):
        for f, t in self.bar.items():
            self._wait("pool", t)

    def barrier(self):
        self.bar = dict(self.last)
        for e in ("pe", "act", "dve", "sp"):
            for f in ("pe", "act", "dve"):
                if f != e and f in self.last:
                    self._wait(e, self.last[f])
        for e in ("pe", "act", "dve"):
            for t in self.stores:
                self._wait(e, t)


def build(debug=False):
    nc = bass.Bass("TRN2", target_bir_lowering=False)

    def din(name, shape):
        return nc.dram_tensor(name, shape, F32, kind="ExternalInput").ap()

    def dout(name, shape):
        return nc.dram_tensor(name, shape, F32, kind="ExternalOutput").ap()

    x_p = din("x_p", [2048, D])
    x_s = din("x_s", [128, D])
    c_d = din("c_in", [17, D])
    st_in = din("st_in", [16, 8, 128, 256])
    w_ada = din("w_ada", [D, 9 * D])
    vecs_d = din("vecs", [104, 128])
    w1g = din("w1g", [D, DFF]); w1u = din("w1u", [D, DFF]); w1d = din("w1d", [DFF, D])
    w2g = din("w2g", [D, DFF]); w2u = din("w2u", [D, DFF]); w2d = din("w2d", [DFF, D])
    w_in = din("w_in", [D, 10 * D])
    lngb_d = din("lngb", [2, D])
    gmws_d = din("gm_ws", [8, 128, 128])
    gmbs_d = din("gm_bs", [D])
    w_a = din("w_a", [D, D]); w_b = din("w_b", [2 * D, D]); w_o = din("w_o", [D, D])
    cst_d = din("cst", [128, NCST])
    y_p = dout("y_p", [2048, D]); y_s = dout("y_s", [128, D])
    sp_out = dout("sp_out", [8, 128, 256]); ss_out = dout("ss_out", [16, 8, 128, 256])
    vs_out = dout("vs_out", [128, D])

    gam = _gammas()
    if debug:
        dbg_oa = nc.dram_tensor("dbg_oa", [128, 8, 128], BF16, kind="ExternalOutput").ap()
        dbg_ob = nc.dram_tensor("dbg_ob", [128, 16, 128], BF16, kind="ExternalOutput").ap()
        dbg_mg = nc.dram_tensor("dbg_mg", [128, 8, 128], BF16, kind="ExternalOutput").ap()

    with ExitStack() as es:
        k = K(nc, es)

        _uid = [0]

        def sb(st, name, shape, dt):
            _uid[0] += 1
            return st.enter_context(nc.sbuf_tensor("%s_%d" % (name, _uid[0]), shape, dt))

        cst = sb(es, "cst", [128, NCST], F32); B_cst = Buf()
        identB = sb(es, "identB", [128, 128], BF16); B_idb = Buf()
        onesM = sb(es, "onesM", [128, 128], BF16); B_ones = Buf()
        ones256 = sb(es, "ones256", [128, 128], BF16); B_ones2 = Buf()
        vecT = sb(es, "vecT", [128, 104], F32); B_vecT = Buf()
        mT = sb(es, "mT", [128, 72, 17], F32); B_mT = Buf()
        Gm = sb(es, "Gm", [128, 3, 8, 17], F32); B_G = Buf()
        GTm = sb(es, "GTm", [128, 3, 8, 17], F32); B_GT = Buf()
        WT = sb(es, "WT", [128, 8, 128], BF16); B_WT = Buf()
        WTs = sb(es, "WTs", [128, 8, 128], BF16); B_WTs = Buf()
        cT = sb(es, "cT", [128, 8, 17], BF16); B_cT = Buf()
        U = sb(es, "U", [128, 8, 256], F32); B_U = [Buf() for _ in range(8)]
        Sbf = sb(es, "Sbf", [128, 8, 256], BF16); B_Sbf = [Buf() for _ in range(8)]
        ring = [sb(es, "ring%d" % i, [128, SLOT], BF16) for i in range(NSLOT)]
        B_ring = [Buf() for _ in range(NSLOT)]
        xT = sb(es, "xT", [128, 8, TMAX], F32)
        hT = sb(es, "hT", [128, 8, TMAX], BF16)
        banks = [es.enter_context(nc.psum_tensor("pb%d" % i, [128, 512], F32)) for i in range(7)]
        B_bank = [Buf() for _ in range(7)]
        pbf = es.enter_context(nc.psum_tensor("pbf", [128, 1024], BF16)); B_pbf = Buf()
        rot_i = [0]
        rot_n = [6]

        def bank():
            i = rot_i[0] % rot_n[0]
            rot_i[0] = (i + 1) % rot_n[0]
            return banks[i], B_bank[i]

        ring_i = [0]

        def load_block(items, qkv=False):
            i = ring_i[0]
            ring_i[0] = (i + 1) % NSLOT
            pairs = []
            for (off, src) in items:
                kk, cc = src.shape[1], src.shape[2]
                if qkv:
                    c0_ = (0, 128, 256)[off]
                    dst = ring[i][:, :].rearrange("p (k c) -> p k c", k=8)[:, :, c0_:c0_ + cc]
                else:
                    dst = ring[i][:, off:off + kk * cc].rearrange("p (k c) -> p k c", k=kk)
                pairs.append((dst, src))
            k.dma("pool", pairs, W=B_ring[i])
            return ring[i], B_ring[i]

        def wview(slot, off, kk, cc):
            return slot[:, off:off + kk * cc].rearrange("p (k c) -> p k c", k=kk)

        def rows(w, c0, c1):
            return w[:, c0:c1].rearrange("(k p) c -> p k c", p=128)

        k.mark("setup")
        with ExitStack() as ps:
            vecs_sb = sb(ps, "vecs_sb", [128, 128], F32); B_vecs = Buf()
            c_sb = sb(ps, "c_sb", [17, D], F32); B_c = Buf()
            csil = sb(ps, "csil", [17, D], BF16); B_csil = Buf()
            gw = sb(ps, "gw", [128, 8, 128], F32); B_gw = Buf()
            gws = sb(ps, "gws", [128, 8, 128], F32); B_gws = Buf()
            gwb = sb(ps, "gwb", [128, 8, 128], BF16); B_gwb = Buf()
            gwsb = sb(ps, "gwsb", [128, 8, 128], BF16); B_gwsb = Buf()

            k.dma("sp", [(cst[:], cst_d[:, :])], W=B_cst)
            k.dma("sp", [(vecs_sb[0:104, :], vecs_d[:, :])], W=B_vecs)
            k.dma("sp", [(c_sb[0:17, :], c_d[:, :])], W=B_c)
            k.dma("sp", [(gw[:], gmws_d.rearrange("g t s -> t g s"))], W=B_gw)
            k.op("dve", lambda: nc.vector.memset(gws[:], 0.0), (), (B_gws,))
            k.dma("sp", [(gws[8 * j:8 * j + 8, :, 8 * j:8 * j + 8], gmws_d[:, 0:8, 0:8].rearrange("g t s -> t g s"))
                         for j in range(16)], W=B_gws)
            k.op("dve", lambda: nc.vector.memset(onesM[:], 1.0 / 1024.0), (), (B_ones,))
            k.op("dve", lambda: nc.vector.memset(ones256[:], 1.0 / 256.0), (), (B_ones2,))
            identF = cst[:, IDN0:IDN0 + 128]
            k.op("act", lambda: nc.scalar.activation(out=identB[:], in_=identF, func=AF.Copy), (B_cst,), (B_idb,))
            bk, Bb = bank()
            k.pe([lambda: nc.tensor.transpose(out=bk[:, 0:104], in_=vecs_sb[0:104, :], identity=cst[0:104, IDN0:IDN0 + 104])],
                 (B_vecs, B_cst), (Bb,))
            k.op("dve", lambda: nc.vector.tensor_copy(out=vecT[:], in_=bk[:, 0:104]), (Bb,), (B_vecT,))
            tlo = cst[:, TLO0:TLO0 + 128].unsqueeze(1).to_broadcast([128, 8, 128])
            k.op("dve", lambda: nc.vector.tensor_tensor(out=gwb[:], in0=gw[:], in1=tlo, op=ALU.mult), (B_gw, B_cst), (B_gwb,))
            k.op("dve", lambda: nc.vector.tensor_tensor(out=gwsb[:], in0=gws[:], in1=tlo, op=ALU.mult), (B_gws, B_cst), (B_gwsb,))
            for (src, Bs, dst, Bd) in ((gwb, B_gwb, WT, B_WT), (gwsb, B_gwsb, WTs, B_WTs)):
                k.pe([(lambda g=g, src=src: nc.tensor.transpose(out=pbf[:, g * 128:(g + 1) * 128], in_=src[:, g, :], identity=identB[:]))
                      for g in range(8)], (Bs, B_idb), (B_pbf,))
                k.op("dve", lambda dst=dst: nc.vector.tensor_copy(out=dst[:].rearrange("p g t -> p (g t)"), in_=pbf[:, :]), (B_pbf,), (Bd,))
            k.op("act", lambda: nc.scalar.activation(out=csil[0:17, :], in_=c_sb[0:17, :], func=AF.Silu), (B_c,), (B_csil,))
            k.pe([(lambda kk=kk: nc.tensor.transpose(out=pbf[:, kk * 32:kk * 32 + 17], in_=csil[0:17, kk * 128:(kk + 1) * 128],
                                                     identity=identB[0:17, 0:17])) for kk in range(8)], (B_csil, B_idb), (B_pbf,))
            k.op("dve", lambda: nc.vector.tensor_copy(out=cT[:], in_=pbf[:, 0:256].rearrange("p (k s) -> p k s", s=32)[:, :, 0:17]), (B_pbf,), (B_cT,))
            mb0, Bmb0 = bank()
            for blk in range(6):
                slot, Bs = load_block([(0, rows(w_ada, blk * 512, (blk + 1) * 512))])
                wv = wview(slot, 0, 8, 512)
                for jj in range(4):
                    j = blk * 4 + jj
                    o = mb0[:, j * 17:j * 17 + 17]
                    k.mm(o, [(wv[:, kk, jj * 128:(jj + 1) * 128], cT[:, kk, :]) for kk in range(8)], (Bs, B_cT), (Bmb0,))
            k.op("dve", lambda: nc.vector.tensor_tensor(
                out=mT[:, 0:24, :], in0=mb0[:, 0:408].rearrange("p (j s) -> p j s", s=17),
                in1=vecT[:, 0:24].unsqueeze(2).to_broadcast([128, 24, 17]), op=ALU.add), (Bmb0, B_vecT), (B_mT,))

            def mod_derive_G(n):
                sc = mT[:, (3 * n + 1) * 8:(3 * n + 1) * 8 + 8, :]
                gain = vecT[:, 72 + 8 * n:80 + 8 * n].unsqueeze(2).to_broadcast([128, 8, 17])
                k.op("dve", lambda: nc.vector.scalar_tensor_tensor(
                    out=Gm[:, n, :, :], in0=sc, scalar=1.0, in1=gain, op0=ALU.add, op1=ALU.mult), (B_mT, B_vecT), (B_G,))

            def mod_derive_GT(n):
                gg = mT[:, (3 * n + 2) * 8:(3 * n + 2) * 8 + 8, :]
                k.op("dve", lambda: nc.vector.tensor_scalar(
                    out=GTm[:, n, :, :], in0=gg, scalar1=(1.0 if n == 1 else 0.5), scalar2=None, op0=ALU.mult), (B_mT,), (B_GT,))
            mod_derive_G(0)
            mod_derive_GT(0)
            k.barrier()

        def adaln_blocks(blks, after):
            def compute(blk, slot, Bs):
                wv = wview(slot, 0, 8, 512)
                bk, Bb = bank()
                for jj in range(4):
                    o = bk[:, jj * 17:jj * 17 + 17]
                    k.mm(o, [(wv[:, kk, jj * 128:(jj + 1) * 128], cT[:, kk, :]) for kk in range(8)], (Bs, B_cT), (Bb,))
                k.op("dve", lambda: nc.vector.tensor_tensor(
                    out=mT[:, blk * 4:blk * 4 + 4, :], in0=bk[:, 0:68].rearrange("p (j s) -> p j s", s=17),
                    in1=vecT[:, blk * 4:blk * 4 + 4].unsqueeze(2).to_broadcast([128, 4, 17]), op=ALU.add), (Bb, B_vecT), (B_mT,))
                if blk in after:
                    after[blk]()
            pending = None
            for blk in blks:
                if pending is not None:
                    compute(*pending)
                slot, Bs = load_block([(0, rows(w_ada, blk * 512, (blk + 1) * 512))])
                pending = (blk, slot, Bs)
                yield
            compute(*pending)

        agen = adaln_blocks(range(6, 10), {9: lambda: mod_derive_G(1)})
        agen2 = adaln_blocks(range(10, 18), {11: lambda: mod_derive_GT(1), 17: lambda: (mod_derive_G(2), mod_derive_GT(2))})

        B_mod = (B_G, B_GT, B_mT, B_vecT)

        def SHv(n, kk):
            return mT[:, 3 * n * 8 + kk, :]

        def bc8(ap16):
            return ap16.unsqueeze(2).to_broadcast([128, 16, 8])

        def v8(ap):
            return ap.rearrange("p (j i) -> p j i", i=8)

        for gi, G in enumerate(GROUPS):
            chunks = G["chunks"]
            npt = len(chunks)
            has_s = G["sample"]
            tiles = [("p", c) for c in chunks] + ([("s", 16)] if has_s else [])
            ntile = len(tiles)
            T = ntile * 128
            PT = npt * 128
            if npt == 5:
                ntl = [(0, 384, "p"), (384, 256, "p")]
            else:
                ntl = [(0, 384, "p"), (384, 384, "p")]
            if has_s:
                ntl.append((PT, 128, "s"))
            B_x = {(kk, i): Buf() for kk in range(8) for i in range(len(ntl))}
            B_h = {(kk, i): Buf() for kk in range(8) for i in range(len(ntl))}

            def nt_of_tile(ti):
                c0 = ti * 128
                for i, (a, w, kd) in enumerate(ntl):
                    if a <= c0 < a + w:
                        return i
                raise AssertionError

            def bx_all(i):
                return tuple(B_x[(kk, i)] for kk in range(8))

            def bh_all(i):
                return tuple(B_h[(kk, i)] for kk in range(8))

            k.mark("g%d xload" % gi)
            with ExitStack() as ps:
                xin = [sb(ps, "xin%d" % i, [128, D], F32) for i in range(2)]
                B_xin = [Buf(), Buf()]
                for ti, (kd, c) in enumerate(tiles):
                    s = ti % 2
                    src = x_p[c * 128:(c + 1) * 128, :] if kd == "p" else x_s[:, :]
                    k.dma("sp", [(xin[s][:], src)], W=B_xin[s])
                    nti = nt_of_tile(ti)
                    for hf in range(2):
                        bk, Bb = bank()
                        k.pe([(lambda q=q, bk=bk, s=s, hf=hf: nc.tensor.transpose(
                            out=bk[:, q * 128:(q + 1) * 128], in_=xin[s][:, (hf * 4 + q) * 128:(hf * 4 + q + 1) * 128],
                            identity=cst[:, IDN0:IDN0 + 128])) for q in range(4)], (B_xin[s], B_cst), (Bb,))
                        wr = tuple(B_x[(hf * 4 + q, nti)] for q in range(4))
                        eng = "act" if hf == 0 else "dve"
                        if eng == "act":
                            k.op("act", lambda bk=bk, hf=hf, ti=ti: nc.scalar.activation(
                                out=xT[:, hf * 4:hf * 4 + 4, ti * 128:(ti + 1) * 128],
                                in_=bk[:, :].rearrange("p (q t) -> p q t", q=4), func=AF.Copy), (Bb,), wr)
                        else:
                            k.op("dve", lambda bk=bk, hf=hf, ti=ti: nc.vector.tensor_copy(
                                out=xT[:, hf * 4:hf * 4 + 4, ti * 128:(ti + 1) * 128],
                                in_=bk[:, :].rearrange("p (q t) -> p q t", q=4)), (Bb,), wr)
                k.barrier()

            gs = ExitStack()
            rstd2 = sb(gs, "rstd2", [128, 2, 384], F32); B_rstd2 = [Buf(), Buf()]
            ntmp = sb(gs, "ntmp", [128, 2, 384], F32); B_ntmp = [Buf(), Buf()]
            if has_s:
                ksrv = sb(gs, "ksrv", [128, 8, 384], BF16); B_ksrv = [Buf() for _ in range(8)]

            def norm(n):
                for i, (c0, w, kd) in enumerate(ntl):
                    par = i % 2
                    for kk in range(8):
                        k.op("act", lambda kk=kk, c0=c0, w=w: nc.scalar.activation(
                            out=hT[:, kk, c0:c0 + w], in_=xT[:, kk, c0:c0 + w], func=AF.Square), (B_x[(kk, i)],), (B_h[(kk, i)],))
                    bk, Bb = bank()
                    k.mm(bk[:, 0:w], [(onesM[:], hT[:, kk, c0:c0 + w]) for kk in range(8)], bh_all(i) + (B_ones,), (Bb,))
                    k.op("act", lambda bk=bk, w=w: nc.scalar.activation(out=rstd2[:, par, 0:w], in_=bk[:, 0:w], func=AF.Ln, bias=EPS),
                         (Bb,), (B_rstd2[par],))
                    k.op("act", lambda w=w: nc.scalar.activation(out=rstd2[:, par, 0:w], in_=rstd2[:, par, 0:w], func=AF.Exp, scale=-0.5),
                         (B_rstd2[par],), (B_rstd2[par],))
                    for kk in range(8):
                        s = kk % 2
                        k.op("dve", lambda kk=kk, s=s, c0=c0, w=w: nc.vector.tensor_tensor(
                            out=ntmp[:, s, 0:w], in0=xT[:, kk, c0:c0 + w], in1=rstd2[:, par, 0:w], op=ALU.mult),
                            (B_x[(kk, i)], B_rstd2[par]), (B_ntmp[s],))
                        if kd == "p":
                            k.op("dve", lambda kk=kk, s=s, c0=c0, w=w: nc.vector.tensor_scalar(
                                out=hT[:, kk, c0:c0 + w], in0=ntmp[:, s, 0:w], scalar1=Gm[:, n, kk, 0:1], scalar2=SHv(n, kk)[:, 0:1],
                                op0=ALU.mult, op1=ALU.add), (B_ntmp[s],) + B_mod, (B_h[(kk, i)],))
                        else:
                            k.op("dve", lambda kk=kk, s=s: nc.vector.tensor_tensor(
                                out=v8(ntmp[:, s, 0:128]), in0=v8(ntmp[:, s, 0:128]), in1=bc8(Gm[:, n, kk, 1:17]), op=ALU.mult),
                                (B_ntmp[s],) + B_mod, (B_ntmp[s],))
                            k.op("dve", lambda kk=kk, s=s, c0=c0: nc.vector.tensor_tensor(
                                out=v8(hT[:, kk, c0:c0 + 128]), in0=v8(ntmp[:, s, 0:128]), in1=bc8(SHv(n, kk)[:, 1:17]), op=ALU.add),
                                (B_ntmp[s],) + B_mod, (B_h[(kk, i)],))

            def resid(n, dc, i, bk, Bb, ps_tmp, B_ps_tmp):
                c0, w, kd = ntl[i]
                if kd == "p":
                    k.op("dve", lambda: nc.vector.scalar_tensor_tensor(
                        out=xT[:, dc, c0:c0 + w], in0=bk[:, 0:w], scalar=GTm[:, n, dc, 0:1], in1=xT[:, dc, c0:c0 + w],
                        op0=ALU.mult, op1=ALU.add), (Bb,) + B_mod, (B_x[(dc, i)],))
                else:
                    k.op("dve", lambda: nc.vector.tensor_tensor(
                        out=v8(ps_tmp[:, 0:128]), in0=v8(bk[:, 0:128]), in1=bc8(GTm[:, n, dc, 1:17]), op=ALU.mult),
                        (Bb,) + B_mod, (B_ps_tmp,))
                    k.op("dve", lambda: nc.vector.tensor_tensor(
                        out=xT[:, dc, c0:c0 + 128], in0=xT[:, dc, c0:c0 + 128], in1=ps_tmp[:, 0:128], op=ALU.add),
                        (B_ps_tmp,), (B_x[(dc, i)],))

            def sstate_gen(p2):
                NSB = 8
                Sin = sb(p2, "Sin", [128, NSB, 2, 256], F32); B_Sin = [Buf() for _ in range(NSB)]
                rvblk = sb(p2, "rvblk", [128, 2, 2, 256], BF16); B_rvblk = [Buf(), Buf()]
                rounds = [(h, r) for h in range(8) for r in range(8)]

                def ld(n_):
                    h_, r_ = rounds[n_]
                    k.dma("sp", [(Sin[:, n_ % NSB, :, :], st_in[2 * r_:2 * r_ + 2, h_, :, :].rearrange("j d e -> d j e"))], W=B_Sin[n_ % NSB])
                for n0 in range(6):
                    ld(n0)

                def mk_rvblk(n_):
                    h_, r_ = rounds[n_]
                    for jj in range(2):
                        j = 2 * r_ + jj
                        k.op("dve", lambda jj=jj, j=j: nc.vector.tensor_scalar(
                            out=rvblk[:, n_ % 2, jj, :], in0=ksrv[:, h_, 128:384], scalar1=cst[:, SEL0 + j:SEL0 + j + 1], scalar2=None, op0=ALU.mult),
                            (B_ksrv[h_], B_cst), (B_rvblk[n_ % 2],))
                mk_rvblk(0)
                for n_, (h, r) in enumerate(rounds):
                    rb, pb_ = n_ % NSB, n_ % 2
                    g8 = gam[h] ** 8
                    if n_ + 1 < len(rounds):
                        mk_rvblk(n_ + 1)
                    yield
                    bkv, Bkv = banks[5 + pb_], B_bank[5 + pb_]
                    k.mm(bkv[:, :], [(ksrv[:, h, 0:128], rvblk[:, pb_, :, :].rearrange("p a e -> p (a e)"))],
                         (B_ksrv[h], B_rvblk[pb_]), (Bkv,))
                    k.op("dve", lambda: nc.vector.scalar_tensor_tensor(
                        out=Sin[:, rb, :, :].rearrange("p a e -> p (a e)"), in0=Sin[:, rb, :, :].rearrange("p a e -> p (a e)"),
                        scalar=g8, in1=bkv[:, :], op0=ALU.mult, op1=ALU.add), (Bkv, B_Sin[rb]), (B_Sin[rb],))
                    if n_ + 6 < len(rounds):
                        ld(n_ + 6)
                    k.dma("sp", [(ss_out[2 * r:2 * r + 2, h, :, :].rearrange("j d e -> d j e"), Sin[:, rb, :, :])], R=B_Sin[rb])
                    yield


            def ffn(n, wg, wu, wd, use_hook=False):
                k.mark("g%d norm%d" % (gi, n))
                norm(n)
                k.mark("g%d ffn%d" % (gi, n))
                with ExitStack() as ps:
                    aT = sb(ps, "aT", [128, NJ, TMAX], BF16)
                    B_a = {(j, i): Buf() for j in range(NJ) for i in range(len(ntl))}
                    sg = sb(ps, "sg", [128, 2, 384], F32); B_sg = [Buf(), Buf()]
                    rtmp = sb(ps, "rtmp", [128, 128], F32); B_rtmp = Buf()
                    si = 0
                    hook = iter(())
                    if use_hook:
                        hook = sstate_gen(ps)
                        rot_n[0] = 5
                        rot_i[0] = 0
                    for jb in range(NJ // 2):
                        slot, Bs = load_block([(0, rows(wg, jb * 256, (jb + 1) * 256)), (2048, rows(wu, jb * 256, (jb + 1) * 256))])
                        gv_ = wview(slot, 0, 8, 256)
                        uv_ = wview(slot, 2048, 8, 256)
                        for jj in range(2):
                            j = jb * 2 + jj
                            for i, (c0, w, kd) in enumerate(ntl):
                                if kd == "p":
                                    next(hook, None)
                                bg, Bg = bank()
                                bu, Bu = bank()
                                k.mm(bg[:, 0:w], [(gv_[:, kk, jj * 128:(jj + 1) * 128], hT[:, kk, c0:c0 + w]) for kk in range(8)],
                                     (Bs,) + bh_all(i), (Bg,))
                                k.mm(bu[:, 0:w], [(uv_[:, kk, jj * 128:(jj + 1) * 128], hT[:, kk, c0:c0 + w]) for kk in range(8)],
                                     (Bs,) + bh_all(i), (Bu,))
                                s = si % 2
                                si += 1
                                k.op("act", lambda bg=bg, w=w, s=s: nc.scalar.activation(out=sg[:, s, 0:w], in_=bg[:, 0:w], func=AF.Silu),
                                     (Bg,), (B_sg[s],))
                                k.op("dve", lambda bu=bu, w=w, s=s, j=j, c0=c0: nc.vector.tensor_tensor(
                                    out=aT[:, j, c0:c0 + w], in0=bu[:, 0:w], in1=sg[:, s, 0:w], op=ALU.mult),
                                    (Bu, B_sg[s]), (B_a[(j, i)],))
                                if kd == "p":
                                    next(hook, None)
                        next(agen, None)
                    for _ in agen:
                        pass
                    for dc in range(8):
                        slot, Bs = load_block([(0, wd[:, dc * 128:(dc + 1) * 128].rearrange("(j p) c -> p j c", p=128))])
                        dv = wview(slot, 0, NJ, 128)
                        for i, (c0, w, kd) in enumerate(ntl):
                            if kd == "p":
                                next(hook, None)
                            bk, Bb = bank()
                            k.mm(bk[:, 0:w], [(dv[:, j, :], aT[:, j, c0:c0 + w]) for j in range(NJ)],
                                 (Bs,) + tuple(B_a[(j, i)] for j in range(NJ)), (Bb,))
                            resid(n, dc, i, bk, Bb, rtmp, B_rtmp)
                            if kd == "p":
                                next(hook, None)
                    for _ in hook:
                        pass
                    rot_n[0] = 6
                    k.barrier()

            ffn(0, w1g, w1u, w1d)

            k.mark("g%d norm1" % gi)
            norm(1)
            k.mark("g%d vsec" % gi)
            with ExitStack() as ps:
                oaT = sb(ps, "oaT", [128, 8, TMAX], BF16)
                B_oa = {(g, i): Buf() for g in range(8) for i in range(len(ntl))}
                obT = sb(ps, "obT", [128, 16, TMAX], BF16)
                B_ob = {(j, i): Buf() for j in range(16) for i in range(len(ntl))}
                pv = ExitStack()
                vln = sb(pv, "vln", [128, 6, D], BF16); B_vln = [Buf() for _ in range(6)]
                lnG = sb(pv, "lnG", [128, D], F32); B_lnG = Buf()
                lnB = sb(pv, "lnB", [128, D], F32); B_lnB = Buf()
                biasP = sb(pv, "biasP", [128, 8, 128], F32); B_biasP = Buf()
                k.dma("sp", [(lnG[:], lngb_d[0, :].partition_broadcast(128))], W=B_lnG)
                k.dma("sp", [(lnB[:], lngb_d[1, :].partition_broadcast(128))], W=B_lnB)
                k.dma("sp", [(biasP[:].rearrange("p g t -> p (g t)"), gmbs_d.partition_broadcast(128))], W=B_biasP)
                pvu = ExitStack()
                p2 = pvu
                if True:
                    NG = 4
                    gv4 = sb(p2, "gv", [128, NG, D], F32); B_gv4 = [Buf() for _ in range(NG)]
                    st64 = sb(p2, "st6", [128, NG, 2, 6], F32); B_st64 = [Buf() for _ in range(NG)]
                    mv4 = sb(p2, "mv", [128, NG, 2], F32); B_mv4 = [Buf() for _ in range(NG)]
                    rsv4 = sb(p2, "rsv", [128, NG], F32); B_rsv4 = [Buf() for _ in range(NG)]
                    vf = sb(p2, "vf", [128, D], F32); B_vf = Buf()
                    slot0, Bs0 = load_block([(0, rows(w_in, 1024, 1536))])
                    slot1, Bs1 = load_block([(0, rows(w_in, 1536, 2048))])
                    wv0 = wview(slot0, 0, 8, 512)
                    wv1 = wview(slot1, 0, 8, 512)

                    def v_front(ti):
                        kd, c = tiles[ti]
                        i = nt_of_tile(ti)
                        tc0 = ti * 128
                        sl = ti % NG
                        gv = gv4[:, sl, :]; B_gv = B_gv4[sl]
                        st6 = st64[:, sl, :, :]; B_st6 = B_st64[sl]
                        b0, Bb0 = bank()
                        b1, Bb1 = bank()
                        k.mm(b0[:, :], [(hT[:, kk, tc0:tc0 + 128], wv0[:, kk, :]) for kk in range(8)], (Bs0,) + bh_all(i), (Bb0,))
                        k.mm(b1[:, :], [(hT[:, kk, tc0:tc0 + 128], wv1[:, kk, :]) for kk in range(8)], (Bs1,) + bh_all(i), (Bb1,))
                        k.op("act", lambda: nc.scalar.activation(out=gv[:, 0:512], in_=b0[:, :], func=AF.Gelu_apprx_tanh), (Bb0,), (B_gv,))
                        k.op("act", lambda: nc.scalar.activation(out=gv[:, 512:1024], in_=b1[:, :], func=AF.Gelu_apprx_tanh), (Bb1,), (B_gv,))
                        k.op("dve", lambda: nc.vector.bn_stats(out=st6[:, 0, :], in_=gv[:, 0:512]), (B_gv,), (B_st6,))
                        k.op("dve", lambda: nc.vector.bn_stats(out=st6[:, 1, :], in_=gv[:, 512:1024]), (B_gv,), (B_st6,))
                        k.op("dve", lambda: nc.vector.bn_aggr(out=mv4[:, sl, :], in_=st6.rearrange("p a s -> p (a s)")), (B_st6,), (B_mv4[sl],))

                    def v_back(ti):
                        kd, c = tiles[ti]
                        sl = ti % NG
                        gv = gv4[:, sl, :]; B_gv = B_gv4[sl]
                        k.op("dve", lambda: nc.vector.tensor_scalar(out=gv, in0=gv, scalar1=mv4[:, sl, 0:1], scalar2=rsv4[:, sl:sl + 1],
                                                                    op0=ALU.subtract, op1=ALU.mult), (B_gv, B_mv4[sl], B_rsv4[sl]), (B_gv,))
                        k.op("dve", lambda: nc.vector.tensor_tensor(out=gv, in0=gv, in1=lnG[:], op=ALU.mult), (B_gv, B_lnG), (B_gv,))
                        if kd == "p":
                            k.op("dve", lambda: nc.vector.tensor_tensor(out=vln[:, ti, :], in0=gv, in1=lnB[:], op=ALU.add),
                                 (B_gv, B_lnB), (B_vln[ti],))
                        else:
                            k.op("dve", lambda: nc.vector.tensor_tensor(out=vf[:], in0=gv, in1=lnB[:], op=ALU.add),
                                 (B_gv, B_lnB), (B_vf,))
                            k.op("act", lambda: nc.scalar.activation(out=vln[:, ti, :], in_=vf[:], func=AF.Copy), (B_vf,), (B_vln[ti],))
                            k.dma("sp", [(vs_out[:, :], vf[:])], R=B_vf)

                    pairs_ = [list(range(t0_, min(t0_ + 2, ntile))) for t0_ in range(0, ntile, 2)]
                    for pi_, pr in enumerate(pairs_):
                        for ti in pr:
                            v_front(ti)
                        s0 = pr[0] % NG
                        n_ = len(pr)
                        k.op("act", lambda: nc.scalar.activation(out=rsv4[:, s0:s0 + n_], in_=mv4[:, s0:s0 + n_, 1], func=AF.Ln, bias=EPS),
                             tuple(B_mv4[ti % NG] for ti in pr), tuple(B_rsv4[ti % NG] for ti in pr))
                        k.op("act", lambda: nc.scalar.activation(out=rsv4[:, s0:s0 + n_], in_=rsv4[:, s0:s0 + n_], func=AF.Exp, scale=-0.5),
                             tuple(B_rsv4[ti % NG] for ti in pr), tuple(B_rsv4[ti % NG] for ti in pr))
                        for ti in pr:
                            v_back(ti)
                k.mark("g%d usec" % gi)
                if True:
                    ug = sb(p2, "ug", [128, 2, 384], F32); B_ug = [Buf(), Buf()]
                    mb = sb(p2, "mb", [128, 2, 384], F32); B_mb = [Buf(), Buf()]
                    slot0, Bs0 = load_block([(0, rows(w_in, 0, 512))])
                    slot1, Bs1 = load_block([(0, rows(w_in, 512, 1024))])
                    si = 0
                    for g in range(8):
                        slot, Bs = (slot0, Bs0) if g < 4 else (slot1, Bs1)
                        wu_ = wview(slot, 0, 8, 512)
                        gc = (g % 4) * 128
                        for i, (c0, w, kd) in enumerate(ntl):
                            s = si % 2
                            si += 1
                            bu, Bu = bank()
                            k.mm(bu[:, 0:w], [(wu_[:, kk, gc:gc + 128], hT[:, kk, c0:c0 + w]) for kk in range(8)], (Bs,) + bh_all(i), (Bu,))
                            k.op("act", lambda bu=bu, w=w, s=s: nc.scalar.activation(out=ug[:, s, 0:w], in_=bu[:, 0:w], func=AF.Gelu_apprx_tanh),
                                 (Bu,), (B_ug[s],))
                            bm, Bm = bank()
                            t0 = c0 // 128
                            nt_ = w // 128
                            mixw = WT if kd == "p" else WTs
                            Bmw = B_WT if kd == "p" else B_WTs
                            k.pe([(lambda tt=tt, bm=bm, g=g, mixw=mixw: nc.tensor.matmul(
                                bm[:, tt * 128:(tt + 1) * 128], lhsT=vln[:, t0 + tt, g * 128:(g + 1) * 128], rhs=mixw[:, g, :],
                                start=True, stop=True)) for tt in range(nt_)],
                                tuple(B_vln[t0 + tt] for tt in range(nt_)) + (Bmw,), (Bm,))
                            if kd == "p":
                                k.op("dve", lambda bm=bm, w=w, s=s, g=g, nt_=nt_: nc.vector.tensor_tensor(
                                    out=mb[:, s, 0:w].rearrange("p (a t) -> p a t", t=128), in0=bm[:, 0:w].rearrange("p (a t) -> p a t", t=128),
                                    in1=biasP[:, g, :].unsqueeze(1).to_broadcast([128, nt_, 128]), op=ALU.add), (Bm, B_biasP), (B_mb[s],))
                            else:
                                k.op("dve", lambda bm=bm, s=s, g=g: nc.vector.tensor_tensor(
                                    out=v8(mb[:, s, 0:128]), in0=v8(bm[:, 0:128]),
                                    in1=biasP[:, g, 0:8].unsqueeze(1).to_broadcast([128, 16, 8]), op=ALU.add), (Bm, B_biasP), (B_mb[s],))
                            k.op("dve", lambda w=w, s=s, g=g, c0=c0: nc.vector.tensor_tensor(
                                out=oaT[:, g, c0:c0 + w], in0=ug[:, s, 0:w], in1=mb[:, s, 0:w], op=ALU.mult),
                                (B_ug[s], B_mb[s]), (B_oa[(g, i)],))
                    k.barrier()
                pvu.close()
                pv.close()
                k.mark("g%d ret" % gi)
                with ExitStack() as p2:
                    D6 = 6
                    srg2 = sb(p2, "srg", [128, 2, 2, TMAX], F32); B_srg2 = [[Buf() for _ in range(len(ntl))] for _ in range(2)]
                    orT2 = sb(p2, "orT", [128, 2, 2, TMAX], F32); B_or2 = [[Buf() for _ in range(ntile)] for _ in range(2)]
                    ssb2 = sb(p2, "ssb", [128, 2, TMAX], F32); B_ssb2 = [[Buf() for _ in range(ntile)] for _ in range(2)]
                    qkf = sb(p2, "qkf", [128, 2, 256], F32); B_qkf = [Buf(), Buf()]
                    rt = sb(p2, "rt", [128, 2, 4, 128], F32); B_rt = [[Buf() for _ in range(4)] for _ in range(2)]
                    qkr = sb(p2, "qkr", [128, D6, 256], BF16); B_qkr = [Buf() for _ in range(D6)]
                    qkT = sb(p2, "qkT", [128, D6, 256], BF16); B_qkT = [Buf() for _ in range(D6)]
                    rvb = sb(p2, "rvb", [128, D6, 256], BF16); B_rvb = [Buf() for _ in range(D6)]
                    scT = sb(p2, "scT", [128, 2, 128], BF16); B_scT = [Buf(), Buf()]
                    sqh = sb(p2, "sqh", [128, 2, 2, 128], BF16); B_sqh = [Buf(), Buf()]
                    B_pbq = [Buf() for _ in range(4)]
                    if has_s:
                        S16 = sb(p2, "S16", [128, 16, 256], BF16); B_S16 = Buf()
                    items = [(h, ti) for h in range(8) for ti in range(ntile)]
                    NI = len(items)
                    hctx = {}
                    hctxA = {}

                    def S1(ii):
                        h, ti = items[ii]
                        kd, c = tiles[ti]
                        def loadA(hh):
                            hctxA[hh] = load_block([(0, rows(w_in, 2048 + hh * 128, 2048 + (hh + 1) * 128)),
                                                    (1, rows(w_in, 3072 + hh * 128, 3072 + (hh + 1) * 128)),
                                                    (2, rows(w_in, 4096 + hh * 256, 4096 + (hh + 1) * 256))], qkv=True)
                        if ti == 0:
                            if h == 0:
                                loadA(0)
                            slotB, BsB = load_block([(0, rows(w_in, 6144 + h * 256, 6144 + (h + 1) * 256))])
                            hctx[h] = hctxA[h] + (slotB, BsB)
                            next(agen2, None)
                            if has_s and h == 0:
                                k.pool_after_barrier()
                                k.dma("pool", [(S16[:, :, :], st_in[:, 0, :, :].rearrange("j d e -> d j e"))], W=B_S16)
                        if ti == 2 and h < 7:
                            loadA(h + 1)
                        slotA, BsA, slotB, BsB = hctx[h]
                        wA = wview(slotA, 0, 8, 512)
                        i = nt_of_tile(ti)
                        tc0 = ti * 128
                        kidx = 0 if kd == "p" else 1
                        p2_, p6 = ii % 2, ii % D6
                        ba, Bba = bank()
                        k.mm(ba[:, 0:512], [(hT[:, kk, tc0:tc0 + 128], wA[:, kk, :]) for kk in range(8)], (BsA,) + bh_all(i), (Bba,))
                        dq = cst[:, DEC0 + kidx * 16 + h:DEC0 + kidx * 16 + h + 1]
                        dk = cst[:, DEC0 + kidx * 16 + 8 + h:DEC0 + kidx * 16 + 8 + h + 1]
                        k.op("act", lambda: nc.scalar.activation(out=qkf[:, p2_, 0:128], in_=ba[:, 0:128], func=AF.Copy, scale=dq),
                             (Bba, B_cst), (B_qkf[p2_],))
                        k.op("act", lambda: nc.scalar.activation(out=qkf[:, p2_, 128:256], in_=ba[:, 128:256], func=AF.Copy, scale=dk),
                             (Bba, B_cst), (B_qkf[p2_],))
                        k.op("act", lambda: nc.scalar.activation(out=rvb[:, p6, :], in_=ba[:, 256:512], func=AF.Copy), (Bba,), (B_rvb[p6],))

                    def S1b(ii):
                        h, ti = items[ii]
                        kd, c = tiles[ti]
                        p2_, p6 = ii % 2, ii % D6
                        tt_ = 16 if kd == "s" else c
                        cosb = cst[:, ROT0 + tt_ * 128:ROT0 + tt_ * 128 + 64].unsqueeze(1).to_broadcast([128, 2, 64])
                        sinb = cst[:, ROT0 + tt_ * 128 + 64:ROT0 + tt_ * 128 + 128].unsqueeze(1).to_broadcast([128, 2, 64])
                        qv = qkf[:, p2_, :].rearrange("p (a b i) -> p a b i", a=2, b=2)
                        x1 = qv[:, :, 0, :]
                        x2 = qv[:, :, 1, :]
                        ov = qkr[:, p6, :].rearrange("p (a b i) -> p a b i", a=2, b=2)
                        Brt = B_rt[p2_]

                        def r4(j):
                            return rt[:, p2_, j, :].rearrange("p (a i) -> p a i", a=2)
                        k.op("dve", lambda: nc.vector.tensor_tensor(out=r4(0), in0=x1, in1=cosb, op=ALU.mult), (B_qkf[p2_], B_cst), (Brt[0],))
                        k.op("dve", lambda: nc.vector.tensor_tensor(out=r4(1), in0=x2, in1=sinb, op=ALU.mult), (B_qkf[p2_], B_cst), (Brt[1],))
                        k.op("dve", lambda: nc.vector.tensor_tensor(out=r4(2), in0=x1, in1=sinb, op=ALU.mult), (B_qkf[p2_], B_cst), (Brt[2],))
                        k.op("dve", lambda: nc.vector.tensor_tensor(out=r4(3), in0=x2, in1=cosb, op=ALU.mult), (B_qkf[p2_], B_cst), (Brt[3],))
                        k.op("dve", lambda: nc.vector.tensor_tensor(out=ov[:, :, 0, :], in0=r4(0), in1=r4(1), op=ALU.subtract),
                             (Brt[0], Brt[1]), (B_qkr[p6],))
                        k.op("dve", lambda: nc.vector.tensor_tensor(out=ov[:, :, 1, :], in0=r4(2), in1=r4(3), op=ALU.add),
                             (Brt[2], Brt[3]), (B_qkr[p6],))

                    def S2(ii):
                        p6 = ii % D6
                        pq = ii % 4
                        k.pe([(lambda a=a: nc.tensor.transpose(out=pbf[:, pq * 256 + a * 128:pq * 256 + (a + 1) * 128],
                                                               in_=qkr[:, p6, a * 128:(a + 1) * 128], identity=identB[:])) for a in range(2)],
                             (B_qkr[p6], B_idb), (B_pbf,))
                        k.op("dve", lambda: nc.vector.tensor_copy(out=qkT[:, p6, :], in_=pbf[:, pq * 256:(pq + 1) * 256]), (B_pbf,), (B_qkT[p6],))

                    def S3(ii):
                        h, ti = items[ii]
                        kd, c = tiles[ti]
                        p2_, p6 = ii % 2, ii % D6
                        bs_, Bbs = bank()
                        k.mm(bs_[:, 0:128], [(qkT[:, p6, 128:256], qkT[:, p6, 0:128])], (B_qkT[p6],), (Bbs,))
                        mk = cst[:, TUP0:TUP0 + 128] if kd == "p" else cst[:, MKS0:MKS0 + 128]
                        k.op("dve", lambda: nc.vector.tensor_tensor(out=scT[:, p2_, :], in0=bs_[:, 0:128], in1=mk, op=ALU.mult),
                             (Bbs, B_cst), (B_scT[p2_],))

                    def S4(ii):
                        h, ti = items[ii]
                        kd, c = tiles[ti]
                        p2_, p6 = ii % 2, ii % D6
                        tc0 = ti * 128
                        gL = gam[h] ** 128
                        first = (kd == "p" and c == 0)
                        orT = orT2[:, h % 2, :, :]
                        B_or = B_or2[h % 2]
                        qdT = qkT[:, p6, 0:128]
                        bo, Bbo = bank()
                        fns = []
                        use_inter = (kd == "p" and not first)
                        for hf in range(2):
                            fns.append(lambda hf=hf: nc.tensor.matmul(
                                bo[:, hf * 128:(hf + 1) * 128], lhsT=rvb[:, p6, hf * 128:(hf + 1) * 128], rhs=scT[:, p2_, :], start=True, stop=(not use_inter)))
                            if use_inter:
                                fns.append(lambda hf=hf: nc.tensor.matmul(
                                    bo[:, hf * 128:(hf + 1) * 128], lhsT=Sbf[:, h, hf * 128:(hf + 1) * 128], rhs=qdT, start=False, stop=True))
                        k.pe(fns, (B_rvb[p6], B_scT[p2_], B_qkT[p6], B_Sbf[h]), (Bbo,))
                        k.op("act", lambda: nc.scalar.activation(
                            out=orT[:, :, tc0:tc0 + 128], in_=bo[:, 0:256].rearrange("p (a t) -> p a t", a=2), func=AF.Copy), (Bbo,), (B_or[ti],))
                        if kd == "p":
                            bkv, Bkv = bank()
                            k.mm(bkv[:, 0:256], [(qkr[:, p6, 128:256], rvb[:, p6, :])], (B_qkr[p6], B_rvb[p6]), (Bkv,))
                            if first:
                                k.op("dve", lambda: nc.vector.tensor_copy(out=U[:, h, :], in_=bkv[:, 0:256]), (Bkv,), (B_U[h],))
                            else:
                                k.op("dve", lambda: nc.vector.scalar_tensor_tensor(
                                    out=U[:, h, :], in0=U[:, h, :], scalar=gL, in1=bkv[:, 0:256], op0=ALU.mult, op1=ALU.add),
                                    (Bkv, B_U[h]), (B_U[h],))
                            k.op("act", lambda: nc.scalar.activation(out=sqh[:, p2_, :, :], in_=bo[:, 0:256].rearrange("p (a t) -> p a t", a=2),
                                                                     func=AF.Square), (Bbo,), (B_sqh[p2_],))
                            k.op("act", lambda: nc.scalar.activation(out=Sbf[:, h, :], in_=U[:, h, :], func=AF.Copy, scale=gL),
                                 (B_U[h],), (B_Sbf[h],))
                        else:
                            bi_, Bbi = banks[6], B_bank[6]
                            fns = []
                            for j in range(16):
                                for hf in range(2):
                                    fns.append(lambda j=j, hf=hf: nc.tensor.matmul(
                                        bi_[:, hf * 128 + 8 * j:hf * 128 + 8 * j + 8], lhsT=S16[:, j, hf * 128:(hf + 1) * 128],
                                        rhs=qkT[:, p6, 8 * j:8 * j + 8], start=True, stop=True))
                            k.pe(fns, (B_S16, B_qkT[p6]), (Bbi,))
                            if h < 7:
                                k.dma("pool", [(S16[:, :, :], st_in[:, h + 1, :, :].rearrange("j d e -> d j e"))], W=B_S16)
                            k.op("dve", lambda: nc.vector.tensor_scalar(out=ksrv[:, h, 0:128], in0=qkr[:, p6, 128:256], scalar1=gam[h] ** 8,
                                                                        scalar2=None, op0=ALU.mult), (B_qkr[p6],), (B_ksrv[h],))
                            k.op("dve", lambda: nc.vector.tensor_copy(out=ksrv[:, h, 128:384], in_=rvb[:, p6, :]), (B_rvb[p6],), (B_ksrv[h],))
                            k.op("dve", lambda: nc.vector.tensor_tensor(
                                out=orT[:, :, tc0:tc0 + 128], in0=bi_[:, 0:256].rearrange("p (a t) -> p a t", a=2),
                                in1=orT[:, :, tc0:tc0 + 128], op=ALU.add), (Bbi, B_or[ti]), (B_or[ti],))
                            k.op("act", lambda: nc.scalar.activation(out=sqh[:, p2_, :, :], in_=orT[:, :, tc0:tc0 + 128], func=AF.Square),
                                 (B_or[ti],), (B_sqh[p2_],))

                    def S5(ii):
                        h, ti = items[ii]
                        p2_ = ii % 2
                        tc0 = ti * 128
                        bss, Bss = bank()
                        k.mm(bss[:, 0:128], [(ones256[:], sqh[:, p2_, 0, :]), (ones256[:], sqh[:, p2_, 1, :])], (B_sqh[p2_], B_ones2), (Bss,))
                        k.op("dve", lambda: nc.vector.tensor_copy(out=ssb2[:, h % 2, tc0:tc0 + 128], in_=bss[:, 0:128]), (Bss,), (B_ssb2[h % 2][ti],))

                    def RG(ii):
                        h, ti = items[ii]
                        if ti != ntile - 1:
                            return
                        slotA, BsA, slotB, BsB = hctx[h]
                        wg_ = wview(slotB, 0, 8, 256)
                        for i, (c0, w, kd) in enumerate(ntl):
                            for hf in range(2):
                                bk, Bb = bank()
                                k.mm(bk[:, 0:w], [(wg_[:, kk, hf * 128:(hf + 1) * 128], hT[:, kk, c0:c0 + w]) for kk in range(8)],
                                     (BsB,) + bh_all(i), (Bb,))
                                k.op("act", lambda bk=bk, w=w, hf=hf, c0=c0: nc.scalar.activation(
                                    out=srg2[:, h % 2, hf, c0:c0 + w], in_=bk[:, 0:w], func=AF.Silu), (Bb,), (B_srg2[h % 2][i],))

                    def E1(ii):
                        h, ti = items[ii]
                        if ti != ntile - 1:
                            return
                        par = h % 2
                        k.op("act", lambda: nc.scalar.activation(out=ssb2[:, par, 0:T], in_=ssb2[:, par, 0:T], func=AF.Ln, bias=EPS),
                             tuple(B_ssb2[par]), tuple(B_ssb2[par]))
                        k.op("act", lambda: nc.scalar.activation(out=ssb2[:, par, 0:T], in_=ssb2[:, par, 0:T], func=AF.Exp, scale=-0.5),
                             tuple(B_ssb2[par]), tuple(B_ssb2[par]))

                    def E2(ii):
                        h, ti = items[ii]
                        if ti != ntile - 1:
                            return
                        par = h % 2
                        for i, (c0, w, kd) in enumerate(ntl):
                            tl = [t_ for t_ in range(ntile) if nt_of_tile(t_) == i]
                            rd = tuple(B_or2[par][t_] for t_ in tl)
                            k.op("dve", lambda c0=c0, w=w: nc.vector.tensor_tensor(
                                out=orT2[:, par, :, c0:c0 + w], in0=orT2[:, par, :, c0:c0 + w],
                                in1=ssb2[:, par, c0:c0 + w].unsqueeze(1).to_broadcast([128, 2, w]),
                                op=ALU.mult), rd + tuple(B_ssb2[par][t_] for t_ in tl), rd)
                            k.op("dve", lambda c0=c0, w=w: nc.vector.tensor_tensor(
                                out=obT[:, 2 * h:2 * h + 2, c0:c0 + w], in0=orT2[:, par, :, c0:c0 + w], in1=srg2[:, par, :, c0:c0 + w], op=ALU.mult),
                                rd + (B_srg2[par][i],), (B_ob[(2 * h, i)], B_ob[(2 * h + 1, i)]))

                    LAG = (0, 2, 3, 4, 5, 0, 0, 8, 9)
                    stages = (S1, S2, S3, S4, S5, S1b, RG, E1, E2)
                    for s in range(NI + 10):
                        for st in (0, 1, 2, 4, 3, 7, 8, 5, 6):
                            ii = s - LAG[st]
                            if 0 <= ii < NI:
                                stages[st](ii)
                    for _ in agen2:
                        pass
                    if gi == len(GROUPS) - 1:
                        Ufin = sb(p2, "Ufin", [128, 8, 256], F32); B_Ufin = Buf()
                        for h in range(8):
                            k.op("act", lambda h=h: nc.scalar.activation(out=Ufin[:, h, :], in_=U[:, h, :], func=AF.Copy, scale=gam[h] ** 128),
                                 (B_U[h],), (B_Ufin,))
                        k.dma("sp", [(sp_out.rearrange("h d e -> d h e"), Ufin[:])], R=B_Ufin)
                    k.barrier()
                if debug and has_s:
                    Bd = Buf()
                    k.dma("sp", [(dbg_oa[:, :, :], oaT[:, :, PT:PT + 128])], R=Bd)
                    k.dma("sp", [(dbg_ob[:, :, :], obT[:, :, PT:PT + 128])], R=Bd)
                k.mark("g%d merge" % gi)
                with ExitStack() as p2:
                    mgT = sb(p2, "mgT", [128, 8, TMAX], BF16)
                    B_mg = {(dc, i): Buf() for dc in range(8) for i in range(len(ntl))}
                    sga = sb(p2, "sga", [128, 384], F32); B_sga = Buf()
                    sgb = sb(p2, "sgb", [128, 384], F32); B_sgb = Buf()
                    m1 = sb(p2, "m1", [128, 384], F32); B_m1 = Buf()
                    m2 = sb(p2, "m2", [128, 384], F32); B_m2 = Buf()
                    rtmp = sb(p2, "rtmp2", [128, 128], F32); B_rtmp = Buf()

                    for dc in range(8):
                        cs = slice(dc * 128, (dc + 1) * 128)
                        slotA, BsA = load_block([(0, w_a[:, cs].rearrange("(j p) c -> p j c", p=128)),
                                                 (1024, w_b[:, cs].rearrange("(j p) c -> p j c", p=128)),
                                                 (3072, rows(w_in, 8192 + dc * 128, 8192 + (dc + 1) * 128))])
                        slotB, BsB = load_block([(0, rows(w_in, 9216 + dc * 128, 9216 + (dc + 1) * 128))])
                        wa_ = wview(slotA, 0, 8, 128)
                        wb_ = wview(slotA, 1024, 16, 128)
                        wga = wview(slotA, 3072, 8, 128)
                        wgb = wview(slotB, 0, 8, 128)
                        for i, (c0, w, kd) in enumerate(ntl):
                            bpa, Bpa = bank()
                            bpb, Bpb = bank()
                            bga, Bga = bank()
                            bgb, Bgb = bank()
                            k.mm(bga[:, 0:w], [(wga[:, kk, :], hT[:, kk, c0:c0 + w]) for kk in range(8)], (BsA,) + bh_all(i), (Bga,))
                            k.mm(bgb[:, 0:w], [(wgb[:, kk, :], hT[:, kk, c0:c0 + w]) for kk in range(8)], (BsB,) + bh_all(i), (Bgb,))
                            k.mm(bpa[:, 0:w], [(wa_[:, j, :], oaT[:, j, c0:c0 + w]) for j in range(8)],
                                 (BsA,) + tuple(B_oa[(j, i)] for j in range(8)), (Bpa,))
                            k.mm(bpb[:, 0:w], [(wb_[:, j, :], obT[:, j, c0:c0 + w]) for j in range(16)],
                                 (BsA,) + tuple(B_ob[(j, i)] for j in range(16)), (Bpb,))
                            k.op("act", lambda bga=bga, w=w: nc.scalar.activation(out=sga[:, 0:w], in_=bga[:, 0:w], func=AF.Sigmoid), (Bga,), (B_sga,))
                            k.op("act", lambda bgb=bgb, w=w: nc.scalar.activation(out=sgb[:, 0:w], in_=bgb[:, 0:w], func=AF.Sigmoid), (Bgb,), (B_sgb,))
                            k.op("dve", lambda bpa=bpa, w=w: nc.vector.tensor_tensor(out=m1[:, 0:w], in0=bpa[:, 0:w], in1=sga[:, 0:w], op=ALU.mult),
                                 (Bpa, B_sga), (B_m1,))
                            k.op("dve", lambda bpb=bpb, w=w: nc.vector.tensor_tensor(out=m2[:, 0:w], in0=bpb[:, 0:w], in1=sgb[:, 0:w], op=ALU.mult),
                                 (Bpb, B_sgb), (B_m2,))
                            k.op("dve", lambda w=w, dc=dc, c0=c0: nc.vector.tensor_tensor(out=mgT[:, dc, c0:c0 + w], in0=m1[:, 0:w], in1=m2[:, 0:w], op=ALU.add),
                                 (B_m1, B_m2), (B_mg[(dc, i)],))
                    for blk in range(2):
                        slot, Bs = load_block([(0, rows(w_o, blk * 512, (blk + 1) * 512))])
                        wo_ = wview(slot, 0, 8, 512)
                        for q in range(4):
                            dc = blk * 4 + q
                            for i, (c0, w, kd) in enumerate(ntl):
                                bk, Bb = bank()
                                k.mm(bk[:, 0:w], [(wo_[:, j, q * 128:(q + 1) * 128], mgT[:, j, c0:c0 + w]) for j in range(8)],
                                     (Bs,) + tuple(B_mg[(j, i)] for j in range(8)), (Bb,))
                                resid(1, dc, i, bk, Bb, rtmp, B_rtmp)
                    if debug and has_s:
                        Bd2 = Buf()
                        k.dma("sp", [(dbg_mg[:, :, :], mgT[:, :, PT:PT + 128])], R=Bd2)
                    k.barrier()

            ffn(2, w2g, w2u, w2d, use_hook=has_s)

            k.mark("g%d final" % gi)
            with ExitStack() as ps:
                sq = sb(ps, "fsq", [128, 8, 384], BF16); B_sq = [Buf() for _ in range(8)]
                sd = sb(ps, "fsd", [128, 384], F32); B_sd = Buf()
                rstd = sb(ps, "frstd", [128, TMAX], F32); B_rstd = [Buf() for _ in range(len(ntl))]
                yT = sb(ps, "yT", [128, 8, TMAX], F32); B_y = {(kk, i): Buf() for kk in range(8) for i in range(len(ntl))}
                yo = [sb(ps, "yo%d" % i, [128, D], F32) for i in range(2)]; B_yo = [Buf(), Buf()]
                for i, (c0, w, kd) in enumerate(ntl):
                    for kk in range(8):
                        k.op("act", lambda kk=kk, c0=c0, w=w: nc.scalar.activation(
                            out=sq[:, kk, 0:w], in_=xT[:, kk, c0:c0 + w], func=AF.Square), (B_x[(kk, i)],), (B_sq[kk],))
                    bk, Bb = bank()
                    k.mm(bk[:, 0:w], [(onesM[:], sq[:, kk, 0:w]) for kk in range(8)], tuple(B_sq) + (B_ones,), (Bb,))
                    k.op("act", lambda bk=bk, w=w: nc.scalar.activation(out=sd[:, 0:w], in_=bk[:, 0:w], func=AF.Ln, bias=EPS), (Bb,), (B_sd,))
                    k.op("act", lambda c0=c0, w=w: nc.scalar.activation(out=rstd[:, c0:c0 + w], in_=sd[:, 0:w], func=AF.Exp, scale=-0.5), (B_sd,), (B_rstd[i],))
                    for kk in range(8):
                        k.op("dve", lambda kk=kk, c0=c0, w=w: nc.vector.scalar_tensor_tensor(
                            out=yT[:, kk, c0:c0 + w], in0=xT[:, kk, c0:c0 + w], scalar=vecT[:, 96 + kk:97 + kk], in1=rstd[:, c0:c0 + w],
                            op0=ALU.mult, op1=ALU.mult), (B_x[(kk, i)], B_rstd[i], B_vecT), (B_y[(kk, i)],))
                for ti, (kd, c) in enumerate(tiles):
                    i = nt_of_tile(ti)
                    s = ti % 2
                    for hf in range(2):
                        bk, Bb = bank()
                        k.pe([(lambda q=q, bk=bk, hf=hf, ti=ti: nc.tensor.transpose(
                            out=bk[:, q * 128:(q + 1) * 128], in_=yT[:, hf * 4 + q, ti * 128:(ti + 1) * 128],
                            identity=cst[:, IDN0:IDN0 + 128])) for q in range(4)],
                            tuple(B_y[(hf * 4 + q, i)] for q in range(4)) + (B_cst,), (Bb,))
                        if hf == 0:
                            k.op("act", lambda bk=bk, s=s: nc.scalar.activation(out=yo[s][:, 0:512], in_=bk[:, :], func=AF.Copy), (Bb,), (B_yo[s],))
                        else:
                            k.op("dve", lambda bk=bk, s=s: nc.vector.tensor_copy(out=yo[s][:, 512:1024], in_=bk[:, :]), (Bb,), (B_yo[s],))
                    dst = y_p[c * 128:(c + 1) * 128, :] if kd == "p" else y_s[:, :]
                    k.dma("sp", [(dst, yo[s][:])], R=B_yo[s])
                k.barrier()
            gs.close()

        for t in k.stores:
            k._wait("sp", t)
        k.mark("end")
    build.marks = k.marks
    return nc


_NC_CACHE = {}


def kernel(x_prompt, x_sample, state_ret, c_prompt, c_sample, w_ada, b_ada, n1_g,
           w1_gate, w1_up, w1_down, nm_g, w_in, gm_ln_g, gm_ln_b, gm_ws, gm_bs,
           w_a, w_b, w_o, n2_g, w2_gate, w2_up, w2_down, final_g):
    f = lambda a: np.ascontiguousarray(np.asarray(a, dtype=np.float32))
    x_prompt = f(x_prompt); x_sample = f(x_sample); state_ret = f(state_ret)
    c_prompt = f(c_prompt); c_sample = f(c_sample)
    vecs = np.concatenate([f(b_ada).reshape(72, 128), f(n1_g).reshape(8, 128), f(nm_g).reshape(8, 128),
                           f(n2_g).reshape(8, 128), f(final_g).reshape(8, 128)], axis=0)
    lngb = np.stack([f(gm_ln_g).reshape(D), f(gm_ln_b).reshape(D)], axis=0)
    shared = {
        "w_ada": f(w_ada)[0], "vecs": np.ascontiguousarray(vecs),
        "w1g": f(w1_gate)[0], "w1u": f(w1_up)[0], "w1d": f(w1_down)[0],
        "w2g": f(w2_gate)[0], "w2u": f(w2_up)[0], "w2d": f(w2_down)[0],
        "w_in": f(w_in)[0], "lngb": np.ascontiguousarray(lngb),
        "gm_ws": f(gm_ws)[0], "gm_bs": f(gm_bs).reshape(D),
        "w_a": f(w_a)[0], "w_b": f(w_b)[0], "w_o": f(w_o)[0],
        "cst": _host_consts(),
    }
    in_maps = []
    for b in range(NCORES):
        m = dict(shared)
        m["x_p"] = x_prompt[b]
        m["x_s"] = np.ascontiguousarray(x_sample[16 * b:16 * b + 16].reshape(128, D))
        m["c_in"] = np.ascontiguousarray(np.concatenate([c_prompt[b:b + 1], c_sample[16 * b:16 * b + 16]], axis=0))
        m["st_in"] = np.ascontiguousarray(state_ret[0, 16 * b:16 * b + 16])
        in_maps.append(m)
    nc = build()
    res = run_bass_kernel_spmd(nc, in_maps, core_ids=list(range(NCORES)))
    rs = res.results
    y_prompt = np.stack([rs[b]["y_p"] for b in range(NCORES)], axis=0).astype(np.float32)
    y_sample = np.concatenate([rs[b]["y_s"].reshape(16, 8, D) for b in range(NCORES)], axis=0).astype(np.float32)
    sp = np.stack([rs[b]["sp_out"] for b in range(NCORES)], axis=0)[None].astype(np.float32)
    ss = np.concatenate([rs[b]["ss_out"] for b in range(NCORES)], axis=0)[None].astype(np.float32)
    vs = np.concatenate([rs[b]["vs_out"].reshape(16, 8, D) for b in range(NCORES)], axis=0)[None].astype(np.float32)
    return (y_prompt, y_sample, sp, ss, vs)
```

```python
import numpy as np
from contextlib import ExitStack
import concourse.bass as bass
import concourse.mybir as mybir
from concourse.bass_utils import run_bass_kernel_spmd

F32 = mybir.dt.float32
BF16 = mybir.dt.bfloat16
AF = mybir.ActivationFunctionType
ALU = mybir.AluOpType

D = 1024
DFF = 2816
NJ = DFF // 128
EPS = 1e-6
NCORES = 8
ROT0 = 0
DEC0 = ROT0 + 17 * 128
TLO0 = DEC0 + 32
TUP0 = TLO0 + 128
MKS0 = TUP0 + 128
SEL0 = MKS0 + 128
IDN0 = SEL0 + 16
NCST = IDN0 + 128

GROUPS = [
    dict(chunks=[0, 1, 2, 3, 4], sample=True),
    dict(chunks=[5, 6, 7, 8, 9, 10], sample=False),
    dict(chunks=[11, 12, 13, 14, 15], sample=False),
]
TMAX = 768
NSLOT = 4
SLOT = 4096


def _gammas():
    return [1.0 - 2.0 ** (-5.0 - h) for h in range(8)]


def _host_consts():
    cst = np.zeros((128, NCST), np.float32)
    half = 64
    inv = (np.float32(10000.0) ** (-(np.arange(half, dtype=np.float32) / np.float32(half)))).astype(np.float32)
    l = np.arange(128)
    rot = np.zeros((128, 17, 2, 64), np.float32)
    for t in range(17):
        pos = (t * 128 + l) if t < 16 else (16384 + (l % 8))
        ang = (pos.astype(np.float32)[:, None] * inv[None, :]).astype(np.float32)
        rot[:, t, 0, :] = np.cos(ang.astype(np.float64)).astype(np.float32)
        rot[:, t, 1, :] = np.sin(ang.astype(np.float64)).astype(np.float32)
    cst[:, ROT0:ROT0 + 17 * 128] = rot.reshape(128, -1)
    g = np.array(_gammas(), np.float64)
    dec = np.zeros((128, 2, 2, 8), np.float64)
    for kind in range(2):
        ll = (l if kind == 0 else (l % 8)).astype(np.float64)
        dec[:, kind, 0, :] = g[None, :] ** (ll[:, None] + 1.0)
        dec[:, kind, 1, :] = g[None, :] ** (-(ll[:, None] + 1.0)) * (128.0 ** -0.5)
    cst[:, DEC0:DEC0 + 32] = dec.reshape(128, 32).astype(np.float32)
    r = l[:, None]
    c = l[None, :]
    cst[:, TLO0:TLO0 + 128] = (c <= r)
    cst[:, TUP0:TUP0 + 128] = (c >= r)
    cst[:, MKS0:MKS0 + 128] = ((c >= r) & ((c // 8) == (r // 8)))
    cst[:, SEL0:SEL0 + 16] = ((l[:, None] // 8) == np.arange(16)[None, :])
    cst[:, IDN0:IDN0 + 128] = np.eye(128)
    return cst


class Buf:
    __slots__ = ("w", "r", "dsem", "dcnt")

    def __init__(self):
        self.w = None
        self.r = {}
        self.dsem = None
        self.dcnt = 0


class K:
    def __init__(self, nc, es):
        self.nc = nc
        self.es = es
        self.eng = {"pe": nc.tensor, "act": nc.scalar, "dve": nc.vector, "pool": nc.gpsimd, "sp": nc.sync}
        self.sem = {}
        self.cnt = {}
        for e in ("pe", "act", "dve"):
            self.sem[e] = es.enter_context(nc.semaphore("s_" + e))
            self.cnt[e] = 0
        self.waited = {e: {} for e in self.eng}
        self.nsem = 0
        self.stores = []
        self.last = {}
        self.npe = 0
        self.marks = []
        self.bar = {}

    def _wait(self, e, tok):
        if tok is None:
            return
        sem, val, key = tok
        if e == "pe" and key == "pe":
            return
        if self.waited[e].get(key, 0) >= val:
            return
        self.eng[e].wait_ge(sem, val)
        self.waited[e][key] = val

    def _pre(self, e, reads, writes):
        for b in reads:
            self._wait(e, b.w)
        for b in writes:
            self._wait(e, b.w)
            for t in b.r.values():
                self._wait(e, t)

    def _post(self, e, inst, reads, writes):
        self.cnt[e] += 1
        inst.then_inc(self.sem[e], 1)
        tok = (self.sem[e], self.cnt[e], e)
        self.last[e] = tok
        for b in reads:
            b.r[e] = tok
        for b in writes:
            b.w = tok
            b.r = {}
        return tok

    def op(self, e, fn, reads=(), writes=()):
        self._pre(e, reads, writes)
        inst = fn()
        return self._post(e, inst, reads, writes)

    def pe(self, fns, reads=(), writes=()):
        self._pre("pe", reads, writes)
        inst = None
        for f in fns:
            inst = f()
        self.npe += len(fns)
        return self._post("pe", inst, reads, writes)

    def mark(self, name):
        self.marks.append((name, self.npe))

    def mm(self, out_ap, pairs, reads, writes):
        n = len(pairs)
        nc = self.nc
        fns = []
        for i, (l, r) in enumerate(pairs):
            fns.append(lambda l=l, r=r, i=i: nc.tensor.matmul(out_ap, lhsT=l, rhs=r, start=(i == 0), stop=(i == n - 1)))
        return self.pe(fns, reads, writes)

    def dma(self, q, pairs, W=None, R=None):
        eng = self.eng[q]
        if W is not None:
            self._pre(q, (), (W,))
            b = W
        else:
            self._pre(q, (R,), ())
            b = R
        if b.dsem is None:
            b.dsem = self.es.enter_context(self.nc.semaphore("d%d" % self.nsem))
            self.nsem += 1
        for (o, i) in pairs:
            eng.dma_start(out=o, in_=i).then_inc(b.dsem, 16)
            b.dcnt += 1
        tok = (b.dsem, 16 * b.dcnt, "d%d" % id(b))
        if W is not None:
            W.w = tok
            W.r = {}
        else:
            R.r["dma"] = tok
            self.stores.append(tok)
        return tok

    def pool_after_barrier(self):
        for f, t in self.bar.items():
            self._wait("pool", t)

    def barrier(self):
        self.bar = dict(self.last)
        for e in ("pe", "act", "dve", "sp"):
            for f in ("pe", "act", "dve"):
                if f != e and f in self.last:
                    self._wait(e, self.last[f])
        for e in ("pe", "act", "dve"):
            for t in self.stores:
                self._wait(e, t)


def build(debug=False):
    nc = bass.Bass("TRN2", target_bir_lowering=False)

    def din(name, shape):
        return nc.dram_tensor(name, shape, F32, kind="ExternalInput").ap()

    def dout(name, shape):
        return nc.dram_tensor(name, shape, F32, kind="ExternalOutput").ap()

    x_p = din("x_p", [2048, D])
    x_s = din("x_s", [128, D])
    c_d = din("c_in", [17, D])
    st_in = din("st_in", [16, 8, 128, 256])
    w_ada = din("w_ada", [D, 9 * D])
    vecs_d = din("vecs", [104, 128])
    w1g = din("w1g", [D, DFF]); w1u = din("w1u", [D, DFF]); w1d = din("w1d", [DFF, D])
    w2g = din("w2g", [D, DFF]); w2u = din("w2u", [D, DFF]); w2d = din("w2d", [DFF, D])
    w_in = din("w_in", [D, 10 * D])
    lngb_d = din("lngb", [2, D])
    gmws_d = din("gm_ws", [8, 128, 128])
    gmbs_d = din("gm_bs", [D])
    w_a = din("w_a", [D, D]); w_b = din("w_b", [2 * D, D]); w_o = din("w_o", [D, D])
    cst_d = din("cst", [128, NCST])
    y_p = dout("y_p", [2048, D]); y_s = dout("y_s", [128, D])
    sp_out = dout("sp_out", [8, 128, 256]); ss_out = dout("ss_out", [16, 8, 128, 256])
    vs_out = dout("vs_out", [128, D])

    gam = _gammas()
    if debug:
        dbg_oa = nc.dram_tensor("dbg_oa", [128, 8, 128], BF16, kind="ExternalOutput").ap()
        dbg_ob = nc.dram_tensor("dbg_ob", [128, 16, 128], BF16, kind="ExternalOutput").ap()
        dbg_mg = nc.dram_tensor("dbg_mg", [128, 8, 128], BF16, kind="ExternalOutput").ap()

    with ExitStack() as es:
        k = K(nc, es)

        _uid = [0]

        def sb(st, name, shape, dt):
            _uid[0] += 1
            return st.enter_context(nc.sbuf_tensor("%s_%d" % (name, _uid[0]), shape, dt))

        cst = sb(es, "cst", [128, NCST], F32); B_cst = Buf()
        identB = sb(es, "identB", [128, 128], BF16); B_idb = Buf()
        onesM = sb(es, "onesM", [128, 128], BF16); B_ones = Buf()
        ones256 = sb(es, "ones256", [128, 128], BF16); B_ones2 = Buf()
        vecT = sb(es, "vecT", [128, 104], F32); B_vecT = Buf()
        mT = sb(es, "mT", [128, 72, 17], F32); B_mT = Buf()
        Gm = sb(es, "Gm", [128, 3, 8, 17], F32); B_G = Buf()
        GTm = sb(es, "GTm", [128, 3, 8, 17], F32); B_GT = Buf()
        WT = sb(es, "WT", [128, 8, 128], BF16); B_WT = Buf()
        WTs = sb(es, "WTs", [128, 8, 128], BF16); B_WTs = Buf()
        cT = sb(es, "cT", [128, 8, 17], BF16); B_cT = Buf()
        ksrv = sb(es, "ksrv", [128, 8, 384], BF16); B_ksrv = [Buf() for _ in range(8)]
        U = sb(es, "U", [128, 8, 256], F32); B_U = [Buf() for _ in range(8)]
        Sbf = sb(es, "Sbf", [128, 8, 256], BF16); B_Sbf = [Buf() for _ in range(8)]
        ring = [sb(es, "ring%d" % i, [128, SLOT], BF16) for i in range(NSLOT)]
        B_ring = [Buf() for _ in range(NSLOT)]
        xT = sb(es, "xT", [128, 8, TMAX], F32)
        hT = sb(es, "hT", [128, 8, TMAX], BF16)
        banks = [es.enter_context(nc.psum_tensor("pb%d" % i, [128, 512], F32)) for i in range(7)]
        B_bank = [Buf() for _ in range(7)]
        pbf = es.enter_context(nc.psum_tensor("pbf", [128, 1024], BF16)); B_pbf = Buf()
        rot_i = [0]
        rot_n = [6]

        def bank():
            i = rot_i[0] % rot_n[0]
            rot_i[0] = (i + 1) % rot_n[0]
            return banks[i], B_bank[i]

        ring_i = [0]

        def load_block(items, qkv=False):
            i = ring_i[0]
            ring_i[0] = (i + 1) % NSLOT
            pairs = []
            for (off, src) in items:
                kk, cc = src.shape[1], src.shape[2]
                if qkv:
                    c0_ = (0, 128, 256)[off]
                    dst = ring[i][:, :].rearrange("p (k c) -> p k c", k=8)[:, :, c0_:c0_ + cc]
                else:
                    dst = ring[i][:, off:off + kk * cc].rearrange("p (k c) -> p k c", k=kk)
                pairs.append((dst, src))
            k.dma("pool", pairs, W=B_ring[i])
            return ring[i], B_ring[i]

        def wview(slot, off, kk, cc):
            return slot[:, off:off + kk * cc].rearrange("p (k c) -> p k c", k=kk)

        def rows(w, c0, c1):
            return w[:, c0:c1].rearrange("(k p) c -> p k c", p=128)

        k.mark("setup")
        with ExitStack() as ps:
            vecs_sb = sb(ps, "vecs_sb", [128, 128], F32); B_vecs = Buf()
            c_sb = sb(ps, "c_sb", [17, D], F32); B_c = Buf()
            csil = sb(ps, "csil", [17, D], BF16); B_csil = Buf()
            gw = sb(ps, "gw", [128, 8, 128], F32); B_gw = Buf()
            gws = sb(ps, "gws", [128, 8, 128], F32); B_gws = Buf()
            gwb = sb(ps, "gwb", [128, 8, 128], BF16); B_gwb = Buf()
            gwsb = sb(ps, "gwsb", [128, 8, 128], BF16); B_gwsb = Buf()

            k.dma("sp", [(cst[:], cst_d[:, :])], W=B_cst)
            k.dma("sp", [(vecs_sb[0:104, :], vecs_d[:, :])], W=B_vecs)
            k.dma("sp", [(c_sb[0:17, :], c_d[:, :])], W=B_c)
            k.dma("sp", [(gw[:], gmws_d.rearrange("g t s -> t g s"))], W=B_gw)
            k.op("dve", lambda: nc.vector.memset(gws[:], 0.0), (), (B_gws,))
            k.dma("sp", [(gws[8 * j:8 * j + 8, :, 8 * j:8 * j + 8], gmws_d[:, 0:8, 0:8].rearrange("g t s -> t g s"))
                         for j in range(16)], W=B_gws)
            k.op("dve", lambda: nc.vector.memset(onesM[:], 1.0 / 1024.0), (), (B_ones,))
            k.op("dve", lambda: nc.vector.memset(ones256[:], 1.0 / 256.0), (), (B_ones2,))
            identF = cst[:, IDN0:IDN0 + 128]
            k.op("act", lambda: nc.scalar.activation(out=identB[:], in_=identF, func=AF.Copy), (B_cst,), (B_idb,))
            bk, Bb = bank()
            k.pe([lambda: nc.tensor.transpose(out=bk[:, 0:104], in_=vecs_sb[0:104, :], identity=cst[0:104, IDN0:IDN0 + 104])],
                 (B_vecs, B_cst), (Bb,))
            k.op("dve", lambda: nc.vector.tensor_copy(out=vecT[:], in_=bk[:, 0:104]), (Bb,), (B_vecT,))
            tlo = cst[:, TLO0:TLO0 + 128].unsqueeze(1).to_broadcast([128, 8, 128])
            k.op("dve", lambda: nc.vector.tensor_tensor(out=gwb[:], in0=gw[:], in1=tlo, op=ALU.mult), (B_gw, B_cst), (B_gwb,))
            k.op("dve", lambda: nc.vector.tensor_tensor(out=gwsb[:], in0=gws[:], in1=tlo, op=ALU.mult), (B_gws, B_cst), (B_gwsb,))
            for (src, Bs, dst, Bd) in ((gwb, B_gwb, WT, B_WT), (gwsb, B_gwsb, WTs, B_WTs)):
                k.pe([(lambda g=g, src=src: nc.tensor.transpose(out=pbf[:, g * 128:(g + 1) * 128], in_=src[:, g, :], identity=identB[:]))
                      for g in range(8)], (Bs, B_idb), (B_pbf,))
                k.op("dve", lambda dst=dst: nc.vector.tensor_copy(out=dst[:].rearrange("p g t -> p (g t)"), in_=pbf[:, :]), (B_pbf,), (Bd,))
            k.op("act", lambda: nc.scalar.activation(out=csil[0:17, :], in_=c_sb[0:17, :], func=AF.Silu), (B_c,), (B_csil,))
            k.pe([(lambda kk=kk: nc.tensor.transpose(out=pbf[:, kk * 32:kk * 32 + 17], in_=csil[0:17, kk * 128:(kk + 1) * 128],
                                                     identity=identB[0:17, 0:17])) for kk in range(8)], (B_csil, B_idb), (B_pbf,))
            k.op("dve", lambda: nc.vector.tensor_copy(out=cT[:], in_=pbf[:, 0:256].rearrange("p (k s) -> p k s", s=32)[:, :, 0:17]), (B_pbf,), (B_cT,))
            mb0, Bmb0 = bank()
            for blk in range(6):
                slot, Bs = load_block([(0, rows(w_ada, blk * 512, (blk + 1) * 512))])
                wv = wview(slot, 0, 8, 512)
                for jj in range(4):
                    j = blk * 4 + jj
                    o = mb0[:, j * 17:j * 17 + 17]
                    k.mm(o, [(wv[:, kk, jj * 128:(jj + 1) * 128], cT[:, kk, :]) for kk in range(8)], (Bs, B_cT), (Bmb0,))
            k.op("dve", lambda: nc.vector.tensor_tensor(
                out=mT[:, 0:24, :], in0=mb0[:, 0:408].rearrange("p (j s) -> p j s", s=17),
                in1=vecT[:, 0:24].unsqueeze(2).to_broadcast([128, 24, 17]), op=ALU.add), (Bmb0, B_vecT), (B_mT,))

            def mod_derive_G(n):
                sc = mT[:, (3 * n + 1) * 8:(3 * n + 1) * 8 + 8, :]
                gain = vecT[:, 72 + 8 * n:80 + 8 * n].unsqueeze(2).to_broadcast([128, 8, 17])
                k.op("dve", lambda: nc.vector.scalar_tensor_tensor(
                    out=Gm[:, n, :, :], in0=sc, scalar=1.0, in1=gain, op0=ALU.add, op1=ALU.mult), (B_mT, B_vecT), (B_G,))

            def mod_derive_GT(n):
                gg = mT[:, (3 * n + 2) * 8:(3 * n + 2) * 8 + 8, :]
                k.op("dve", lambda: nc.vector.tensor_scalar(
                    out=GTm[:, n, :, :], in0=gg, scalar1=(1.0 if n == 1 else 0.5), scalar2=None, op0=ALU.mult), (B_mT,), (B_GT,))
            mod_derive_G(0)
            mod_derive_GT(0)
            k.barrier()

        def adaln_blocks(blks, after):
            def compute(blk, slot, Bs):
                wv = wview(slot, 0, 8, 512)
                bk, Bb = bank()
                for jj in range(4):
                    o = bk[:, jj * 17:jj * 17 + 17]
                    k.mm(o, [(wv[:, kk, jj * 128:(jj + 1) * 128], cT[:, kk, :]) for kk in range(8)], (Bs, B_cT), (Bb,))
                k.op("dve", lambda: nc.vector.tensor_tensor(
                    out=mT[:, blk * 4:blk * 4 + 4, :], in0=bk[:, 0:68].rearrange("p (j s) -> p j s", s=17),
                    in1=vecT[:, blk * 4:blk * 4 + 4].unsqueeze(2).to_broadcast([128, 4, 17]), op=ALU.add), (Bb, B_vecT), (B_mT,))
                if blk in after:
                    after[blk]()
            pending = None
            for blk in blks:
                if pending is not None:
                    compute(*pending)
                slot, Bs = load_block([(0, rows(w_ada, blk * 512, (blk + 1) * 512))])
                pending = (blk, slot, Bs)
                yield
            compute(*pending)

        agen = adaln_blocks(range(6, 10), {9: lambda: mod_derive_G(1)})
        agen2 = adaln_blocks(range(10, 18), {11: lambda: mod_derive_GT(1), 17: lambda: (mod_derive_G(2), mod_derive_GT(2))})

        B_mod = (B_G, B_GT, B_mT, B_vecT)

        def SHv(n, kk):
            return mT[:, 3 * n * 8 + kk, :]

        def bc8(ap16):
            return ap16.unsqueeze(2).to_broadcast([128, 16, 8])

        def v8(ap):
            return ap.rearrange("p (j i) -> p j i", i=8)

        for gi, G in enumerate(GROUPS):
            chunks = G["chunks"]
            npt = len(chunks)
            has_s = G["sample"]
            tiles = [("p", c) for c in chunks] + ([("s", 16)] if has_s else [])
            ntile = len(tiles)
            T = ntile * 128
            PT = npt * 128
            if npt == 5:
                ntl = [(0, 384, "p"), (384, 256, "p")]
            else:
                ntl = [(0, 384, "p"), (384, 384, "p")]
            if has_s:
                ntl.append((PT, 128, "s"))
            B_x = {(kk, i): Buf() for kk in range(8) for i in range(len(ntl))}
            B_h = {(kk, i): Buf() for kk in range(8) for i in range(len(ntl))}

            def nt_of_tile(ti):
                c0 = ti * 128
                for i, (a, w, kd) in enumerate(ntl):
                    if a <= c0 < a + w:
                        return i
                raise AssertionError

            def bx_all(i):
                return tuple(B_x[(kk, i)] for kk in range(8))

            def bh_all(i):
                return tuple(B_h[(kk, i)] for kk in range(8))

            k.mark("g%d xload" % gi)
            with ExitStack() as ps:
                xin = [sb(ps, "xin%d" % i, [128, D], F32) for i in range(2)]
                B_xin = [Buf(), Buf()]
                for ti, (kd, c) in enumerate(tiles):
                    s = ti % 2
                    src = x_p[c * 128:(c + 1) * 128, :] if kd == "p" else x_s[:, :]
                    k.dma("sp", [(xin[s][:], src)], W=B_xin[s])
                    nti = nt_of_tile(ti)
                    for hf in range(2):
                        bk, Bb = bank()
                        k.pe([(lambda q=q, bk=bk, s=s, hf=hf: nc.tensor.transpose(
                            out=bk[:, q * 128:(q + 1) * 128], in_=xin[s][:, (hf * 4 + q) * 128:(hf * 4 + q + 1) * 128],
                            identity=cst[:, IDN0:IDN0 + 128])) for q in range(4)], (B_xin[s], B_cst), (Bb,))
                        wr = tuple(B_x[(hf * 4 + q, nti)] for q in range(4))
                        eng = "act" if hf == 0 else "dve"
                        if eng == "act":
                            k.op("act", lambda bk=bk, hf=hf, ti=ti: nc.scalar.activation(
                                out=xT[:, hf * 4:hf * 4 + 4, ti * 128:(ti + 1) * 128],
                                in_=bk[:, :].rearrange("p (q t) -> p q t", q=4), func=AF.Copy), (Bb,), wr)
                        else:
                            k.op("dve", lambda bk=bk, hf=hf, ti=ti: nc.vector.tensor_copy(
                                out=xT[:, hf * 4:hf * 4 + 4, ti * 128:(ti + 1) * 128],
                                in_=bk[:, :].rearrange("p (q t) -> p q t", q=4)), (Bb,), wr)
                k.barrier()

            gs = ExitStack()
            rstd2 = sb(gs, "rstd2", [128, 2, 384], F32); B_rstd2 = [Buf(), Buf()]
            ntmp = sb(gs, "ntmp", [128, 2, 384], F32); B_ntmp = [Buf(), Buf()]

            def norm(n):
                for i, (c0, w, kd) in enumerate(ntl):
                    par = i % 2
                    for kk in range(8):
                        k.op("act", lambda kk=kk, c0=c0, w=w: nc.scalar.activation(
                            out=hT[:, kk, c0:c0 + w], in_=xT[:, kk, c0:c0 + w], func=AF.Square), (B_x[(kk, i)],), (B_h[(kk, i)],))
                    bk, Bb = bank()
                    k.mm(bk[:, 0:w], [(onesM[:], hT[:, kk, c0:c0 + w]) for kk in range(8)], bh_all(i) + (B_ones,), (Bb,))
                    k.op("act", lambda bk=bk, w=w: nc.scalar.activation(out=rstd2[:, par, 0:w], in_=bk[:, 0:w], func=AF.Ln, bias=EPS),
                         (Bb,), (B_rstd2[par],))
                    k.op("act", lambda w=w: nc.scalar.activation(out=rstd2[:, par, 0:w], in_=rstd2[:, par, 0:w], func=AF.Exp, scale=-0.5),
                         (B_rstd2[par],), (B_rstd2[par],))
                    for kk in range(8):
                        s = kk % 2
                        k.op("dve", lambda kk=kk, s=s, c0=c0, w=w: nc.vector.tensor_tensor(
                            out=ntmp[:, s, 0:w], in0=xT[:, kk, c0:c0 + w], in1=rstd2[:, par, 0:w], op=ALU.mult),
                            (B_x[(kk, i)], B_rstd2[par]), (B_ntmp[s],))
                        if kd == "p":
                            k.op("dve", lambda kk=kk, s=s, c0=c0, w=w: nc.vector.tensor_scalar(
                                out=hT[:, kk, c0:c0 + w], in0=ntmp[:, s, 0:w], scalar1=Gm[:, n, kk, 0:1], scalar2=SHv(n, kk)[:, 0:1],
                                op0=ALU.mult, op1=ALU.add), (B_ntmp[s],) + B_mod, (B_h[(kk, i)],))
                        else:
                            k.op("dve", lambda kk=kk, s=s: nc.vector.tensor_tensor(
                                out=v8(ntmp[:, s, 0:128]), in0=v8(ntmp[:, s, 0:128]), in1=bc8(Gm[:, n, kk, 1:17]), op=ALU.mult),
                                (B_ntmp[s],) + B_mod, (B_ntmp[s],))
                            k.op("dve", lambda kk=kk, s=s, c0=c0: nc.vector.tensor_tensor(
                                out=v8(hT[:, kk, c0:c0 + 128]), in0=v8(ntmp[:, s, 0:128]), in1=bc8(SHv(n, kk)[:, 1:17]), op=ALU.add),
                                (B_ntmp[s],) + B_mod, (B_h[(kk, i)],))

            def resid(n, dc, i, bk, Bb, ps_tmp, B_ps_tmp):
                c0, w, kd = ntl[i]
                if kd == "p":
                    k.op("dve", lambda: nc.vector.scalar_tensor_tensor(
                        out=xT[:, dc, c0:c0 + w], in0=bk[:, 0:w], scalar=GTm[:, n, dc, 0:1], in1=xT[:, dc, c0:c0 + w],
                        op0=ALU.mult, op1=ALU.add), (Bb,) + B_mod, (B_x[(dc, i)],))
                else:
                    k.op("dve", lambda: nc.vector.tensor_tensor(
                        out=v8(ps_tmp[:, 0:128]), in0=v8(bk[:, 0:128]), in1=bc8(GTm[:, n, dc, 1:17]), op=ALU.mult),
                        (Bb,) + B_mod, (B_ps_tmp,))
                    k.op("dve", lambda: nc.vector.tensor_tensor(
                        out=xT[:, dc, c0:c0 + 128], in0=xT[:, dc, c0:c0 + 128], in1=ps_tmp[:, 0:128], op=ALU.add),
                        (B_ps_tmp,), (B_x[(dc, i)],))

            def sstate_gen(p2):
                NSB = 8
                Sin = sb(p2, "Sin", [128, NSB, 2, 256], F32); B_Sin = [Buf() for _ in range(NSB)]
                rvblk = sb(p2, "rvblk", [128, 2, 2, 256], BF16); B_rvblk = [Buf(), Buf()]
                rounds = [(h, r) for h in range(8) for r in range(8)]

                def ld(n_):
                    h_, r_ = rounds[n_]
                    k.dma("sp", [(Sin[:, n_ % NSB, :, :], st_in[2 * r_:2 * r_ + 2, h_, :, :].rearrange("j d e -> d j e"))], W=B_Sin[n_ % NSB])
                for n0 in range(6):
                    ld(n0)

                def mk_rvblk(n_):
                    h_, r_ = rounds[n_]
                    for jj in range(2):
                        j = 2 * r_ + jj
                        k.op("dve", lambda jj=jj, j=j: nc.vector.tensor_scalar(
                            out=rvblk[:, n_ % 2, jj, :], in0=ksrv[:, h_, 128:384], scalar1=cst[:, SEL0 + j:SEL0 + j + 1], scalar2=None, op0=ALU.mult),
                            (B_ksrv[h_], B_cst), (B_rvblk[n_ % 2],))
                mk_rvblk(0)
                for n_, (h, r) in enumerate(rounds):
                    rb, pb_ = n_ % NSB, n_ % 2
                    g8 = gam[h] ** 8
                    if n_ + 1 < len(rounds):
                        mk_rvblk(n_ + 1)
                    yield
                    bkv, Bkv = banks[5 + pb_], B_bank[5 + pb_]
                    k.mm(bkv[:, :], [(ksrv[:, h, 0:128], rvblk[:, pb_, :, :].rearrange("p a e -> p (a e)"))],
                         (B_ksrv[h], B_rvblk[pb_]), (Bkv,))
                    k.op("dve", lambda: nc.vector.scalar_tensor_tensor(
                        out=Sin[:, rb, :, :].rearrange("p a e -> p (a e)"), in0=Sin[:, rb, :, :].rearrange("p a e -> p (a e)"),
                        scalar=g8, in1=bkv[:, :], op0=ALU.mult, op1=ALU.add), (Bkv, B_Sin[rb]), (B_Sin[rb],))
                    if n_ + 6 < len(rounds):
                        ld(n_ + 6)
                    k.dma("sp", [(ss_out[2 * r:2 * r + 2, h, :, :].rearrange("j d e -> d j e"), Sin[:, rb, :, :])], R=B_Sin[rb])
                    yield


            def ffn(n, wg, wu, wd, use_hook=False):
                k.mark("g%d norm%d" % (gi, n))
                norm(n)
                k.mark("g%d ffn%d" % (gi, n))
                with ExitStack() as ps:
                    aT = sb(ps, "aT", [128, NJ, TMAX], BF16)
                    B_a = {(j, i): Buf() for j in range(NJ) for i in range(len(ntl))}
                    sg = sb(ps, "sg", [128, 2, 384], F32); B_sg = [Buf(), Buf()]
                    rtmp = sb(ps, "rtmp", [128, 128], F32); B_rtmp = Buf()
                    si = 0
                    hook = iter(())
                    if use_hook:
                        hook = sstate_gen(ps)
                        rot_n[0] = 5
                        rot_i[0] = 0
                    for jb in range(NJ // 2):
                        slot, Bs = load_block([(0, rows(wg, jb * 256, (jb + 1) * 256)), (2048, rows(wu, jb * 256, (jb + 1) * 256))])
                        gv_ = wview(slot, 0, 8, 256)
                        uv_ = wview(slot, 2048, 8, 256)
                        for jj in range(2):
                            j = jb * 2 + jj
                            for i, (c0, w, kd) in enumerate(ntl):
                                next(hook, None)
                                bg, Bg = bank()
                                bu, Bu = bank()
                                k.mm(bg[:, 0:w], [(gv_[:, kk, jj * 128:(jj + 1) * 128], hT[:, kk, c0:c0 + w]) for kk in range(8)],
                                     (Bs,) + bh_all(i), (Bg,))
                                k.mm(bu[:, 0:w], [(uv_[:, kk, jj * 128:(jj + 1) * 128], hT[:, kk, c0:c0 + w]) for kk in range(8)],
                                     (Bs,) + bh_all(i), (Bu,))
                                s = si % 2
                                si += 1
                                k.op("act", lambda bg=bg, w=w, s=s: nc.scalar.activation(out=sg[:, s, 0:w], in_=bg[:, 0:w], func=AF.Silu),
                                     (Bg,), (B_sg[s],))
                                k.op("dve", lambda bu=bu, w=w, s=s, j=j, c0=c0: nc.vector.tensor_tensor(
                                    out=aT[:, j, c0:c0 + w], in0=bu[:, 0:w], in1=sg[:, s, 0:w], op=ALU.mult),
                                    (Bu, B_sg[s]), (B_a[(j, i)],))
                                next(hook, None)
                        next(agen, None)
                    for _ in agen:
                        pass
                    for _ in hook:
                        pass
                    rot_n[0] = 6
                    for dc in range(8):
                        slot, Bs = load_block([(0, wd[:, dc * 128:(dc + 1) * 128].rearrange("(j p) c -> p j c", p=128))])
                        dv = wview(slot, 0, NJ, 128)
                        for i, (c0, w, kd) in enumerate(ntl):
                            bk, Bb = bank()
                            k.mm(bk[:, 0:w], [(dv[:, j, :], aT[:, j, c0:c0 + w]) for j in range(NJ)],
                                 (Bs,) + tuple(B_a[(j, i)] for j in range(NJ)), (Bb,))
                            resid(n, dc, i, bk, Bb, rtmp, B_rtmp)
                    k.barrier()

            ffn(0, w1g, w1u, w1d, use_hook=(gi == 1))

            k.mark("g%d norm1" % gi)
            norm(1)
            k.mark("g%d vsec" % gi)
            with ExitStack() as ps:
                oaT = sb(ps, "oaT", [128, 8, TMAX], BF16)
                B_oa = {(g, i): Buf() for g in range(8) for i in range(len(ntl))}
                obT = sb(ps, "obT", [128, 16, TMAX], BF16)
                B_ob = {(j, i): Buf() for j in range(16) for i in range(len(ntl))}
                pv = ExitStack()
                vln = sb(pv, "vln", [128, 6, D], BF16); B_vln = [Buf() for _ in range(6)]
                lnG = sb(pv, "lnG", [128, D], F32); B_lnG = Buf()
                lnB = sb(pv, "lnB", [128, D], F32); B_lnB = Buf()
                biasP = sb(pv, "biasP", [128, 8, 128], F32); B_biasP = Buf()
                k.dma("sp", [(lnG[:], lngb_d[0, :].partition_broadcast(128))], W=B_lnG)
                k.dma("sp", [(lnB[:], lngb_d[1, :].partition_broadcast(128))], W=B_lnB)
                k.dma("sp", [(biasP[:].rearrange("p g t -> p (g t)"), gmbs_d.partition_broadcast(128))], W=B_biasP)
                pvu = ExitStack()
                p2 = pvu
                if True:
                    NG = 4
                    gv4 = sb(p2, "gv", [128, NG, D], F32); B_gv4 = [Buf() for _ in range(NG)]
                    st64 = sb(p2, "st6", [128, NG, 2, 6], F32); B_st64 = [Buf() for _ in range(NG)]
                    mv4 = sb(p2, "mv", [128, NG, 2], F32); B_mv4 = [Buf() for _ in range(NG)]
                    rsv4 = sb(p2, "rsv", [128, NG], F32); B_rsv4 = [Buf() for _ in range(NG)]
                    vf = sb(p2, "vf", [128, D], F32); B_vf = Buf()
                    slot0, Bs0 = load_block([(0, rows(w_in, 1024, 1536))])
                    slot1, Bs1 = load_block([(0, rows(w_in, 1536, 2048))])
                    wv0 = wview(slot0, 0, 8, 512)
                    wv1 = wview(slot1, 0, 8, 512)

                    def v_front(ti):
                        kd, c = tiles[ti]
                        i = nt_of_tile(ti)
                        tc0 = ti * 128
                        sl = ti % NG
                        gv = gv4[:, sl, :]; B_gv = B_gv4[sl]
                        st6 = st64[:, sl, :, :]; B_st6 = B_st64[sl]
                        b0, Bb0 = bank()
                        b1, Bb1 = bank()
                        k.mm(b0[:, :], [(hT[:, kk, tc0:tc0 + 128], wv0[:, kk, :]) for kk in range(8)], (Bs0,) + bh_all(i), (Bb0,))
                        k.mm(b1[:, :], [(hT[:, kk, tc0:tc0 + 128], wv1[:, kk, :]) for kk in range(8)], (Bs1,) + bh_all(i), (Bb1,))
                        k.op("act", lambda: nc.scalar.activation(out=gv[:, 0:512], in_=b0[:, :], func=AF.Gelu_apprx_tanh), (Bb0,), (B_gv,))
                        k.op("act", lambda: nc.scalar.activation(out=gv[:, 512:1024], in_=b1[:, :], func=AF.Gelu_apprx_tanh), (Bb1,), (B_gv,))
                        k.op("dve", lambda: nc.vector.bn_stats(out=st6[:, 0, :], in_=gv[:, 0:512]), (B_gv,), (B_st6,))
                        k.op("dve", lambda: nc.vector.bn_stats(out=st6[:, 1, :], in_=gv[:, 512:1024]), (B_gv,), (B_st6,))
                        k.op("dve", lambda: nc.vector.bn_aggr(out=mv4[:, sl, :], in_=st6.rearrange("p a s -> p (a s)")), (B_st6,), (B_mv4[sl],))

                    def v_back(ti):
                        kd, c = tiles[ti]
                        sl = ti % NG
                        gv = gv4[:, sl, :]; B_gv = B_gv4[sl]
                        k.op("dve", lambda: nc.vector.tensor_scalar(out=gv, in0=gv, scalar1=mv4[:, sl, 0:1], scalar2=rsv4[:, sl:sl + 1],
                                                                    op0=ALU.subtract, op1=ALU.mult), (B_gv, B_mv4[sl], B_rsv4[sl]), (B_gv,))
                        k.op("dve", lambda: nc.vector.tensor_tensor(out=gv, in0=gv, in1=lnG[:], op=ALU.mult), (B_gv, B_lnG), (B_gv,))
                        if kd == "p":
                            k.op("dve", lambda: nc.vector.tensor_tensor(out=vln[:, ti, :], in0=gv, in1=lnB[:], op=ALU.add),
                                 (B_gv, B_lnB), (B_vln[ti],))
                        else:
                            k.op("dve", lambda: nc.vector.tensor_tensor(out=vf[:], in0=gv, in1=lnB[:], op=ALU.add),
                                 (B_gv, B_lnB), (B_vf,))
                            k.op("act", lambda: nc.scalar.activation(out=vln[:, ti, :], in_=vf[:], func=AF.Copy), (B_vf,), (B_vln[ti],))
                            k.dma("sp", [(vs_out[:, :], vf[:])], R=B_vf)

                    pairs_ = [list(range(t0_, min(t0_ + 2, ntile))) for t0_ in range(0, ntile, 2)]
                    for pi_, pr in enumerate(pairs_):
                        for ti in pr:
                            v_front(ti)
                        s0 = pr[0] % NG
                        n_ = len(pr)
                        k.op("act", lambda: nc.scalar.activation(out=rsv4[:, s0:s0 + n_], in_=mv4[:, s0:s0 + n_, 1], func=AF.Ln, bias=EPS),
                             tuple(B_mv4[ti % NG] for ti in pr), tuple(B_rsv4[ti % NG] for ti in pr))
                        k.op("act", lambda: nc.scalar.activation(out=rsv4[:, s0:s0 + n_], in_=rsv4[:, s0:s0 + n_], func=AF.Exp, scale=-0.5),
                             tuple(B_rsv4[ti % NG] for ti in pr), tuple(B_rsv4[ti % NG] for ti in pr))
                        for ti in pr:
                            v_back(ti)
                k.mark("g%d usec" % gi)
                if True:
                    ug = sb(p2, "ug", [128, 2, 384], F32); B_ug = [Buf(), Buf()]
                    mb = sb(p2, "mb", [128, 2, 384], F32); B_mb = [Buf(), Buf()]
                    slot0, Bs0 = load_block([(0, rows(w_in, 0, 512))])
                    slot1, Bs1 = load_block([(0, rows(w_in, 512, 1024))])
                    si = 0
                    for g in range(8):
                        slot, Bs = (slot0, Bs0) if g < 4 else (slot1, Bs1)
                        wu_ = wview(slot, 0, 8, 512)
                        gc = (g % 4) * 128
                        for i, (c0, w, kd) in enumerate(ntl):
                            s = si % 2
                            si += 1
                            bu, Bu = bank()
                            k.mm(bu[:, 0:w], [(wu_[:, kk, gc:gc + 128], hT[:, kk, c0:c0 + w]) for kk in range(8)], (Bs,) + bh_all(i), (Bu,))
                            k.op("act", lambda bu=bu, w=w, s=s: nc.scalar.activation(out=ug[:, s, 0:w], in_=bu[:, 0:w], func=AF.Gelu_apprx_tanh),
                                 (Bu,), (B_ug[s],))
                            bm, Bm = bank()
                            t0 = c0 // 128
                            nt_ = w // 128
                            mixw = WT if kd == "p" else WTs
                            Bmw = B_WT if kd == "p" else B_WTs
                            k.pe([(lambda tt=tt, bm=bm, g=g, mixw=mixw: nc.tensor.matmul(
                                bm[:, tt * 128:(tt + 1) * 128], lhsT=vln[:, t0 + tt, g * 128:(g + 1) * 128], rhs=mixw[:, g, :],
                                start=True, stop=True)) for tt in range(nt_)],
                                tuple(B_vln[t0 + tt] for tt in range(nt_)) + (Bmw,), (Bm,))
                            if kd == "p":
                                k.op("dve", lambda bm=bm, w=w, s=s, g=g, nt_=nt_: nc.vector.tensor_tensor(
                                    out=mb[:, s, 0:w].rearrange("p (a t) -> p a t", t=128), in0=bm[:, 0:w].rearrange("p (a t) -> p a t", t=128),
                                    in1=biasP[:, g, :].unsqueeze(1).to_broadcast([128, nt_, 128]), op=ALU.add), (Bm, B_biasP), (B_mb[s],))
                            else:
                                k.op("dve", lambda bm=bm, s=s, g=g: nc.vector.tensor_tensor(
                                    out=v8(mb[:, s, 0:128]), in0=v8(bm[:, 0:128]),
                                    in1=biasP[:, g, 0:8].unsqueeze(1).to_broadcast([128, 16, 8]), op=ALU.add), (Bm, B_biasP), (B_mb[s],))
                            k.op("dve", lambda w=w, s=s, g=g, c0=c0: nc.vector.tensor_tensor(
                                out=oaT[:, g, c0:c0 + w], in0=ug[:, s, 0:w], in1=mb[:, s, 0:w], op=ALU.mult),
                                (B_ug[s], B_mb[s]), (B_oa[(g, i)],))
                    k.barrier()
                pvu.close()
                pv.close()
                k.mark("g%d ret" % gi)
                with ExitStack() as p2:
                    D6 = 6
                    srg2 = sb(p2, "srg", [128, 2, 2, TMAX], F32); B_srg2 = [[Buf() for _ in range(len(ntl))] for _ in range(2)]
                    orT2 = sb(p2, "orT", [128, 2, 2, TMAX], F32); B_or2 = [[Buf() for _ in range(ntile)] for _ in range(2)]
                    ssb2 = sb(p2, "ssb", [128, 2, TMAX], F32); B_ssb2 = [[Buf() for _ in range(ntile)] for _ in range(2)]
                    qkf = sb(p2, "qkf", [128, 2, 256], F32); B_qkf = [Buf(), Buf()]
                    rt = sb(p2, "rt", [128, 2, 4, 128], F32); B_rt = [[Buf() for _ in range(4)] for _ in range(2)]
                    qkr = sb(p2, "qkr", [128, D6, 256], BF16); B_qkr = [Buf() for _ in range(D6)]
                    qkT = sb(p2, "qkT", [128, D6, 256], BF16); B_qkT = [Buf() for _ in range(D6)]
                    rvb = sb(p2, "rvb", [128, D6, 256], BF16); B_rvb = [Buf() for _ in range(D6)]
                    scT = sb(p2, "scT", [128, 2, 128], BF16); B_scT = [Buf(), Buf()]
                    sqh = sb(p2, "sqh", [128, 2, 2, 128], BF16); B_sqh = [Buf(), Buf()]
                    B_pbq = [Buf() for _ in range(4)]
                    if has_s:
                        S16 = sb(p2, "S16", [128, 16, 256], BF16); B_S16 = Buf()
                    items = [(h, ti) for h in range(8) for ti in range(ntile)]
                    NI = len(items)
                    hctx = {}
                    hctxA = {}

                    def S1(ii):
                        h, ti = items[ii]
                        kd, c = tiles[ti]
                        def loadA(hh):
                            hctxA[hh] = load_block([(0, rows(w_in, 2048 + hh * 128, 2048 + (hh + 1) * 128)),
                                                    (1, rows(w_in, 3072 + hh * 128, 3072 + (hh + 1) * 128)),
                                                    (2, rows(w_in, 4096 + hh * 256, 4096 + (hh + 1) * 256))], qkv=True)
                        if ti == 0:
                            if h == 0:
                                loadA(0)
                            slotB, BsB = load_block([(0, rows(w_in, 6144 + h * 256, 6144 + (h + 1) * 256))])
                            hctx[h] = hctxA[h] + (slotB, BsB)
                            next(agen2, None)
                            if has_s and h == 0:
                                k.pool_after_barrier()
                                k.dma("pool", [(S16[:, :, :], st_in[:, 0, :, :].rearrange("j d e -> d j e"))], W=B_S16)
                        if ti == 2 and h < 7:
                            loadA(h + 1)
                        slotA, BsA, slotB, BsB = hctx[h]
                        wA = wview(slotA, 0, 8, 512)
                        i = nt_of_tile(ti)
                        tc0 = ti * 128
                        kidx = 0 if kd == "p" else 1
                        p2_, p6 = ii % 2, ii % D6
                        ba, Bba = bank()
                        k.mm(ba[:, 0:512], [(hT[:, kk, tc0:tc0 + 128], wA[:, kk, :]) for kk in range(8)], (BsA,) + bh_all(i), (Bba,))
                        dq = cst[:, DEC0 + kidx * 16 + h:DEC0 + kidx * 16 + h + 1]
                        dk = cst[:, DEC0 + kidx * 16 + 8 + h:DEC0 + kidx * 16 + 8 + h + 1]
                        k.op("act", lambda: nc.scalar.activation(out=qkf[:, p2_, 0:128], in_=ba[:, 0:128], func=AF.Copy, scale=dq),
                             (Bba, B_cst), (B_qkf[p2_],))
                        k.op("act", lambda: nc.scalar.activation(out=qkf[:, p2_, 128:256], in_=ba[:, 128:256], func=AF.Copy, scale=dk),
                             (Bba, B_cst), (B_qkf[p2_],))
                        k.op("act", lambda: nc.scalar.activation(out=rvb[:, p6, :], in_=ba[:, 256:512], func=AF.Copy), (Bba,), (B_rvb[p6],))

                    def S1b(ii):
                        h, ti = items[ii]
                        kd, c = tiles[ti]
                        p2_, p6 = ii % 2, ii % D6
                        tt_ = 16 if kd == "s" else c
                        cosb = cst[:, ROT0 + tt_ * 128:ROT0 + tt_ * 128 + 64].unsqueeze(1).to_broadcast([128, 2, 64])
                        sinb = cst[:, ROT0 + tt_ * 128 + 64:ROT0 + tt_ * 128 + 128].unsqueeze(1).to_broadcast([128, 2, 64])
                        qv = qkf[:, p2_, :].rearrange("p (a b i) -> p a b i", a=2, b=2)
                        x1 = qv[:, :, 0, :]
                        x2 = qv[:, :, 1, :]
                        ov = qkr[:, p6, :].rearrange("p (a b i) -> p a b i", a=2, b=2)
                        Brt = B_rt[p2_]

                        def r4(j):
                            return rt[:, p2_, j, :].rearrange("p (a i) -> p a i", a=2)
                        k.op("dve", lambda: nc.vector.tensor_tensor(out=r4(0), in0=x1, in1=cosb, op=ALU.mult), (B_qkf[p2_], B_cst), (Brt[0],))
                        k.op("dve", lambda: nc.vector.tensor_tensor(out=r4(1), in0=x2, in1=sinb, op=ALU.mult), (B_qkf[p2_], B_cst), (Brt[1],))
                        k.op("dve", lambda: nc.vector.tensor_tensor(out=r4(2), in0=x1, in1=sinb, op=ALU.mult), (B_qkf[p2_], B_cst), (Brt[2],))
                        k.op("dve", lambda: nc.vector.tensor_tensor(out=r4(3), in0=x2, in1=cosb, op=ALU.mult), (B_qkf[p2_], B_cst), (Brt[3],))
                        k.op("dve", lambda: nc.vector.tensor_tensor(out=ov[:, :, 0, :], in0=r4(0), in1=r4(1), op=ALU.subtract),
                             (Brt[0], Brt[1]), (B_qkr[p6],))
                        k.op("dve", lambda: nc.vector.tensor_tensor(out=ov[:, :, 1, :], in0=r4(2), in1=r4(3), op=ALU.add),
                             (Brt[2], Brt[3]), (B_qkr[p6],))

                    def S2(ii):
                        p6 = ii % D6
                        pq = ii % 4
                        k.pe([(lambda a=a: nc.tensor.transpose(out=pbf[:, pq * 256 + a * 128:pq * 256 + (a + 1) * 128],
                                                               in_=qkr[:, p6, a * 128:(a + 1) * 128], identity=identB[:])) for a in range(2)],
                             (B_qkr[p6], B_idb), (B_pbf,))
                        k.op("dve", lambda: nc.vector.tensor_copy(out=qkT[:, p6, :], in_=pbf[:, pq * 256:(pq + 1) * 256]), (B_pbf,), (B_qkT[p6],))

                    def S3(ii):
                        h, ti = items[ii]
                        kd, c = tiles[ti]
                        p2_, p6 = ii % 2, ii % D6
                        bs_, Bbs = bank()
                        k.mm(bs_[:, 0:128], [(qkT[:, p6, 128:256], qkT[:, p6, 0:128])], (B_qkT[p6],), (Bbs,))
                        mk = cst[:, TUP0:TUP0 + 128] if kd == "p" else cst[:, MKS0:MKS0 + 128]
                        k.op("dve", lambda: nc.vector.tensor_tensor(out=scT[:, p2_, :], in0=bs_[:, 0:128], in1=mk, op=ALU.mult),
                             (Bbs, B_cst), (B_scT[p2_],))

                    def S4(ii):
                        h, ti = items[ii]
                        kd, c = tiles[ti]
                        p2_, p6 = ii % 2, ii % D6
                        tc0 = ti * 128
                        gL = gam[h] ** 128
                        first = (kd == "p" and c == 0)
                        orT = orT2[:, h % 2, :, :]
                        B_or = B_or2[h % 2]
                        qdT = qkT[:, p6, 0:128]
                        bo, Bbo = bank()
                        fns = []
                        use_inter = (kd == "p" and not first)
                        for hf in range(2):
                            fns.append(lambda hf=hf: nc.tensor.matmul(
                                bo[:, hf * 128:(hf + 1) * 128], lhsT=rvb[:, p6, hf * 128:(hf + 1) * 128], rhs=scT[:, p2_, :], start=True, stop=(not use_inter)))
                            if use_inter:
                                fns.append(lambda hf=hf: nc.tensor.matmul(
                                    bo[:, hf * 128:(hf + 1) * 128], lhsT=Sbf[:, h, hf * 128:(hf + 1) * 128], rhs=qdT, start=False, stop=True))
                        k.pe(fns, (B_rvb[p6], B_scT[p2_], B_qkT[p6], B_Sbf[h]), (Bbo,))
                        k.op("act", lambda: nc.scalar.activation(
                            out=orT[:, :, tc0:tc0 + 128], in_=bo[:, 0:256].rearrange("p (a t) -> p a t", a=2), func=AF.Copy), (Bbo,), (B_or[ti],))
                        if kd == "p":
                            bkv, Bkv = bank()
                            k.mm(bkv[:, 0:256], [(qkr[:, p6, 128:256], rvb[:, p6, :])], (B_qkr[p6], B_rvb[p6]), (Bkv,))
                            if first:
                                k.op("dve", lambda: nc.vector.tensor_copy(out=U[:, h, :], in_=bkv[:, 0:256]), (Bkv,), (B_U[h],))
                            else:
                                k.op("dve", lambda: nc.vector.scalar_tensor_tensor(
                                    out=U[:, h, :], in0=U[:, h, :], scalar=gL, in1=bkv[:, 0:256], op0=ALU.mult, op1=ALU.add),
                                    (Bkv, B_U[h]), (B_U[h],))
                            k.op("act", lambda: nc.scalar.activation(out=sqh[:, p2_, :, :], in_=bo[:, 0:256].rearrange("p (a t) -> p a t", a=2),
                                                                     func=AF.Square), (Bbo,), (B_sqh[p2_],))
                            k.op("act", lambda: nc.scalar.activation(out=Sbf[:, h, :], in_=U[:, h, :], func=AF.Copy, scale=gL),
                                 (B_U[h],), (B_Sbf[h],))
                        else:
                            bi_, Bbi = banks[6], B_bank[6]
                            fns = []
                            for j in range(16):
                                for hf in range(2):
                                    fns.append(lambda j=j, hf=hf: nc.tensor.matmul(
                                        bi_[:, hf * 128 + 8 * j:hf * 128 + 8 * j + 8], lhsT=S16[:, j, hf * 128:(hf + 1) * 128],
                                        rhs=qkT[:, p6, 8 * j:8 * j + 8], start=True, stop=True))
                            k.pe(fns, (B_S16, B_qkT[p6]), (Bbi,))
                            if h < 7:
                                k.dma("pool", [(S16[:, :, :], st_in[:, h + 1, :, :].rearrange("j d e -> d j e"))], W=B_S16)
                            k.op("dve", lambda: nc.vector.tensor_scalar(out=ksrv[:, h, 0:128], in0=qkr[:, p6, 128:256], scalar1=gam[h] ** 8,
                                                                        scalar2=None, op0=ALU.mult), (B_qkr[p6],), (B_ksrv[h],))
                            k.op("dve", lambda: nc.vector.tensor_copy(out=ksrv[:, h, 128:384], in_=rvb[:, p6, :]), (B_rvb[p6],), (B_ksrv[h],))
                            k.op("dve", lambda: nc.vector.tensor_tensor(
                                out=orT[:, :, tc0:tc0 + 128], in0=bi_[:, 0:256].rearrange("p (a t) -> p a t", a=2),
                                in1=orT[:, :, tc0:tc0 + 128], op=ALU.add), (Bbi, B_or[ti]), (B_or[ti],))
                            k.op("act", lambda: nc.scalar.activation(out=sqh[:, p2_, :, :], in_=orT[:, :, tc0:tc0 + 128], func=AF.Square),
                                 (B_or[ti],), (B_sqh[p2_],))

                    def S5(ii):
                        h, ti = items[ii]
                        p2_ = ii % 2
                        tc0 = ti * 128
                        bss, Bss = bank()
                        k.mm(bss[:, 0:128], [(ones256[:], sqh[:, p2_, 0, :]), (ones256[:], sqh[:, p2_, 1, :])], (B_sqh[p2_], B_ones2), (Bss,))
                        k.op("dve", lambda: nc.vector.tensor_copy(out=ssb2[:, h % 2, tc0:tc0 + 128], in_=bss[:, 0:128]), (Bss,), (B_ssb2[h % 2][ti],))

                    def RG(ii):
                        h, ti = items[ii]
                        if ti != ntile - 1:
                            return
                        slotA, BsA, slotB, BsB = hctx[h]
                        wg_ = wview(slotB, 0, 8, 256)
                        for i, (c0, w, kd) in enumerate(ntl):
                            for hf in range(2):
                                bk, Bb = bank()
                                k.mm(bk[:, 0:w], [(wg_[:, kk, hf * 128:(hf + 1) * 128], hT[:, kk, c0:c0 + w]) for kk in range(8)],
                                     (BsB,) + bh_all(i), (Bb,))
                                k.op("act", lambda bk=bk, w=w, hf=hf, c0=c0: nc.scalar.activation(
                                    out=srg2[:, h % 2, hf, c0:c0 + w], in_=bk[:, 0:w], func=AF.Silu), (Bb,), (B_srg2[h % 2][i],))

                    def E1(ii):
                        h, ti = items[ii]
                        if ti != ntile - 1:
                            return
                        par = h % 2
                        k.op("act", lambda: nc.scalar.activation(out=ssb2[:, par, 0:T], in_=ssb2[:, par, 0:T], func=AF.Ln, bias=EPS),
                             tuple(B_ssb2[par]), tuple(B_ssb2[par]))
                        k.op("act", lambda: nc.scalar.activation(out=ssb2[:, par, 0:T], in_=ssb2[:, par, 0:T], func=AF.Exp, scale=-0.5),
                             tuple(B_ssb2[par]), tuple(B_ssb2[par]))

                    def E2(ii):
                        h, ti = items[ii]
                        if ti != ntile - 1:
                            return
                        par = h % 2
                        for i, (c0, w, kd) in enumerate(ntl):
                            tl = [t_ for t_ in range(ntile) if nt_of_tile(t_) == i]
                            rd = tuple(B_or2[par][t_] for t_ in tl)
                            k.op("dve", lambda c0=c0, w=w: nc.vector.tensor_tensor(
                                out=orT2[:, par, :, c0:c0 + w], in0=orT2[:, par, :, c0:c0 + w],
                                in1=ssb2[:, par, c0:c0 + w].unsqueeze(1).to_broadcast([128, 2, w]),
                                op=ALU.mult), rd + tuple(B_ssb2[par][t_] for t_ in tl), rd)
                            k.op("dve", lambda c0=c0, w=w: nc.vector.tensor_tensor(
                                out=obT[:, 2 * h:2 * h + 2, c0:c0 + w], in0=orT2[:, par, :, c0:c0 + w], in1=srg2[:, par, :, c0:c0 + w], op=ALU.mult),
                                rd + (B_srg2[par][i],), (B_ob[(2 * h, i)], B_ob[(2 * h + 1, i)]))

                    LAG = (0, 2, 3, 4, 5, 0, 0, 8, 9)
                    stages = (S1, S2, S3, S4, S5, S1b, RG, E1, E2)
                    for s in range(NI + 10):
                        for st in (0, 1, 2, 4, 3, 7, 8, 5, 6):
                            ii = s - LAG[st]
                            if 0 <= ii < NI:
                                stages[st](ii)
                    for _ in agen2:
                        pass
                    if gi == len(GROUPS) - 1:
                        Ufin = sb(p2, "Ufin", [128, 8, 256], F32); B_Ufin = Buf()
                        for h in range(8):
                            k.op("act", lambda h=h: nc.scalar.activation(out=Ufin[:, h, :], in_=U[:, h, :], func=AF.Copy, scale=gam[h] ** 128),
                                 (B_U[h],), (B_Ufin,))
                        k.dma("sp", [(sp_out.rearrange("h d e -> d h e"), Ufin[:])], R=B_Ufin)
                    k.barrier()
                if debug and has_s:
                    Bd = Buf()
                    k.dma("sp", [(dbg_oa[:, :, :], oaT[:, :, PT:PT + 128])], R=Bd)
                    k.dma("sp", [(dbg_ob[:, :, :], obT[:, :, PT:PT + 128])], R=Bd)
                k.mark("g%d merge" % gi)
                with ExitStack() as p2:
                    mgT = sb(p2, "mgT", [128, 8, TMAX], BF16)
                    B_mg = {(dc, i): Buf() for dc in range(8) for i in range(len(ntl))}
                    sga = sb(p2, "sga", [128, 384], F32); B_sga = Buf()
                    sgb = sb(p2, "sgb", [128, 384], F32); B_sgb = Buf()
                    m1 = sb(p2, "m1", [128, 384], F32); B_m1 = Buf()
                    m2 = sb(p2, "m2", [128, 384], F32); B_m2 = Buf()
                    rtmp = sb(p2, "rtmp2", [128, 128], F32); B_rtmp = Buf()

                    for dc in range(8):
                        cs = slice(dc * 128, (dc + 1) * 128)
                        slotA, BsA = load_block([(0, w_a[:, cs].rearrange("(j p) c -> p j c", p=128)),
                                                 (1024, w_b[:, cs].rearrange("(j p) c -> p j c", p=128)),
                                                 (3072, rows(w_in, 8192 + dc * 128, 8192 + (dc + 1) * 128))])
                        slotB, BsB = load_block([(0, rows(w_in, 9216 + dc * 128, 9216 + (dc + 1) * 128))])
                        wa_ = wview(slotA, 0, 8, 128)
                        wb_ = wview(slotA, 1024, 16, 128)
                        wga = wview(slotA, 3072, 8, 128)
                        wgb = wview(slotB, 0, 8, 128)
                        for i, (c0, w, kd) in enumerate(ntl):
                            bpa, Bpa = bank()
                            bpb, Bpb = bank()
                            bga, Bga = bank()
                            bgb, Bgb = bank()
                            k.mm(bga[:, 0:w], [(wga[:, kk, :], hT[:, kk, c0:c0 + w]) for kk in range(8)], (BsA,) + bh_all(i), (Bga,))
                            k.mm(bgb[:, 0:w], [(wgb[:, kk, :], hT[:, kk, c0:c0 + w]) for kk in range(8)], (BsB,) + bh_all(i), (Bgb,))
                            k.mm(bpa[:, 0:w], [(wa_[:, j, :], oaT[:, j, c0:c0 + w]) for j in range(8)],
                                 (BsA,) + tuple(B_oa[(j, i)] for j in range(8)), (Bpa,))
                            k.mm(bpb[:, 0:w], [(wb_[:, j, :], obT[:, j, c0:c0 + w]) for j in range(16)],
                                 (BsA,) + tuple(B_ob[(j, i)] for j in range(16)), (Bpb,))
                            k.op("act", lambda bga=bga, w=w: nc.scalar.activation(out=sga[:, 0:w], in_=bga[:, 0:w], func=AF.Sigmoid), (Bga,), (B_sga,))
                            k.op("act", lambda bgb=bgb, w=w: nc.scalar.activation(out=sgb[:, 0:w], in_=bgb[:, 0:w], func=AF.Sigmoid), (Bgb,), (B_sgb,))
                            k.op("dve", lambda bpa=bpa, w=w: nc.vector.tensor_tensor(out=m1[:, 0:w], in0=bpa[:, 0:w], in1=sga[:, 0:w], op=ALU.mult),
                                 (Bpa, B_sga), (B_m1,))
                            k.op("dve", lambda bpb=bpb, w=w: nc.vector.tensor_tensor(out=m2[:, 0:w], in0=bpb[:, 0:w], in1=sgb[:, 0:w], op=ALU.mult),
                                 (Bpb, B_sgb), (B_m2,))
                            k.op("dve", lambda w=w, dc=dc, c0=c0: nc.vector.tensor_tensor(out=mgT[:, dc, c0:c0 + w], in0=m1[:, 0:w], in1=m2[:, 0:w], op=ALU.add),
                                 (B_m1, B_m2), (B_mg[(dc, i)],))
                    for blk in range(2):
                        slot, Bs = load_block([(0, rows(w_o, blk * 512, (blk + 1) * 512))])
                        wo_ = wview(slot, 0, 8, 512)
                        for q in range(4):
                            dc = blk * 4 + q
                            for i, (c0, w, kd) in enumerate(ntl):
                                bk, Bb = bank()
                                k.mm(bk[:, 0:w], [(wo_[:, j, q * 128:(q + 1) * 128], mgT[:, j, c0:c0 + w]) for j in range(8)],
                                     (Bs,) + tuple(B_mg[(j, i)] for j in range(8)), (Bb,))
                                resid(1, dc, i, bk, Bb, rtmp, B_rtmp)
                    if debug and has_s:
                        Bd2 = Buf()
                        k.dma("sp", [(dbg_mg[:, :, :], mgT[:, :, PT:PT + 128])], R=Bd2)
                    k.barrier()

            ffn(2, w2g, w2u, w2d)

            k.mark("g%d final" % gi)
            with ExitStack() as ps:
                sq = sb(ps, "fsq", [128, 8, 384], BF16); B_sq = [Buf() for _ in range(8)]
                sd = sb(ps, "fsd", [128, 384], F32); B_sd = Buf()
                rstd = sb(ps, "frstd", [128, TMAX], F32); B_rstd = [Buf() for _ in range(len(ntl))]
                yT = sb(ps, "yT", [128, 8, TMAX], F32); B_y = {(kk, i): Buf() for kk in range(8) for i in range(len(ntl))}
                yo = [sb(ps, "yo%d" % i, [128, D], F32) for i in range(2)]; B_yo = [Buf(), Buf()]
                for i, (c0, w, kd) in enumerate(ntl):
                    for kk in range(8):
                        k.op("act", lambda kk=kk, c0=c0, w=w: nc.scalar.activation(
                            out=sq[:, kk, 0:w], in_=xT[:, kk, c0:c0 + w], func=AF.Square), (B_x[(kk, i)],), (B_sq[kk],))
                    bk, Bb = bank()
                    k.mm(bk[:, 0:w], [(onesM[:], sq[:, kk, 0:w]) for kk in range(8)], tuple(B_sq) + (B_ones,), (Bb,))
                    k.op("act", lambda bk=bk, w=w: nc.scalar.activation(out=sd[:, 0:w], in_=bk[:, 0:w], func=AF.Ln, bias=EPS), (Bb,), (B_sd,))
                    k.op("act", lambda c0=c0, w=w: nc.scalar.activation(out=rstd[:, c0:c0 + w], in_=sd[:, 0:w], func=AF.Exp, scale=-0.5), (B_sd,), (B_rstd[i],))
                    for kk in range(8):
                        k.op("dve", lambda kk=kk, c0=c0, w=w: nc.vector.scalar_tensor_tensor(
                            out=yT[:, kk, c0:c0 + w], in0=xT[:, kk, c0:c0 + w], scalar=vecT[:, 96 + kk:97 + kk], in1=rstd[:, c0:c0 + w],
                            op0=ALU.mult, op1=ALU.mult), (B_x[(kk, i)], B_rstd[i], B_vecT), (B_y[(kk, i)],))
                for ti, (kd, c) in enumerate(tiles):
                    i = nt_of_tile(ti)
                    s = ti % 2
                    for hf in range(2):
                        bk, Bb = bank()
                        k.pe([(lambda q=q, bk=bk, hf=hf, ti=ti: nc.tensor.transpose(
                            out=bk[:, q * 128:(q + 1) * 128], in_=yT[:, hf * 4 + q, ti * 128:(ti + 1) * 128],
                            identity=cst[:, IDN0:IDN0 + 128])) for q in range(4)],
                            tuple(B_y[(hf * 4 + q, i)] for q in range(4)) + (B_cst,), (Bb,))
                        if hf == 0:
                            k.op("act", lambda bk=bk, s=s: nc.scalar.activation(out=yo[s][:, 0:512], in_=bk[:, :], func=AF.Copy), (Bb,), (B_yo[s],))
                        else:
                            k.op("dve", lambda bk=bk, s=s: nc.vector.tensor_copy(out=yo[s][:, 512:1024], in_=bk[:, :]), (Bb,), (B_yo[s],))
                    dst = y_p[c * 128:(c + 1) * 128, :] if kd == "p" else y_s[:, :]
                    k.dma("sp", [(dst, yo[s][:])], R=B_yo[s])
                k.barrier()
            gs.close()

        for t in k.stores:
            k._wait("sp", t)
        k.mark("end")
    build.marks = k.marks
    return nc


_NC_CACHE = {}


def kernel(x_prompt, x_sample, state_ret, c_prompt, c_sample, w_ada, b_ada, n1_g,
           w1_gate, w1_up, w1_down, nm_g, w_in, gm_ln_g, gm_ln_b, gm_ws, gm_bs,
           w_a, w_b, w_o, n2_g, w2_gate, w2_up, w2_down, final_g):
    f = lambda a: np.ascontiguousarray(np.asarray(a, dtype=np.float32))
    x_prompt = f(x_prompt); x_sample = f(x_sample); state_ret = f(state_ret)
    c_prompt = f(c_prompt); c_sample = f(c_sample)
    vecs = np.concatenate([f(b_ada).reshape(72, 128), f(n1_g).reshape(8, 128), f(nm_g).reshape(8, 128),
                           f(n2_g).reshape(8, 128), f(final_g).reshape(8, 128)], axis=0)
    lngb = np.stack([f(gm_ln_g).reshape(D), f(gm_ln_b).reshape(D)], axis=0)
    shared = {
        "w_ada": f(w_ada)[0], "vecs": np.ascontiguousarray(vecs),
        "w1g": f(w1_gate)[0], "w1u": f(w1_up)[0], "w1d": f(w1_down)[0],
        "w2g": f(w2_gate)[0], "w2u": f(w2_up)[0], "w2d": f(w2_down)[0],
        "w_in": f(w_in)[0], "lngb": np.ascontiguousarray(lngb),
        "gm_ws": f(gm_ws)[0], "gm_bs": f(gm_bs).reshape(D),
        "w_a": f(w_a)[0], "w_b": f(w_b)[0], "w_o": f(w_o)[0],
        "cst": _host_consts(),
    }
    in_maps = []
    for b in range(NCORES):
        m = dict(shared)
        m["x_p"] = x_prompt[b]
        m["x_s"] = np.ascontiguousarray(x_sample[16 * b:16 * b + 16].reshape(128, D))
        m["c_in"] = np.ascontiguousarray(np.concatenate([c_prompt[b:b + 1], c_sample[16 * b:16 * b + 16]], axis=0))
        m["st_in"] = np.ascontiguousarray(state_ret[0, 16 * b:16 * b + 16])
        in_maps.append(m)
    nc = build()
    res = run_bass_kernel_spmd(nc, in_maps, core_ids=list(range(NCORES)))
    rs = res.results
    y_prompt = np.stack([rs[b]["y_p"] for b in range(NCORES)], axis=0).astype(np.float32)
    y_sample = np.concatenate([rs[b]["y_s"].reshape(16, 8, D) for b in range(NCORES)], axis=0).astype(np.float32)
    sp = np.stack([rs[b]["sp_out"] for b in range(NCORES)], axis=0)[None].astype(np.float32)
    ss = np.concatenate([rs[b]["ss_out"] for b in range(NCORES)], axis=0)[None].astype(np.float32)
    vs = np.concatenate([rs[b]["vs_out"].reshape(16, 8, D) for b in range(NCORES)], axis=0)[None].astype(np.float32)
    return (y_prompt, y_sample, sp, ss, vs)
```

```python
import numpy as np
from contextlib import ExitStack
import concourse.bass as bass
import concourse.mybir as mybir
from concourse.bass_utils import run_bass_kernel_spmd

F32 = mybir.dt.float32
BF16 = mybir.dt.bfloat16
AF = mybir.ActivationFunctionType
ALU = mybir.AluOpType

D = 1024
DFF = 2816
NJ = DFF // 128
EPS = 1e-6
NCORES = 8
ROT0 = 0
DEC0 = ROT0 + 17 * 128
TLO0 = DEC0 + 32
TUP0 = TLO0 + 128
MKS0 = TUP0 + 128
SEL0 = MKS0 + 128
IDN0 = SEL0 + 16
NCST = IDN0 + 128

GROUPS = [
    dict(chunks=[0, 1, 2, 3, 4], sample=True),
    dict(chunks=[5, 6, 7, 8, 9, 10], sample=False),
    dict(chunks=[11, 12, 13, 14, 15], sample=False),
]
TMAX = 768
NSLOT = 4
SLOT = 4096


def _gammas():
    return [1.0 - 2.0 ** (-5.0 - h) for h in range(8)]


def _host_consts():
    cst = np.zeros((128, NCST), np.float32)
    half = 64
    inv = (np.float32(10000.0) ** (-(np.arange(half, dtype=np.float32) / np.float32(half)))).astype(np.float32)
    l = np.arange(128)
    rot = np.zeros((128, 17, 2, 64), np.float32)
    for t in range(17):
        pos = (t * 128 + l) if t < 16 else (16384 + (l % 8))
        ang = (pos.astype(np.float32)[:, None] * inv[None, :]).astype(np.float32)
        rot[:, t, 0, :] = np.cos(ang.astype(np.float64)).astype(np.float32)
        rot[:, t, 1, :] = np.sin(ang.astype(np.float64)).astype(np.float32)
    cst[:, ROT0:ROT0 + 17 * 128] = rot.reshape(128, -1)
    g = np.array(_gammas(), np.float64)
    dec = np.zeros((128, 2, 2, 8), np.float64)
    for kind in range(2):
        ll = (l if kind == 0 else (l % 8)).astype(np.float64)
        dec[:, kind, 0, :] = g[None, :] ** (ll[:, None] + 1.0)
        dec[:, kind, 1, :] = g[None, :] ** (-(ll[:, None] + 1.0)) * (128.0 ** -0.5)
    cst[:, DEC0:DEC0 + 32] = dec.reshape(128, 32).astype(np.float32)
    r = l[:, None]
    c = l[None, :]
    cst[:, TLO0:TLO0 + 128] = (c <= r)
    cst[:, TUP0:TUP0 + 128] = (c >= r)
    cst[:, MKS0:MKS0 + 128] = ((c >= r) & ((c // 8) == (r // 8)))
    cst[:, SEL0:SEL0 + 16] = ((l[:, None] // 8) == np.arange(16)[None, :])
    cst[:, IDN0:IDN0 + 128] = np.eye(128)
    return cst


class Buf:
    __slots__ = ("w", "r", "dsem", "dcnt")

    def __init__(self):
        self.w = None
        self.r = {}
        self.dsem = None
        self.dcnt = 0


class K:
    def __init__(self, nc, es):
        self.nc = nc
        self.es = es
        self.eng = {"pe": nc.tensor, "act": nc.scalar, "dve": nc.vector, "pool": nc.gpsimd, "sp": nc.sync}
        self.sem = {}
        self.cnt = {}
        for e in ("pe", "act", "dve"):
            self.sem[e] = es.enter_context(nc.semaphore("s_" + e))
            self.cnt[e] = 0
        self.waited = {e: {} for e in self.eng}
        self.nsem = 0
        self.stores = []
        self.last = {}
        self.npe = 0
        self.marks = []
        self.bar = {}

    def _wait(self, e, tok):
        if tok is None:
            return
        sem, val, key = tok
        if e == "pe" and key == "pe":
            return
        if self.waited[e].get(key, 0) >= val:
            return
        self.eng[e].wait_ge(sem, val)
        self.waited[e][key] = val

    def _pre(self, e, reads, writes):
        for b in reads:
            self._wait(e, b.w)
        for b in writes:
            self._wait(e, b.w)
            for t in b.r.values():
                self._wait(e, t)

    def _post(self, e, inst, reads, writes):
        self.cnt[e] += 1
        inst.then_inc(self.sem[e], 1)
        tok = (self.sem[e], self.cnt[e], e)
        self.last[e] = tok
        for b in reads:
            b.r[e] = tok
        for b in writes:
            b.w = tok
            b.r = {}
        return tok

    def op(self, e, fn, reads=(), writes=()):
        self._pre(e, reads, writes)
        inst = fn()
        return self._post(e, inst, reads, writes)

    def pe(self, fns, reads=(), writes=()):
        self._pre("pe", reads, writes)
        inst = None
        for f in fns:
            inst = f()
        self.npe += len(fns)
        return self._post("pe", inst, reads, writes)

    def mark(self, name):
        self.marks.append((name, self.npe))

    def mm(self, out_ap, pairs, reads, writes):
        n = len(pairs)
        nc = self.nc
        fns = []
        for i, (l, r) in enumerate(pairs):
            fns.append(lambda l=l, r=r, i=i: nc.tensor.matmul(out_ap, lhsT=l, rhs=r, start=(i == 0), stop=(i == n - 1)))
        return self.pe(fns, reads, writes)

    def dma(self, q, pairs, W=None, R=None):
        eng = self.eng[q]
        if W is not None:
            self._pre(q, (), (W,))
            b = W
        else:
            self._pre(q, (R,), ())
            b = R
        if b.dsem is None:
            b.dsem = self.es.enter_context(self.nc.semaphore("d%d" % self.nsem))
            self.nsem += 1
        for (o, i) in pairs:
            eng.dma_start(out=o, in_=i).then_inc(b.dsem, 16)
            b.dcnt += 1
        tok = (b.dsem, 16 * b.dcnt, "d%d" % id(b))
        if W is not None:
            W.w = tok
            W.r = {}
        else:
            R.r["dma"] = tok
            self.stores.append(tok)
        return tok

    def pool_after_barrier(self):
        for f, t in self.bar.items():
            self._wait("pool", t)

    def barrier(self):
        self.bar = dict(self.last)
        for e in ("pe", "act", "dve", "sp"):
            for f in ("pe", "act", "dve"):
                if f != e and f in self.last:
                    self._wait(e, self.last[f])
        for e in ("pe", "act", "dve"):
            for t in self.stores:
                self._wait(e, t)


def build(debug=False):
    nc = bass.Bass("TRN2", target_bir_lowering=False)

    def din(name, shape):
        return nc.dram_tensor(name, shape, F32, kind="ExternalInput").ap()

    def dout(name, shape):
        return nc.dram_tensor(name, shape, F32, kind="ExternalOutput").ap()

    x_p = din("x_p", [2048, D])
    x_s = din("x_s", [128, D])
    c_d = din("c_in", [17, D])
    st_in = din("st_in", [16, 8, 128, 256])
    w_ada = din("w_ada", [D, 9 * D])
    vecs_d = din("vecs", [104, 128])
    w1g = din("w1g", [D, DFF]); w1u = din("w1u", [D, DFF]); w1d = din("w1d", [DFF, D])
    w2g = din("w2g", [D, DFF]); w2u = din("w2u", [D, DFF]); w2d = din("w2d", [DFF, D])
    w_in = din("w_in", [D, 10 * D])
    lngb_d = din("lngb", [2, D])
    gmws_d = din("gm_ws", [8, 128, 128])
    gmbs_d = din("gm_bs", [D])
    w_a = din("w_a", [D, D]); w_b = din("w_b", [2 * D, D]); w_o = din("w_o", [D, D])
    cst_d = din("cst", [128, NCST])
    y_p = dout("y_p", [2048, D]); y_s = dout("y_s", [128, D])
    sp_out = dout("sp_out", [8, 128, 256]); ss_out = dout("ss_out", [16, 8, 128, 256])
    vs_out = dout("vs_out", [128, D])

    gam = _gammas()
    if debug:
        dbg_oa = nc.dram_tensor("dbg_oa", [128, 8, 128], BF16, kind="ExternalOutput").ap()
        dbg_ob = nc.dram_tensor("dbg_ob", [128, 16, 128], BF16, kind="ExternalOutput").ap()
        dbg_mg = nc.dram_tensor("dbg_mg", [128, 8, 128], BF16, kind="ExternalOutput").ap()

    with ExitStack() as es:
        k = K(nc, es)

        _uid = [0]

        def sb(st, name, shape, dt):
            _uid[0] += 1
            return st.enter_context(nc.sbuf_tensor("%s_%d" % (name, _uid[0]), shape, dt))

        cst = sb(es, "cst", [128, NCST], F32); B_cst = Buf()
        identB = sb(es, "identB", [128, 128], BF16); B_idb = Buf()
        onesM = sb(es, "onesM", [128, 128], BF16); B_ones = Buf()
        ones256 = sb(es, "ones256", [128, 128], BF16); B_ones2 = Buf()
        vecT = sb(es, "vecT", [128, 104], F32); B_vecT = Buf()
        mT = sb(es, "mT", [128, 72, 17], F32); B_mT = Buf()
        Gm = sb(es, "Gm", [128, 3, 8, 17], F32); B_G = Buf()
        GTm = sb(es, "GTm", [128, 3, 8, 17], F32); B_GT = Buf()
        WT = sb(es, "WT", [128, 8, 128], BF16); B_WT = Buf()
        WTs = sb(es, "WTs", [128, 8, 128], BF16); B_WTs = Buf()
        cT = sb(es, "cT", [128, 8, 17], BF16); B_cT = Buf()
        ksrv = sb(es, "ksrv", [128, 8, 384], BF16); B_ksrv = [Buf() for _ in range(8)]
        U = sb(es, "U", [128, 8, 256], F32); B_U = [Buf() for _ in range(8)]
        Sbf = sb(es, "Sbf", [128, 8, 256], BF16); B_Sbf = [Buf() for _ in range(8)]
        ring = [sb(es, "ring%d" % i, [128, SLOT], BF16) for i in range(NSLOT)]
        B_ring = [Buf() for _ in range(NSLOT)]
        xT = sb(es, "xT", [128, 8, TMAX], F32)
        hT = sb(es, "hT", [128, 8, TMAX], BF16)
        banks = [es.enter_context(nc.psum_tensor("pb%d" % i, [128, 512], F32)) for i in range(7)]
        B_bank = [Buf() for _ in range(7)]
        pbf = es.enter_context(nc.psum_tensor("pbf", [128, 1024], BF16)); B_pbf = Buf()
        rot_i = [0]
        rot_n = [6]

        def bank():
            i = rot_i[0] % rot_n[0]
            rot_i[0] = (i + 1) % rot_n[0]
            return banks[i], B_bank[i]

        ring_i = [0]

        def load_block(items, qkv=False):
            i = ring_i[0]
            ring_i[0] = (i + 1) % NSLOT
            pairs = []
            for (off, src) in items:
                kk, cc = src.shape[1], src.shape[2]
                if qkv:
                    c0_ = (0, 128, 256)[off]
                    dst = ring[i][:, :].rearrange("p (k c) -> p k c", k=8)[:, :, c0_:c0_ + cc]
                else:
                    dst = ring[i][:, off:off + kk * cc].rearrange("p (k c) -> p k c", k=kk)
                pairs.append((dst, src))
            k.dma("pool", pairs, W=B_ring[i])
            return ring[i], B_ring[i]

        def wview(slot, off, kk, cc):
            return slot[:, off:off + kk * cc].rearrange("p (k c) -> p k c", k=kk)

        def rows(w, c0, c1):
            return w[:, c0:c1].rearrange("(k p) c -> p k c", p=128)

        k.mark("setup")
        with ExitStack() as ps:
            vecs_sb = sb(ps, "vecs_sb", [128, 128], F32); B_vecs = Buf()
            c_sb = sb(ps, "c_sb", [17, D], F32); B_c = Buf()
            csil = sb(ps, "csil", [17, D], BF16); B_csil = Buf()
            gw = sb(ps, "gw", [128, 8, 128], F32); B_gw = Buf()
            gws = sb(ps, "gws", [128, 8, 128], F32); B_gws = Buf()
            gwb = sb(ps, "gwb", [128, 8, 128], BF16); B_gwb = Buf()
            gwsb = sb(ps, "gwsb", [128, 8, 128], BF16); B_gwsb = Buf()

            k.dma("sp", [(cst[:], cst_d[:, :])], W=B_cst)
            k.dma("sp", [(vecs_sb[0:104, :], vecs_d[:, :])], W=B_vecs)
            k.dma("sp", [(c_sb[0:17, :], c_d[:, :])], W=B_c)
            k.dma("sp", [(gw[:], gmws_d.rearrange("g t s -> t g s"))], W=B_gw)
            k.op("dve", lambda: nc.vector.memset(gws[:], 0.0), (), (B_gws,))
            k.dma("sp", [(gws[8 * j:8 * j + 8, :, 8 * j:8 * j + 8], gmws_d[:, 0:8, 0:8].rearrange("g t s -> t g s"))
                         for j in range(16)], W=B_gws)
            k.op("dve", lambda: nc.vector.memset(onesM[:], 1.0 / 1024.0), (), (B_ones,))
            k.op("dve", lambda: nc.vector.memset(ones256[:], 1.0 / 256.0), (), (B_ones2,))
            identF = cst[:, IDN0:IDN0 + 128]
            k.op("act", lambda: nc.scalar.activation(out=identB[:], in_=identF, func=AF.Copy), (B_cst,), (B_idb,))
            bk, Bb = bank()
            k.pe([lambda: nc.tensor.transpose(out=bk[:, 0:104], in_=vecs_sb[0:104, :], identity=cst[0:104, IDN0:IDN0 + 104])],
                 (B_vecs, B_cst), (Bb,))
            k.op("dve", lambda: nc.vector.tensor_copy(out=vecT[:], in_=bk[:, 0:104]), (Bb,), (B_vecT,))
            tlo = cst[:, TLO0:TLO0 + 128].unsqueeze(1).to_broadcast([128, 8, 128])
            k.op("dve", lambda: nc.vector.tensor_tensor(out=gwb[:], in0=gw[:], in1=tlo, op=ALU.mult), (B_gw, B_cst), (B_gwb,))
            k.op("dve", lambda: nc.vector.tensor_tensor(out=gwsb[:], in0=gws[:], in1=tlo, op=ALU.mult), (B_gws, B_cst), (B_gwsb,))
            for (src, Bs, dst, Bd) in ((gwb, B_gwb, WT, B_WT), (gwsb, B_gwsb, WTs, B_WTs)):
                k.pe([(lambda g=g, src=src: nc.tensor.transpose(out=pbf[:, g * 128:(g + 1) * 128], in_=src[:, g, :], identity=identB[:]))
                      for g in range(8)], (Bs, B_idb), (B_pbf,))
                k.op("dve", lambda dst=dst: nc.vector.tensor_copy(out=dst[:].rearrange("p g t -> p (g t)"), in_=pbf[:, :]), (B_pbf,), (Bd,))
            k.op("act", lambda: nc.scalar.activation(out=csil[0:17, :], in_=c_sb[0:17, :], func=AF.Silu), (B_c,), (B_csil,))
            k.pe([(lambda kk=kk: nc.tensor.transpose(out=pbf[:, kk * 32:kk * 32 + 17], in_=csil[0:17, kk * 128:(kk + 1) * 128],
                                                     identity=identB[0:17, 0:17])) for kk in range(8)], (B_csil, B_idb), (B_pbf,))
            k.op("dve", lambda: nc.vector.tensor_copy(out=cT[:], in_=pbf[:, 0:256].rearrange("p (k s) -> p k s", s=32)[:, :, 0:17]), (B_pbf,), (B_cT,))
            mb0, Bmb0 = bank()
            for blk in range(6):
                slot, Bs = load_block([(0, rows(w_ada, blk * 512, (blk + 1) * 512))])
                wv = wview(slot, 0, 8, 512)
                for jj in range(4):
                    j = blk * 4 + jj
                    o = mb0[:, j * 17:j * 17 + 17]
                    k.mm(o, [(wv[:, kk, jj * 128:(jj + 1) * 128], cT[:, kk, :]) for kk in range(8)], (Bs, B_cT), (Bmb0,))
            k.op("dve", lambda: nc.vector.tensor_tensor(
                out=mT[:, 0:24, :], in0=mb0[:, 0:408].rearrange("p (j s) -> p j s", s=17),
                in1=vecT[:, 0:24].unsqueeze(2).to_broadcast([128, 24, 17]), op=ALU.add), (Bmb0, B_vecT), (B_mT,))

            def mod_derive_G(n):
                sc = mT[:, (3 * n + 1) * 8:(3 * n + 1) * 8 + 8, :]
                gain = vecT[:, 72 + 8 * n:80 + 8 * n].unsqueeze(2).to_broadcast([128, 8, 17])
                k.op("dve", lambda: nc.vector.scalar_tensor_tensor(
                    out=Gm[:, n, :, :], in0=sc, scalar=1.0, in1=gain, op0=ALU.add, op1=ALU.mult), (B_mT, B_vecT), (B_G,))

            def mod_derive_GT(n):
                gg = mT[:, (3 * n + 2) * 8:(3 * n + 2) * 8 + 8, :]
                k.op("dve", lambda: nc.vector.tensor_scalar(
                    out=GTm[:, n, :, :], in0=gg, scalar1=(1.0 if n == 1 else 0.5), scalar2=None, op0=ALU.mult), (B_mT,), (B_GT,))
            mod_derive_G(0)
            mod_derive_GT(0)
            k.barrier()

        def adaln_blocks(blks, after):
            def compute(blk, slot, Bs):
                wv = wview(slot, 0, 8, 512)
                bk, Bb = bank()
                for jj in range(4):
                    o = bk[:, jj * 17:jj * 17 + 17]
                    k.mm(o, [(wv[:, kk, jj * 128:(jj + 1) * 128], cT[:, kk, :]) for kk in range(8)], (Bs, B_cT), (Bb,))
                k.op("dve", lambda: nc.vector.tensor_tensor(
                    out=mT[:, blk * 4:blk * 4 + 4, :], in0=bk[:, 0:68].rearrange("p (j s) -> p j s", s=17),
                    in1=vecT[:, blk * 4:blk * 4 + 4].unsqueeze(2).to_broadcast([128, 4, 17]), op=ALU.add), (Bb, B_vecT), (B_mT,))
                if blk in after:
                    after[blk]()
            pending = None
            for blk in blks:
                if pending is not None:
                    compute(*pending)
                slot, Bs = load_block([(0, rows(w_ada, blk * 512, (blk + 1) * 512))])
                pending = (blk, slot, Bs)
                yield
            compute(*pending)

        agen = adaln_blocks(range(6, 10), {9: lambda: mod_derive_G(1)})
        agen2 = adaln_blocks(range(10, 18), {11: lambda: mod_derive_GT(1), 17: lambda: (mod_derive_G(2), mod_derive_GT(2))})

        B_mod = (B_G, B_GT, B_mT, B_vecT)

        def SHv(n, kk):
            return mT[:, 3 * n * 8 + kk, :]

        def bc8(ap16):
            return ap16.unsqueeze(2).to_broadcast([128, 16, 8])

        def v8(ap):
            return ap.rearrange("p (j i) -> p j i", i=8)

        for gi, G in enumerate(GROUPS):
            chunks = G["chunks"]
            npt = len(chunks)
            has_s = G["sample"]
            tiles = [("p", c) for c in chunks] + ([("s", 16)] if has_s else [])
            ntile = len(tiles)
            T = ntile * 128
            PT = npt * 128
            if npt == 5:
                ntl = [(0, 384, "p"), (384, 256, "p")]
            else:
                ntl = [(0, 384, "p"), (384, 384, "p")]
            if has_s:
                ntl.append((PT, 128, "s"))
            B_x = {(kk, i): Buf() for kk in range(8) for i in range(len(ntl))}
            B_h = {(kk, i): Buf() for kk in range(8) for i in range(len(ntl))}

            def nt_of_tile(ti):
                c0 = ti * 128
                for i, (a, w, kd) in enumerate(ntl):
                    if a <= c0 < a + w:
                        return i
                raise AssertionError

            def bx_all(i):
                return tuple(B_x[(kk, i)] for kk in range(8))

            def bh_all(i):
                return tuple(B_h[(kk, i)] for kk in range(8))

            k.mark("g%d xload" % gi)
            with ExitStack() as ps:
                xin = [sb(ps, "xin%d" % i, [128, D], F32) for i in range(2)]
                B_xin = [Buf(), Buf()]
                for ti, (kd, c) in enumerate(tiles):
                    s = ti % 2
                    src = x_p[c * 128:(c + 1) * 128, :] if kd == "p" else x_s[:, :]
                    k.dma("sp", [(xin[s][:], src)], W=B_xin[s])
                    nti = nt_of_tile(ti)
                    for hf in range(2):
                        bk, Bb = bank()
                        k.pe([(lambda q=q, bk=bk, s=s, hf=hf: nc.tensor.transpose(
                            out=bk[:, q * 128:(q + 1) * 128], in_=xin[s][:, (hf * 4 + q) * 128:(hf * 4 + q + 1) * 128],
                            identity=cst[:, IDN0:IDN0 + 128])) for q in range(4)], (B_xin[s], B_cst), (Bb,))
                        wr = tuple(B_x[(hf * 4 + q, nti)] for q in range(4))
                        eng = "act" if hf == 0 else "dve"
                        if eng == "act":
                            k.op("act", lambda bk=bk, hf=hf, ti=ti: nc.scalar.activation(
                                out=xT[:, hf * 4:hf * 4 + 4, ti * 128:(ti + 1) * 128],
                                in_=bk[:, :].rearrange("p (q t) -> p q t", q=4), func=AF.Copy), (Bb,), wr)
                        else:
                            k.op("dve", lambda bk=bk, hf=hf, ti=ti: nc.vector.tensor_copy(
                                out=xT[:, hf * 4:hf * 4 + 4, ti * 128:(ti + 1) * 128],
                                in_=bk[:, :].rearrange("p (q t) -> p q t", q=4)), (Bb,), wr)
                k.barrier()

            gs = ExitStack()
            rstd2 = sb(gs, "rstd2", [128, 2, 384], F32); B_rstd2 = [Buf(), Buf()]
            ntmp = sb(gs, "ntmp", [128, 2, 384], F32); B_ntmp = [Buf(), Buf()]

            def norm(n):
                for i, (c0, w, kd) in enumerate(ntl):
                    par = i % 2
                    for kk in range(8):
                        k.op("act", lambda kk=kk, c0=c0, w=w: nc.scalar.activation(
                            out=hT[:, kk, c0:c0 + w], in_=xT[:, kk, c0:c0 + w], func=AF.Square), (B_x[(kk, i)],), (B_h[(kk, i)],))
                    bk, Bb = bank()
                    k.mm(bk[:, 0:w], [(onesM[:], hT[:, kk, c0:c0 + w]) for kk in range(8)], bh_all(i) + (B_ones,), (Bb,))
                    k.op("act", lambda bk=bk, w=w: nc.scalar.activation(out=rstd2[:, par, 0:w], in_=bk[:, 0:w], func=AF.Ln, bias=EPS),
                         (Bb,), (B_rstd2[par],))
                    k.op("act", lambda w=w: nc.scalar.activation(out=rstd2[:, par, 0:w], in_=rstd2[:, par, 0:w], func=AF.Exp, scale=-0.5),
                         (B_rstd2[par],), (B_rstd2[par],))
                    for kk in range(8):
                        s = kk % 2
                        k.op("dve", lambda kk=kk, s=s, c0=c0, w=w: nc.vector.tensor_tensor(
                            out=ntmp[:, s, 0:w], in0=xT[:, kk, c0:c0 + w], in1=rstd2[:, par, 0:w], op=ALU.mult),
                            (B_x[(kk, i)], B_rstd2[par]), (B_ntmp[s],))
                        if kd == "p":
                            k.op("dve", lambda kk=kk, s=s, c0=c0, w=w: nc.vector.tensor_scalar(
                                out=hT[:, kk, c0:c0 + w], in0=ntmp[:, s, 0:w], scalar1=Gm[:, n, kk, 0:1], scalar2=SHv(n, kk)[:, 0:1],
                                op0=ALU.mult, op1=ALU.add), (B_ntmp[s],) + B_mod, (B_h[(kk, i)],))
                        else:
                            k.op("dve", lambda kk=kk, s=s: nc.vector.tensor_tensor(
                                out=v8(ntmp[:, s, 0:128]), in0=v8(ntmp[:, s, 0:128]), in1=bc8(Gm[:, n, kk, 1:17]), op=ALU.mult),
                                (B_ntmp[s],) + B_mod, (B_ntmp[s],))
                            k.op("dve", lambda kk=kk, s=s, c0=c0: nc.vector.tensor_tensor(
                                out=v8(hT[:, kk, c0:c0 + 128]), in0=v8(ntmp[:, s, 0:128]), in1=bc8(SHv(n, kk)[:, 1:17]), op=ALU.add),
                                (B_ntmp[s],) + B_mod, (B_h[(kk, i)],))

            def resid(n, dc, i, bk, Bb, ps_tmp, B_ps_tmp):
                c0, w, kd = ntl[i]
                if kd == "p":
                    k.op("dve", lambda: nc.vector.scalar_tensor_tensor(
                        out=xT[:, dc, c0:c0 + w], in0=bk[:, 0:w], scalar=GTm[:, n, dc, 0:1], in1=xT[:, dc, c0:c0 + w],
                        op0=ALU.mult, op1=ALU.add), (Bb,) + B_mod, (B_x[(dc, i)],))
                else:
                    k.op("dve", lambda: nc.vector.tensor_tensor(
                        out=v8(ps_tmp[:, 0:128]), in0=v8(bk[:, 0:128]), in1=bc8(GTm[:, n, dc, 1:17]), op=ALU.mult),
                        (Bb,) + B_mod, (B_ps_tmp,))
                    k.op("dve", lambda: nc.vector.tensor_tensor(
                        out=xT[:, dc, c0:c0 + 128], in0=xT[:, dc, c0:c0 + 128], in1=ps_tmp[:, 0:128], op=ALU.add),
                        (B_ps_tmp,), (B_x[(dc, i)],))

            def sstate_gen(p2, heads):
                NSB = 8
                Sin = sb(p2, "Sin", [128, NSB, 2, 256], F32); B_Sin = [Buf() for _ in range(NSB)]
                rvblk = sb(p2, "rvblk", [128, 2, 2, 256], BF16); B_rvblk = [Buf(), Buf()]
                rounds = [(h, r) for h in heads for r in range(8)]

                def ld(n_):
                    h_, r_ = rounds[n_]
                    k.dma("sp", [(Sin[:, n_ % NSB, :, :], st_in[2 * r_:2 * r_ + 2, h_, :, :].rearrange("j d e -> d j e"))], W=B_Sin[n_ % NSB])
                for n0 in range(6):
                    ld(n0)

                def mk_rvblk(n_):
                    h_, r_ = rounds[n_]
                    for jj in range(2):
                        j = 2 * r_ + jj
                        k.op("dve", lambda jj=jj, j=j: nc.vector.tensor_scalar(
                            out=rvblk[:, n_ % 2, jj, :], in0=ksrv[:, h_, 128:384], scalar1=cst[:, SEL0 + j:SEL0 + j + 1], scalar2=None, op0=ALU.mult),
                            (B_ksrv[h_], B_cst), (B_rvblk[n_ % 2],))
                mk_rvblk(0)
                for n_, (h, r) in enumerate(rounds):
                    rb, pb_ = n_ % NSB, n_ % 2
                    g8 = gam[h] ** 8
                    if n_ + 1 < len(rounds):
                        mk_rvblk(n_ + 1)
                    yield
                    bkv, Bkv = banks[5 + pb_], B_bank[5 + pb_]
                    k.mm(bkv[:, :], [(ksrv[:, h, 0:128], rvblk[:, pb_, :, :].rearrange("p a e -> p (a e)"))],
                         (B_ksrv[h], B_rvblk[pb_]), (Bkv,))
                    k.op("dve", lambda: nc.vector.scalar_tensor_tensor(
                        out=Sin[:, rb, :, :].rearrange("p a e -> p (a e)"), in0=Sin[:, rb, :, :].rearrange("p a e -> p (a e)"),
                        scalar=g8, in1=bkv[:, :], op0=ALU.mult, op1=ALU.add), (Bkv, B_Sin[rb]), (B_Sin[rb],))
                    if n_ + 6 < len(rounds):
                        ld(n_ + 6)
                    k.dma("sp", [(ss_out[2 * r:2 * r + 2, h, :, :].rearrange("j d e -> d j e"), Sin[:, rb, :, :])], R=B_Sin[rb])
                    yield


            def ffn(n, wg, wu, wd, use_hook=None):
                k.mark("g%d norm%d" % (gi, n))
                norm(n)
                k.mark("g%d ffn%d" % (gi, n))
                with ExitStack() as ps:
                    aT = sb(ps, "aT", [128, NJ, TMAX], BF16)
                    B_a = {(j, i): Buf() for j in range(NJ) for i in range(len(ntl))}
                    sg = sb(ps, "sg", [128, 2, 384], F32); B_sg = [Buf(), Buf()]
                    rtmp = sb(ps, "rtmp", [128, 128], F32); B_rtmp = Buf()
                    si = 0
                    hook = iter(())
                    hk_skip = 0
                    if use_hook:
                        hook = sstate_gen(ps, use_hook[0])
                        hk_skip = use_hook[1]
                        rot_n[0] = 5
                        rot_i[0] = 0
                    hk_i = 0
                    for jb in range(NJ // 2):
                        slot, Bs = load_block([(0, rows(wg, jb * 256, (jb + 1) * 256)), (2048, rows(wu, jb * 256, (jb + 1) * 256))])
                        gv_ = wview(slot, 0, 8, 256)
                        uv_ = wview(slot, 2048, 8, 256)
                        for jj in range(2):
                            j = jb * 2 + jj
                            for i, (c0, w, kd) in enumerate(ntl):
                                hk_i += 1
                                hk_on = (hk_skip == 0) or (hk_i % hk_skip != 0)
                                if hk_on:
                                    next(hook, None)
                                bg, Bg = bank()
                                bu, Bu = bank()
                                k.mm(bg[:, 0:w], [(gv_[:, kk, jj * 128:(jj + 1) * 128], hT[:, kk, c0:c0 + w]) for kk in range(8)],
                                     (Bs,) + bh_all(i), (Bg,))
                                k.mm(bu[:, 0:w], [(uv_[:, kk, jj * 128:(jj + 1) * 128], hT[:, kk, c0:c0 + w]) for kk in range(8)],
                                     (Bs,) + bh_all(i), (Bu,))
                                s = si % 2
                                si += 1
                                k.op("act", lambda bg=bg, w=w, s=s: nc.scalar.activation(out=sg[:, s, 0:w], in_=bg[:, 0:w], func=AF.Silu),
                                     (Bg,), (B_sg[s],))
                                k.op("dve", lambda bu=bu, w=w, s=s, j=j, c0=c0: nc.vector.tensor_tensor(
                                    out=aT[:, j, c0:c0 + w], in0=bu[:, 0:w], in1=sg[:, s, 0:w], op=ALU.mult),
                                    (Bu, B_sg[s]), (B_a[(j, i)],))
                                if hk_on:
                                    next(hook, None)
                        next(agen, None)
                    for _ in agen:
                        pass
                    for _ in hook:
                        pass
                    rot_n[0] = 6
                    for dc in range(8):
                        slot, Bs = load_block([(0, wd[:, dc * 128:(dc + 1) * 128].rearrange("(j p) c -> p j c", p=128))])
                        dv = wview(slot, 0, NJ, 128)
                        for i, (c0, w, kd) in enumerate(ntl):
                            bk, Bb = bank()
                            k.mm(bk[:, 0:w], [(dv[:, j, :], aT[:, j, c0:c0 + w]) for j in range(NJ)],
                                 (Bs,) + tuple(B_a[(j, i)] for j in range(NJ)), (Bb,))
                            resid(n, dc, i, bk, Bb, rtmp, B_rtmp)
                    k.barrier()

            ffn(0, w1g, w1u, w1d, use_hook=((range(4, 8), 4) if gi == 1 else None))

            k.mark("g%d norm1" % gi)
            norm(1)
            k.mark("g%d vsec" % gi)
            with ExitStack() as ps:
                oaT = sb(ps, "oaT", [128, 8, TMAX], BF16)
                B_oa = {(g, i): Buf() for g in range(8) for i in range(len(ntl))}
                obT = sb(ps, "obT", [128, 16, TMAX], BF16)
                B_ob = {(j, i): Buf() for j in range(16) for i in range(len(ntl))}
                pv = ExitStack()
                vln = sb(pv, "vln", [128, 6, D], BF16); B_vln = [Buf() for _ in range(6)]
                lnG = sb(pv, "lnG", [128, D], F32); B_lnG = Buf()
                lnB = sb(pv, "lnB", [128, D], F32); B_lnB = Buf()
                biasP = sb(pv, "biasP", [128, 8, 128], F32); B_biasP = Buf()
                k.dma("sp", [(lnG[:], lngb_d[0, :].partition_broadcast(128))], W=B_lnG)
                k.dma("sp", [(lnB[:], lngb_d[1, :].partition_broadcast(128))], W=B_lnB)
                k.dma("sp", [(biasP[:].rearrange("p g t -> p (g t)"), gmbs_d.partition_broadcast(128))], W=B_biasP)
                pvu = ExitStack()
                p2 = pvu
                if True:
                    NG = 4
                    gv4 = sb(p2, "gv", [128, NG, D], F32); B_gv4 = [Buf() for _ in range(NG)]
                    st64 = sb(p2, "st6", [128, NG, 2, 6], F32); B_st64 = [Buf() for _ in range(NG)]
                    mv4 = sb(p2, "mv", [128, NG, 2], F32); B_mv4 = [Buf() for _ in range(NG)]
                    rsv4 = sb(p2, "rsv", [128, NG], F32); B_rsv4 = [Buf() for _ in range(NG)]
                    vf = sb(p2, "vf", [128, D], F32); B_vf = Buf()
                    slot0, Bs0 = load_block([(0, rows(w_in, 1024, 1536))])
                    slot1, Bs1 = load_block([(0, rows(w_in, 1536, 2048))])
                    wv0 = wview(slot0, 0, 8, 512)
                    wv1 = wview(slot1, 0, 8, 512)

                    def v_front(ti):
                        kd, c = tiles[ti]
                        i = nt_of_tile(ti)
                        tc0 = ti * 128
                        sl = ti % NG
                        gv = gv4[:, sl, :]; B_gv = B_gv4[sl]
                        st6 = st64[:, sl, :, :]; B_st6 = B_st64[sl]
                        b0, Bb0 = bank()
                        b1, Bb1 = bank()
                        k.mm(b0[:, :], [(hT[:, kk, tc0:tc0 + 128], wv0[:, kk, :]) for kk in range(8)], (Bs0,) + bh_all(i), (Bb0,))
                        k.mm(b1[:, :], [(hT[:, kk, tc0:tc0 + 128], wv1[:, kk, :]) for kk in range(8)], (Bs1,) + bh_all(i), (Bb1,))
                        k.op("act", lambda: nc.scalar.activation(out=gv[:, 0:512], in_=b0[:, :], func=AF.Gelu_apprx_tanh), (Bb0,), (B_gv,))
                        k.op("act", lambda: nc.scalar.activation(out=gv[:, 512:1024], in_=b1[:, :], func=AF.Gelu_apprx_tanh), (Bb1,), (B_gv,))
                        k.op("dve", lambda: nc.vector.bn_stats(out=st6[:, 0, :], in_=gv[:, 0:512]), (B_gv,), (B_st6,))
                        k.op("dve", lambda: nc.vector.bn_stats(out=st6[:, 1, :], in_=gv[:, 512:1024]), (B_gv,), (B_st6,))
                        k.op("dve", lambda: nc.vector.bn_aggr(out=mv4[:, sl, :], in_=st6.rearrange("p a s -> p (a s)")), (B_st6,), (B_mv4[sl],))

                    def v_back(ti):
                        kd, c = tiles[ti]
                        sl = ti % NG
                        gv = gv4[:, sl, :]; B_gv = B_gv4[sl]
                        k.op("dve", lambda: nc.vector.tensor_scalar(out=gv, in0=gv, scalar1=mv4[:, sl, 0:1], scalar2=rsv4[:, sl:sl + 1],
                                                                    op0=ALU.subtract, op1=ALU.mult), (B_gv, B_mv4[sl], B_rsv4[sl]), (B_gv,))
                        k.op("dve", lambda: nc.vector.tensor_tensor(out=gv, in0=gv, in1=lnG[:], op=ALU.mult), (B_gv, B_lnG), (B_gv,))
                        if kd == "p":
                            k.op("dve", lambda: nc.vector.tensor_tensor(out=vln[:, ti, :], in0=gv, in1=lnB[:], op=ALU.add),
                                 (B_gv, B_lnB), (B_vln[ti],))
                        else:
                            k.op("dve", lambda: nc.vector.tensor_tensor(out=vf[:], in0=gv, in1=lnB[:], op=ALU.add),
                                 (B_gv, B_lnB), (B_vf,))
                            k.op("act", lambda: nc.scalar.activation(out=vln[:, ti, :], in_=vf[:], func=AF.Copy), (B_vf,), (B_vln[ti],))
                            k.dma("sp", [(vs_out[:, :], vf[:])], R=B_vf)

                    pairs_ = [list(range(t0_, min(t0_ + 2, ntile))) for t0_ in range(0, ntile, 2)]
                    for pi_, pr in enumerate(pairs_):
                        for ti in pr:
                            v_front(ti)
                        s0 = pr[0] % NG
                        n_ = len(pr)
                        k.op("act", lambda: nc.scalar.activation(out=rsv4[:, s0:s0 + n_], in_=mv4[:, s0:s0 + n_, 1], func=AF.Ln, bias=EPS),
                             tuple(B_mv4[ti % NG] for ti in pr), tuple(B_rsv4[ti % NG] for ti in pr))
                        k.op("act", lambda: nc.scalar.activation(out=rsv4[:, s0:s0 + n_], in_=rsv4[:, s0:s0 + n_], func=AF.Exp, scale=-0.5),
                             tuple(B_rsv4[ti % NG] for ti in pr), tuple(B_rsv4[ti % NG] for ti in pr))
                        for ti in pr:
                            v_back(ti)
                k.mark("g%d usec" % gi)
                if True:
                    ug = sb(p2, "ug", [128, 2, 384], F32); B_ug = [Buf(), Buf()]
                    mb = sb(p2, "mb", [128, 2, 384], F32); B_mb = [Buf(), Buf()]
                    slot0, Bs0 = load_block([(0, rows(w_in, 0, 512))])
                    slot1, Bs1 = load_block([(0, rows(w_in, 512, 1024))])
                    si = 0
                    for g in range(8):
                        slot, Bs = (slot0, Bs0) if g < 4 else (slot1, Bs1)
                        wu_ = wview(slot, 0, 8, 512)
                        gc = (g % 4) * 128
                        for i, (c0, w, kd) in enumerate(ntl):
                            s = si % 2
                            si += 1
                            bu, Bu = bank()
                            k.mm(bu[:, 0:w], [(wu_[:, kk, gc:gc + 128], hT[:, kk, c0:c0 + w]) for kk in range(8)], (Bs,) + bh_all(i), (Bu,))
                            k.op("act", lambda bu=bu, w=w, s=s: nc.scalar.activation(out=ug[:, s, 0:w], in_=bu[:, 0:w], func=AF.Gelu_apprx_tanh),
                                 (Bu,), (B_ug[s],))
                            bm, Bm = bank()
                            t0 = c0 // 128
                            nt_ = w // 128
                            mixw = WT if kd == "p" else WTs
                            Bmw = B_WT if kd == "p" else B_WTs
                            k.pe([(lambda tt=tt, bm=bm, g=g, mixw=mixw: nc.tensor.matmul(
                                bm[:, tt * 128:(tt + 1) * 128], lhsT=vln[:, t0 + tt, g * 128:(g + 1) * 128], rhs=mixw[:, g, :],
                                start=True, stop=True)) for tt in range(nt_)],
                                tuple(B_vln[t0 + tt] for tt in range(nt_)) + (Bmw,), (Bm,))
                            if kd == "p":
                                k.op("dve", lambda bm=bm, w=w, s=s, g=g, nt_=nt_: nc.vector.tensor_tensor(
                                    out=mb[:, s, 0:w].rearrange("p (a t) -> p a t", t=128), in0=bm[:, 0:w].rearrange("p (a t) -> p a t", t=128),
                                    in1=biasP[:, g, :].unsqueeze(1).to_broadcast([128, nt_, 128]), op=ALU.add), (Bm, B_biasP), (B_mb[s],))
                            else:
                                k.op("dve", lambda bm=bm, s=s, g=g: nc.vector.tensor_tensor(
                                    out=v8(mb[:, s, 0:128]), in0=v8(bm[:, 0:128]),
                                    in1=biasP[:, g, 0:8].unsqueeze(1).to_broadcast([128, 16, 8]), op=ALU.add), (Bm, B_biasP), (B_mb[s],))
                            k.op("dve", lambda w=w, s=s, g=g, c0=c0: nc.vector.tensor_tensor(
                                out=oaT[:, g, c0:c0 + w], in0=ug[:, s, 0:w], in1=mb[:, s, 0:w], op=ALU.mult),
                                (B_ug[s], B_mb[s]), (B_oa[(g, i)],))
                    k.barrier()
                pvu.close()
                pv.close()
                k.mark("g%d ret" % gi)
                with ExitStack() as p2:
                    D6 = 6
                    srg2 = sb(p2, "srg", [128, 2, 2, TMAX], F32); B_srg2 = [[Buf() for _ in range(len(ntl))] for _ in range(2)]
                    orT2 = sb(p2, "orT", [128, 2, 2, TMAX], F32); B_or2 = [[Buf() for _ in range(ntile)] for _ in range(2)]
                    ssb2 = sb(p2, "ssb", [128, 2, TMAX], F32); B_ssb2 = [[Buf() for _ in range(ntile)] for _ in range(2)]
                    qkf = sb(p2, "qkf", [128, 2, 256], F32); B_qkf = [Buf(), Buf()]
                    rt = sb(p2, "rt", [128, 2, 4, 128], F32); B_rt = [[Buf() for _ in range(4)] for _ in range(2)]
                    qkr = sb(p2, "qkr", [128, D6, 256], BF16); B_qkr = [Buf() for _ in range(D6)]
                    qkT = sb(p2, "qkT", [128, D6, 256], BF16); B_qkT = [Buf() for _ in range(D6)]
                    rvb = sb(p2, "rvb", [128, D6, 256], BF16); B_rvb = [Buf() for _ in range(D6)]
                    scT = sb(p2, "scT", [128, 2, 128], BF16); B_scT = [Buf(), Buf()]
                    sqh = sb(p2, "sqh", [128, 2, 2, 128], BF16); B_sqh = [Buf(), Buf()]
                    B_pbq = [Buf() for _ in range(4)]
                    if has_s:
                        S16 = sb(p2, "S16", [128, 16, 256], BF16); B_S16 = Buf()
                    items = [(h, ti) for h in range(8) for ti in range(ntile)]
                    NI = len(items)
                    hctx = {}
                    hctxA = {}

                    def S1(ii):
                        h, ti = items[ii]
                        kd, c = tiles[ti]
                        def loadA(hh):
                            hctxA[hh] = load_block([(0, rows(w_in, 2048 + hh * 128, 2048 + (hh + 1) * 128)),
                                                    (1, rows(w_in, 3072 + hh * 128, 3072 + (hh + 1) * 128)),
                                                    (2, rows(w_in, 4096 + hh * 256, 4096 + (hh + 1) * 256))], qkv=True)
                        if ti == 0:
                            if h == 0:
                                loadA(0)
                            slotB, BsB = load_block([(0, rows(w_in, 6144 + h * 256, 6144 + (h + 1) * 256))])
                            hctx[h] = hctxA[h] + (slotB, BsB)
                            next(agen2, None)
                            if has_s and h == 0:
                                k.pool_after_barrier()
                                k.dma("pool", [(S16[:, :, :], st_in[:, 0, :, :].rearrange("j d e -> d j e"))], W=B_S16)
                        if ti == 2 and h < 7:
                            loadA(h + 1)
                        slotA, BsA, slotB, BsB = hctx[h]
                        wA = wview(slotA, 0, 8, 512)
                        i = nt_of_tile(ti)
                        tc0 = ti * 128
                        kidx = 0 if kd == "p" else 1
                        p2_, p6 = ii % 2, ii % D6
                        ba, Bba = bank()
                        k.mm(ba[:, 0:512], [(hT[:, kk, tc0:tc0 + 128], wA[:, kk, :]) for kk in range(8)], (BsA,) + bh_all(i), (Bba,))
                        dq = cst[:, DEC0 + kidx * 16 + h:DEC0 + kidx * 16 + h + 1]
                        dk = cst[:, DEC0 + kidx * 16 + 8 + h:DEC0 + kidx * 16 + 8 + h + 1]
                        k.op("act", lambda: nc.scalar.activation(out=qkf[:, p2_, 0:128], in_=ba[:, 0:128], func=AF.Copy, scale=dq),
                             (Bba, B_cst), (B_qkf[p2_],))
                        k.op("act", lambda: nc.scalar.activation(out=qkf[:, p2_, 128:256], in_=ba[:, 128:256], func=AF.Copy, scale=dk),
                             (Bba, B_cst), (B_qkf[p2_],))
                        k.op("act", lambda: nc.scalar.activation(out=rvb[:, p6, :], in_=ba[:, 256:512], func=AF.Copy), (Bba,), (B_rvb[p6],))

                    def S1b(ii):
                        h, ti = items[ii]
                        kd, c = tiles[ti]
                        p2_, p6 = ii % 2, ii % D6
                        tt_ = 16 if kd == "s" else c
                        cosb = cst[:, ROT0 + tt_ * 128:ROT0 + tt_ * 128 + 64].unsqueeze(1).to_broadcast([128, 2, 64])
                        sinb = cst[:, ROT0 + tt_ * 128 + 64:ROT0 + tt_ * 128 + 128].unsqueeze(1).to_broadcast([128, 2, 64])
                        qv = qkf[:, p2_, :].rearrange("p (a b i) -> p a b i", a=2, b=2)
                        x1 = qv[:, :, 0, :]
                        x2 = qv[:, :, 1, :]
                        ov = qkr[:, p6, :].rearrange("p (a b i) -> p a b i", a=2, b=2)
                        Brt = B_rt[p2_]

                        def r4(j):
                            return rt[:, p2_, j, :].rearrange("p (a i) -> p a i", a=2)
                        k.op("dve", lambda: nc.vector.tensor_tensor(out=r4(0), in0=x1, in1=cosb, op=ALU.mult), (B_qkf[p2_], B_cst), (Brt[0],))
                        k.op("dve", lambda: nc.vector.tensor_tensor(out=r4(1), in0=x2, in1=sinb, op=ALU.mult), (B_qkf[p2_], B_cst), (Brt[1],))
                        k.op("dve", lambda: nc.vector.tensor_tensor(out=r4(2), in0=x1, in1=sinb, op=ALU.mult), (B_qkf[p2_], B_cst), (Brt[2],))
                        k.op("dve", lambda: nc.vector.tensor_tensor(out=r4(3), in0=x2, in1=cosb, op=ALU.mult), (B_qkf[p2_], B_cst), (Brt[3],))
                        k.op("dve", lambda: nc.vector.tensor_tensor(out=ov[:, :, 0, :], in0=r4(0), in1=r4(1), op=ALU.subtract),
                             (Brt[0], Brt[1]), (B_qkr[p6],))
                        k.op("dve", lambda: nc.vector.tensor_tensor(out=ov[:, :, 1, :], in0=r4(2), in1=r4(3), op=ALU.add),
                             (Brt[2], Brt[3]), (B_qkr[p6],))

                    def S2(ii):
                        p6 = ii % D6
                        pq = ii % 4
                        k.pe([(lambda a=a: nc.tensor.transpose(out=pbf[:, pq * 256 + a * 128:pq * 256 + (a + 1) * 128],
                                                               in_=qkr[:, p6, a * 128:(a + 1) * 128], identity=identB[:])) for a in range(2)],
                             (B_qkr[p6], B_idb), (B_pbf,))
                        k.op("dve", lambda: nc.vector.tensor_copy(out=qkT[:, p6, :], in_=pbf[:, pq * 256:(pq + 1) * 256]), (B_pbf,), (B_qkT[p6],))

                    def S3(ii):
                        h, ti = items[ii]
                        kd, c = tiles[ti]
                        p2_, p6 = ii % 2, ii % D6
                        bs_, Bbs = bank()
                        k.mm(bs_[:, 0:128], [(qkT[:, p6, 128:256], qkT[:, p6, 0:128])], (B_qkT[p6],), (Bbs,))
                        mk = cst[:, TUP0:TUP0 + 128] if kd == "p" else cst[:, MKS0:MKS0 + 128]
                        k.op("dve", lambda: nc.vector.tensor_tensor(out=scT[:, p2_, :], in0=bs_[:, 0:128], in1=mk, op=ALU.mult),
                             (Bbs, B_cst), (B_scT[p2_],))

                    def S4(ii):
                        h, ti = items[ii]
                        kd, c = tiles[ti]
                        p2_, p6 = ii % 2, ii % D6
                        tc0 = ti * 128
                        gL = gam[h] ** 128
                        first = (kd == "p" and c == 0)
                        orT = orT2[:, h % 2, :, :]
                        B_or = B_or2[h % 2]
                        qdT = qkT[:, p6, 0:128]
                        bo, Bbo = bank()
                        fns = []
                        use_inter = (kd == "p" and not first)
                        for hf in range(2):
                            fns.append(lambda hf=hf: nc.tensor.matmul(
                                bo[:, hf * 128:(hf + 1) * 128], lhsT=rvb[:, p6, hf * 128:(hf + 1) * 128], rhs=scT[:, p2_, :], start=True, stop=(not use_inter)))
                            if use_inter:
                                fns.append(lambda hf=hf: nc.tensor.matmul(
                                    bo[:, hf * 128:(hf + 1) * 128], lhsT=Sbf[:, h, hf * 128:(hf + 1) * 128], rhs=qdT, start=False, stop=True))
                        k.pe(fns, (B_rvb[p6], B_scT[p2_], B_qkT[p6], B_Sbf[h]), (Bbo,))
                        k.op("act", lambda: nc.scalar.activation(
                            out=orT[:, :, tc0:tc0 + 128], in_=bo[:, 0:256].rearrange("p (a t) -> p a t", a=2), func=AF.Copy), (Bbo,), (B_or[ti],))
                        if kd == "p":
                            bkv, Bkv = bank()
                            k.mm(bkv[:, 0:256], [(qkr[:, p6, 128:256], rvb[:, p6, :])], (B_qkr[p6], B_rvb[p6]), (Bkv,))
                            if first:
                                k.op("dve", lambda: nc.vector.tensor_copy(out=U[:, h, :], in_=bkv[:, 0:256]), (Bkv,), (B_U[h],))
                            else:
                                k.op("dve", lambda: nc.vector.scalar_tensor_tensor(
                                    out=U[:, h, :], in0=U[:, h, :], scalar=gL, in1=bkv[:, 0:256], op0=ALU.mult, op1=ALU.add),
                                    (Bkv, B_U[h]), (B_U[h],))
                            k.op("act", lambda: nc.scalar.activation(out=sqh[:, p2_, :, :], in_=bo[:, 0:256].rearrange("p (a t) -> p a t", a=2),
                                                                     func=AF.Square), (Bbo,), (B_sqh[p2_],))
                            k.op("act", lambda: nc.scalar.activation(out=Sbf[:, h, :], in_=U[:, h, :], func=AF.Copy, scale=gL),
                                 (B_U[h],), (B_Sbf[h],))
                        else:
                            bi_, Bbi = banks[6], B_bank[6]
                            fns = []
                            for j in range(16):
                                for hf in range(2):
                                    fns.append(lambda j=j, hf=hf: nc.tensor.matmul(
                                        bi_[:, hf * 128 + 8 * j:hf * 128 + 8 * j + 8], lhsT=S16[:, j, hf * 128:(hf + 1) * 128],
                                        rhs=qkT[:, p6, 8 * j:8 * j + 8], start=True, stop=True))
                            k.pe(fns, (B_S16, B_qkT[p6]), (Bbi,))
                            if h < 7:
                                k.dma("pool", [(S16[:, :, :], st_in[:, h + 1, :, :].rearrange("j d e -> d j e"))], W=B_S16)
                            k.op("dve", lambda: nc.vector.tensor_scalar(out=ksrv[:, h, 0:128], in0=qkr[:, p6, 128:256], scalar1=gam[h] ** 8,
                                                                        scalar2=None, op0=ALU.mult), (B_qkr[p6],), (B_ksrv[h],))
                            k.op("dve", lambda: nc.vector.tensor_copy(out=ksrv[:, h, 128:384], in_=rvb[:, p6, :]), (B_rvb[p6],), (B_ksrv[h],))
                            k.op("dve", lambda: nc.vector.tensor_tensor(
                                out=orT[:, :, tc0:tc0 + 128], in0=bi_[:, 0:256].rearrange("p (a t) -> p a t", a=2),
                                in1=orT[:, :, tc0:tc0 + 128], op=ALU.add), (Bbi, B_or[ti]), (B_or[ti],))
                            k.op("act", lambda: nc.scalar.activation(out=sqh[:, p2_, :, :], in_=orT[:, :, tc0:tc0 + 128], func=AF.Square),
                                 (B_or[ti],), (B_sqh[p2_],))

                    def S5(ii):
                        h, ti = items[ii]
                        p2_ = ii % 2
                        tc0 = ti * 128
                        bss, Bss = bank()
                        k.mm(bss[:, 0:128], [(ones256[:], sqh[:, p2_, 0, :]), (ones256[:], sqh[:, p2_, 1, :])], (B_sqh[p2_], B_ones2), (Bss,))
                        k.op("dve", lambda: nc.vector.tensor_copy(out=ssb2[:, h % 2, tc0:tc0 + 128], in_=bss[:, 0:128]), (Bss,), (B_ssb2[h % 2][ti],))

                    def RG(ii):
                        h, ti = items[ii]
                        if ti != ntile - 1:
                            return
                        slotA, BsA, slotB, BsB = hctx[h]
                        wg_ = wview(slotB, 0, 8, 256)
                        for i, (c0, w, kd) in enumerate(ntl):
                            for hf in range(2):
                                bk, Bb = bank()
                                k.mm(bk[:, 0:w], [(wg_[:, kk, hf * 128:(hf + 1) * 128], hT[:, kk, c0:c0 + w]) for kk in range(8)],
                                     (BsB,) + bh_all(i), (Bb,))
                                k.op("act", lambda bk=bk, w=w, hf=hf, c0=c0: nc.scalar.activation(
                                    out=srg2[:, h % 2, hf, c0:c0 + w], in_=bk[:, 0:w], func=AF.Silu), (Bb,), (B_srg2[h % 2][i],))

                    def E1(ii):
                        h, ti = items[ii]
                        if ti != ntile - 1:
                            return
                        par = h % 2
                        k.op("act", lambda: nc.scalar.activation(out=ssb2[:, par, 0:T], in_=ssb2[:, par, 0:T], func=AF.Ln, bias=EPS),
                             tuple(B_ssb2[par]), tuple(B_ssb2[par]))
                        k.op("act", lambda: nc.scalar.activation(out=ssb2[:, par, 0:T], in_=ssb2[:, par, 0:T], func=AF.Exp, scale=-0.5),
                             tuple(B_ssb2[par]), tuple(B_ssb2[par]))

                    def E2(ii):
                        h, ti = items[ii]
                        if ti != ntile - 1:
                            return
                        par = h % 2
                        for i, (c0, w, kd) in enumerate(ntl):
                            tl = [t_ for t_ in range(ntile) if nt_of_tile(t_) == i]
                            rd = tuple(B_or2[par][t_] for t_ in tl)
                            k.op("dve", lambda c0=c0, w=w: nc.vector.tensor_tensor(
                                out=orT2[:, par, :, c0:c0 + w], in0=orT2[:, par, :, c0:c0 + w],
                                in1=ssb2[:, par, c0:c0 + w].unsqueeze(1).to_broadcast([128, 2, w]),
                                op=ALU.mult), rd + tuple(B_ssb2[par][t_] for t_ in tl), rd)
                            k.op("dve", lambda c0=c0, w=w: nc.vector.tensor_tensor(
                                out=obT[:, 2 * h:2 * h + 2, c0:c0 + w], in0=orT2[:, par, :, c0:c0 + w], in1=srg2[:, par, :, c0:c0 + w], op=ALU.mult),
                                rd + (B_srg2[par][i],), (B_ob[(2 * h, i)], B_ob[(2 * h + 1, i)]))

                    LAG = (0, 2, 3, 4, 5, 0, 0, 8, 9)
                    stages = (S1, S2, S3, S4, S5, S1b, RG, E1, E2)
                    for s in range(NI + 10):
                        for st in (0, 1, 2, 4, 3, 7, 8, 5, 6):
                            ii = s - LAG[st]
                            if 0 <= ii < NI:
                                stages[st](ii)
                    for _ in agen2:
                        pass
                    if gi == len(GROUPS) - 1:
                        Ufin = sb(p2, "Ufin", [128, 8, 256], F32); B_Ufin = Buf()
                        for h in range(8):
                            k.op("act", lambda h=h: nc.scalar.activation(out=Ufin[:, h, :], in_=U[:, h, :], func=AF.Copy, scale=gam[h] ** 128),
                                 (B_U[h],), (B_Ufin,))
                        k.dma("sp", [(sp_out.rearrange("h d e -> d h e"), Ufin[:])], R=B_Ufin)
                    k.barrier()
                if debug and has_s:
                    Bd = Buf()
                    k.dma("sp", [(dbg_oa[:, :, :], oaT[:, :, PT:PT + 128])], R=Bd)
                    k.dma("sp", [(dbg_ob[:, :, :], obT[:, :, PT:PT + 128])], R=Bd)
                k.mark("g%d merge" % gi)
                with ExitStack() as p2:
                    mgT = sb(p2, "mgT", [128, 8, TMAX], BF16)
                    B_mg = {(dc, i): Buf() for dc in range(8) for i in range(len(ntl))}
                    sga = sb(p2, "sga", [128, 384], F32); B_sga = Buf()
                    sgb = sb(p2, "sgb", [128, 384], F32); B_sgb = Buf()
                    m1 = sb(p2, "m1", [128, 384], F32); B_m1 = Buf()
                    m2 = sb(p2, "m2", [128, 384], F32); B_m2 = Buf()
                    rtmp = sb(p2, "rtmp2", [128, 128], F32); B_rtmp = Buf()

                    for dc in range(8):
                        cs = slice(dc * 128, (dc + 1) * 128)
                        slotA, BsA = load_block([(0, w_a[:, cs].rearrange("(j p) c -> p j c", p=128)),
                                                 (1024, w_b[:, cs].rearrange("(j p) c -> p j c", p=128)),
                                                 (3072, rows(w_in, 8192 + dc * 128, 8192 + (dc + 1) * 128))])
                        slotB, BsB = load_block([(0, rows(w_in, 9216 + dc * 128, 9216 + (dc + 1) * 128))])
                        wa_ = wview(slotA, 0, 8, 128)
                        wb_ = wview(slotA, 1024, 16, 128)
                        wga = wview(slotA, 3072, 8, 128)
                        wgb = wview(slotB, 0, 8, 128)
                        for i, (c0, w, kd) in enumerate(ntl):
                            bpa, Bpa = bank()
                            bpb, Bpb = bank()
                            bga, Bga = bank()
                            bgb, Bgb = bank()
                            k.mm(bga[:, 0:w], [(wga[:, kk, :], hT[:, kk, c0:c0 + w]) for kk in range(8)], (BsA,) + bh_all(i), (Bga,))
                            k.mm(bgb[:, 0:w], [(wgb[:, kk, :], hT[:, kk, c0:c0 + w]) for kk in range(8)], (BsB,) + bh_all(i), (Bgb,))
                            k.mm(bpa[:, 0:w], [(wa_[:, j, :], oaT[:, j, c0:c0 + w]) for j in range(8)],
                                 (BsA,) + tuple(B_oa[(j, i)] for j in range(8)), (Bpa,))
                            k.mm(bpb[:, 0:w], [(wb_[:, j, :], obT[:, j, c0:c0 + w]) for j in range(16)],
                                 (BsA,) + tuple(B_ob[(j, i)] for j in range(16)), (Bpb,))
                            k.op("act", lambda bga=bga, w=w: nc.scalar.activation(out=sga[:, 0:w], in_=bga[:, 0:w], func=AF.Sigmoid), (Bga,), (B_sga,))
                            k.op("act", lambda bgb=bgb, w=w: nc.scalar.activation(out=sgb[:, 0:w], in_=bgb[:, 0:w], func=AF.Sigmoid), (Bgb,), (B_sgb,))
                            k.op("dve", lambda bpa=bpa, w=w: nc.vector.tensor_tensor(out=m1[:, 0:w], in0=bpa[:, 0:w], in1=sga[:, 0:w], op=ALU.mult),
                                 (Bpa, B_sga), (B_m1,))
                            k.op("dve", lambda bpb=bpb, w=w: nc.vector.tensor_tensor(out=m2[:, 0:w], in0=bpb[:, 0:w], in1=sgb[:, 0:w], op=ALU.mult),
                                 (Bpb, B_sgb), (B_m2,))
                            k.op("dve", lambda w=w, dc=dc, c0=c0: nc.vector.tensor_tensor(out=mgT[:, dc, c0:c0 + w], in0=m1[:, 0:w], in1=m2[:, 0:w], op=ALU.add),
                                 (B_m1, B_m2), (B_mg[(dc, i)],))
                    for blk in range(2):
                        slot, Bs = load_block([(0, rows(w_o, blk * 512, (blk + 1) * 512))])
                        wo_ = wview(slot, 0, 8, 512)
                        for q in range(4):
                            dc = blk * 4 + q
                            for i, (c0, w, kd) in enumerate(ntl):
                                bk, Bb = bank()
                                k.mm(bk[:, 0:w], [(wo_[:, j, q * 128:(q + 1) * 128], mgT[:, j, c0:c0 + w]) for j in range(8)],
                                     (Bs,) + tuple(B_mg[(j, i)] for j in range(8)), (Bb,))
                                resid(1, dc, i, bk, Bb, rtmp, B_rtmp)
                    if debug and has_s:
                        Bd2 = Buf()
                        k.dma("sp", [(dbg_mg[:, :, :], mgT[:, :, PT:PT + 128])], R=Bd2)
                    k.barrier()

            ffn(2, w2g, w2u, w2d, use_hook=((range(0, 4), 2) if has_s else None))

            k.mark("g%d final" % gi)
            with ExitStack() as ps:
                sq = sb(ps, "fsq", [128, 8, 384], BF16); B_sq = [Buf() for _ in range(8)]
                sd = sb(ps, "fsd", [128, 384], F32); B_sd = Buf()
                rstd = sb(ps, "frstd", [128, TMAX], F32); B_rstd = [Buf() for _ in range(len(ntl))]
                yT = sb(ps, "yT", [128, 8, TMAX], F32); B_y = {(kk, i): Buf() for kk in range(8) for i in range(len(ntl))}
                yo = [sb(ps, "yo%d" % i, [128, D], F32) for i in range(2)]; B_yo = [Buf(), Buf()]
                for i, (c0, w, kd) in enumerate(ntl):
                    for kk in range(8):
                        k.op("act", lambda kk=kk, c0=c0, w=w: nc.scalar.activation(
                            out=sq[:, kk, 0:w], in_=xT[:, kk, c0:c0 + w], func=AF.Square), (B_x[(kk, i)],), (B_sq[kk],))
                    bk, Bb = bank()
                    k.mm(bk[:, 0:w], [(onesM[:], sq[:, kk, 0:w]) for kk in range(8)], tuple(B_sq) + (B_ones,), (Bb,))
                    k.op("act", lambda bk=bk, w=w: nc.scalar.activation(out=sd[:, 0:w], in_=bk[:, 0:w], func=AF.Ln, bias=EPS), (Bb,), (B_sd,))
                    k.op("act", lambda c0=c0, w=w: nc.scalar.activation(out=rstd[:, c0:c0 + w], in_=sd[:, 0:w], func=AF.Exp, scale=-0.5), (B_sd,), (B_rstd[i],))
                    for kk in range(8):
                        k.op("dve", lambda kk=kk, c0=c0, w=w: nc.vector.scalar_tensor_tensor(
                            out=yT[:, kk, c0:c0 + w], in0=xT[:, kk, c0:c0 + w], scalar=vecT[:, 96 + kk:97 + kk], in1=rstd[:, c0:c0 + w],
                            op0=ALU.mult, op1=ALU.mult), (B_x[(kk, i)], B_rstd[i], B_vecT), (B_y[(kk, i)],))
                for ti, (kd, c) in enumerate(tiles):
                    i = nt_of_tile(ti)
                    s = ti % 2
                    for hf in range(2):
                        bk, Bb = bank()
                        k.pe([(lambda q=q, bk=bk, hf=hf, ti=ti: nc.tensor.transpose(
                            out=bk[:, q * 128:(q + 1) * 128], in_=yT[:, hf * 4 + q, ti * 128:(ti + 1) * 128],
                            identity=cst[:, IDN0:IDN0 + 128])) for q in range(4)],
                            tuple(B_y[(hf * 4 + q, i)] for q in range(4)) + (B_cst,), (Bb,))
                        if hf == 0:
                            k.op("act", lambda bk=bk, s=s: nc.scalar.activation(out=yo[s][:, 0:512], in_=bk[:, :], func=AF.Copy), (Bb,), (B_yo[s],))
                        else:
                            k.op("dve", lambda bk=bk, s=s: nc.vector.tensor_copy(out=yo[s][:, 512:1024], in_=bk[:, :]), (Bb,), (B_yo[s],))
                    dst = y_p[c * 128:(c + 1) * 128, :] if kd == "p" else y_s[:, :]
                    k.dma("sp", [(dst, yo[s][:])], R=B_yo[s])
                k.barrier()
            gs.close()

        for t in k.stores:
            k._wait("sp", t)
        k.mark("end")
    build.marks = k.marks
    return nc


_NC_CACHE = {}


def kernel(x_prompt, x_sample, state_ret, c_prompt, c_sample, w_ada, b_ada, n1_g,
           w1_gate, w1_up, w1_down, nm_g, w_in, gm_ln_g, gm_ln_b, gm_ws, gm_bs,
           w_a, w_b, w_o, n2_g, w2_gate, w2_up, w2_down, final_g):
    f = lambda a: np.ascontiguousarray(np.asarray(a, dtype=np.float32))
    x_prompt = f(x_prompt); x_sample = f(x_sample); state_ret = f(state_ret)
    c_prompt = f(c_prompt); c_sample = f(c_sample)
    vecs = np.concatenate([f(b_ada).reshape(72, 128), f(n1_g).reshape(8, 128), f(nm_g).reshape(8, 128),
                           f(n2_g).reshape(8, 128), f(final_g).reshape(8, 128)], axis=0)
    lngb = np.stack([f(gm_ln_g).reshape(D), f(gm_ln_b).reshape(D)], axis=0)
    shared = {
        "w_ada": f(w_ada)[0], "vecs": np.ascontiguousarray(vecs),
        "w1g": f(w1_gate)[0], "w1u": f(w1_up)[0], "w1d": f(w1_down)[0],
        "w2g": f(w2_gate)[0], "w2u": f(w2_up)[0], "w2d": f(w2_down)[0],
        "w_in": f(w_in)[0], "lngb": np.ascontiguousarray(lngb),
        "gm_ws": f(gm_ws)[0], "gm_bs": f(gm_bs).reshape(D),
        "w_a": f(w_a)[0], "w_b": f(w_b)[0], "w_o": f(w_o)[0],
        "cst": _host_consts(),
    }
    in_maps = []
    for b in range(NCORES):
        m = dict(shared)
        m["x_p"] = x_prompt[b]
        m["x_s"] = np.ascontiguousarray(x_sample[16 * b:16 * b + 16].reshape(128, D))
        m["c_in"] = np.ascontiguousarray(np.concatenate([c_prompt[b:b + 1], c_sample[16 * b:16 * b + 16]], axis=0))
        m["st_in"] = np.ascontiguousarray(state_ret[0, 16 * b:16 * b + 16])
        in_maps.append(m)
    nc = build()
    res = run_bass_kernel_spmd(nc, in_maps, core_ids=list(range(NCORES)))
    rs = res.results
    y_prompt = np.stack([rs[b]["y_p"] for b in range(NCORES)], axis=0).astype(np.float32)
    y_sample = np.concatenate([rs[b]["y_s"].reshape(16, 8, D) for b in range(NCORES)], axis=0).astype(np.float32)
    sp = np.stack([rs[b]["sp_out"] for b in range(NCORES)], axis=0)[None].astype(np.float32)
    ss = np.concatenate([rs[b]["ss_out"] for b in range(NCORES)], axis=0)[None].astype(np.float32)
    vs = np.concatenate([rs[b]["vs_out"].reshape(16, 8, D) for b in range(NCORES)], axis=0)[None].astype(np.float32)
    return (y_prompt, y_sample, sp, ss, vs)
```
